# Optimizing a Trainium2 kernel written in Bass

```python
import jax, jax.numpy as jnp
from jax import lax
import numpy as np

D_MODEL = 4096
BATCH = 2
SEQ = 8192
DEPTH = 2

N_MEM = 256
E_A = D_MODEL // 2
CONV_A = 3
E_B = D_MODEL // 2
POOL_WINDOWS = (2, 4, 8, 16)
N_POOL_GROUPS = len(POOL_WINDOWS)
G_B = E_B // N_POOL_GROUPS
GO_B = D_MODEL // N_POOL_GROUPS
E_C = D_MODEL // 2
CONV_C = 31
X_HEADS = 4
X_HEAD_DIM = D_MODEL // 16
X_W = X_HEADS * X_HEAD_DIM
N_BRANCH = 4
EPS = 1e-6

IN_SIZES = (E_A, E_A, E_A, E_A,
            E_B, E_B,
            E_C, E_C, E_C,
            X_W,
            N_BRANCH * D_MODEL)
N_IN = sum(IN_SIZES)
IN_SPLITS = tuple(int(s) for s in np.cumsum(IN_SIZES)[:-1])

kernel_name = "gated_parallel_conv_pool_conformer_memxattn"


def rmsnorm(x, g):
    xf = x.astype(jnp.float32)
    y = xf * lax.rsqrt(jnp.mean(xf * xf, axis=-1, keepdims=True) + EPS)
    return (y * g.astype(jnp.float32)).astype(x.dtype)


def layernorm(x, g, b):
    xf = x.astype(jnp.float32)
    mu = jnp.mean(xf, axis=-1, keepdims=True)
    var = jnp.mean(jnp.square(xf - mu), axis=-1, keepdims=True)
    y = (xf - mu) * lax.rsqrt(var + EPS)
    return (y * g.astype(jnp.float32) + b.astype(jnp.float32)).astype(x.dtype)


def causal_depthwise_conv(x, w):
    k, c = w.shape
    return lax.conv_general_dilated(
        x, w.astype(x.dtype)[:, None, :], window_strides=(1,), padding=[(k - 1, 0)],
        dimension_numbers=("NWC", "WIO", "NWC"), feature_group_count=c)


def causal_multiscale_pool(u):
    t = u.shape[1]
    cs = jnp.cumsum(u.astype(jnp.float32), axis=1)
    pos = jnp.arange(t, dtype=jnp.float32)
    outs = []
    for gi, w in enumerate(POOL_WINDOWS):
        c = cs[:, :, gi]
        c_shift = jnp.pad(c, ((0, 0), (w, 0), (0, 0)))[:, :t]
        cnt = jnp.minimum(pos + 1.0, float(w))[None, :, None]
        outs.append((c - c_shift) / cnt)
    mean = jnp.stack(outs, axis=2)
    return mean.astype(u.dtype) - u


def setup_inputs(seed: int = 0) -> dict:
    key = jax.random.key(seed)
    ks = jax.random.split(key, 20)
    f32 = jnp.float32
    nrm = lambda k, shape, fan_in: jax.random.normal(k, shape, f32) * (fan_in ** -0.5)
    gain = lambda k, shape: 1.0 + 0.05 * jax.random.normal(k, shape, f32)
    return {
        "x": jax.random.normal(ks[0], (BATCH, SEQ, D_MODEL), f32),
        "mem": jax.random.normal(ks[1], (BATCH, N_MEM, D_MODEL), f32),
        "g_pre": gain(ks[2], (DEPTH, D_MODEL)),
        "g_post": gain(ks[3], (DEPTH, D_MODEL)),
        "g_mem": gain(ks[4], (DEPTH, D_MODEL)),
        "w_in": nrm(ks[5], (DEPTH, D_MODEL, N_IN), D_MODEL),
        "conv_a_w": nrm(ks[6], (DEPTH, CONV_A, E_A), CONV_A),
        "w_out_a": nrm(ks[7], (DEPTH, E_A, D_MODEL), E_A),
        "pool_scale": gain(ks[8], (DEPTH, E_B)),
        "w_pool": nrm(ks[9], (DEPTH, N_POOL_GROUPS, G_B, GO_B), G_B),
        "conv_c_w": nrm(ks[10], (DEPTH, CONV_C, E_C), CONV_C),
        "conv_c_b": 0.02 * jax.random.normal(ks[11], (DEPTH, E_C), f32),
        "ln_c_g": gain(ks[12], (DEPTH, E_C)),
        "ln_c_b": 0.02 * jax.random.normal(ks[13], (DEPTH, E_C), f32),
        "w_out_c": nrm(ks[14], (DEPTH, E_C, D_MODEL), E_C),
        "w_mem_kv": nrm(ks[15], (DEPTH, D_MODEL, 2 * X_W), D_MODEL),
        "w_out_x": nrm(ks[16], (DEPTH, X_W, D_MODEL), X_W),
        "w_o": nrm(ks[17], (DEPTH, D_MODEL, D_MODEL), D_MODEL),
    }


def reference(x, mem, g_pre, g_post, g_mem, w_in, conv_a_w, w_out_a, pool_scale, w_pool,
              conv_c_w, conv_c_b, ln_c_g, ln_c_b, w_out_c, w_mem_kv, w_out_x, w_o):
    bsz, t, _ = x.shape
    for l in range(DEPTH):
        h = rmsnorm(x, g_pre[l])
        proj = jnp.einsum("btd,dn->btn", h, w_in[l])
        (v_a, b_a, c_a, z_a, u_b, z_b, a_c, gt_c, z_c, q_x, gate_logits) = jnp.split(
            proj, IN_SPLITS, axis=-1)

        y_a = b_a * causal_depthwise_conv(c_a * v_a, conv_a_w[l])
        br_a = jnp.einsum("bte,ed->btd", y_a * jax.nn.silu(z_a), w_out_a[l])

        p = causal_multiscale_pool(u_b.reshape(bsz, t, N_POOL_GROUPS, G_B))
        p = p * pool_scale[l].reshape(N_POOL_GROUPS, G_B) * jax.nn.silu(
            z_b.reshape(bsz, t, N_POOL_GROUPS, G_B))
        br_b = jnp.einsum("btgc,gco->btgo", p, w_pool[l]).reshape(bsz, t, D_MODEL)

        g_c = a_c * jax.nn.sigmoid(gt_c)
        d_c = causal_depthwise_conv(g_c, conv_c_w[l]) + conv_c_b[l]
        s_c = jax.nn.silu(layernorm(d_c, ln_c_g[l], ln_c_b[l])) * jax.nn.silu(z_c)
        br_c = jnp.einsum("bte,ed->btd", s_c, w_out_c[l])

        m = rmsnorm(mem, g_mem[l])
        kv = jnp.einsum("bmd,dn->bmn", m, w_mem_kv[l])
        k_m, v_m = jnp.split(kv, 2, axis=-1)
        qh = q_x.reshape(bsz, t, X_HEADS, X_HEAD_DIM)
        kh = k_m.reshape(bsz, N_MEM, X_HEADS, X_HEAD_DIM)
        vh = v_m.reshape(bsz, N_MEM, X_HEADS, X_HEAD_DIM)
        scores = jnp.einsum("bthd,bmhd->bhtm", qh.astype(jnp.float32), kh.astype(jnp.float32))
        probs = jax.nn.softmax(scores * (X_HEAD_DIM ** -0.5), axis=-1).astype(vh.dtype)
        att = jnp.einsum("bhtm,bmhd->bthd", probs, vh).reshape(bsz, t, X_W)
        br_x = jnp.einsum("btw,wd->btd", att, w_out_x[l])

        gates = jax.nn.sigmoid(gate_logits.reshape(bsz, t, N_BRANCH, D_MODEL))
        merged = (gates[:, :, 0] * br_a + gates[:, :, 1] * br_b
                  + gates[:, :, 2] * br_c + gates[:, :, 3] * br_x)
        y = jnp.einsum("btd,de->bte", merged, w_o[l])

        x = x + rmsnorm(y, g_post[l])
    return x
```

```python
import os
import contextlib
import numpy as np
import concourse.bass as bass
import concourse.mybir as mybir
from concourse.bass_utils import run_bass_kernel_spmd

F32 = mybir.dt.float32
BF16 = mybir.dt.bfloat16
ALU = mybir.AluOpType
AF = mybir.ActivationFunctionType

D = 4096
NCORES = 8
SEQ = 8192
TOK = 2048
HALO = 64
NTOK = TOK + HALO
NT = 704
NS = 352
NTILES = 3
DEPTH = 2
NMEM = 256
EPS = 1e-6
PADW = 736
NBW = 3
NTMP = 6
NOP_CYC = int(os.environ.get("MK_NOP_CYC", "0"))
NOP_EVERY = int(os.environ.get("MK_NOP_EVERY", "1"))
NIO = 3

C_V, C_B, C_C, C_Z = 0, 2048, 4096, 6144
C_U, C_ZB = 8192, 10240
C_AC, C_GT, C_ZC = 12288, 14336, 16384
C_Q = 18432
C_G = 19456

P_GPRE, P_GPOST, P_GMEM = 0, 32, 64
P_CAW = 96
P_PS = 144
P_CCW = 160
P_CCB = 656
P_LNG = 672
P_LNB = 688
NPAR = 704


def _slabs(wcols):
    k = wcols.shape[0] // 128
    return wcols.reshape(k, 128, 128).transpose(1, 0, 2).reshape(128, k * 128)


def _main_entries():
    ent = []
    for j in range(16):
        ent += [("in", C_V + j * 128), ("in", C_C + j * 128), ("in", C_B + j * 128), ("in", C_Z + j * 128)]
    for j in range(16):
        ent += [("in", C_U + j * 128), ("in", C_ZB + j * 128)]
    for oc in range(32):
        ent += [("in", C_G + oc * 128), ("in", C_G + 4096 + oc * 128), ("ab", oc)]
        if oc % 2 == 1:
            j = oc // 2
            ent += [("in", C_GT + j * 128), ("in", C_AC + j * 128)]
    for i in range(8):
        ent += [("in", C_Q + i * 128)]
    for j in range(16):
        ent += [("in", C_ZC + j * 128)]
    for oc in range(32):
        ent += [("in", C_G + 8192 + oc * 128), ("in", C_G + 12288 + oc * 128), ("cx", oc)]
    for oc in range(32):
        ent += [("wo", oc)]
    return ent


def _entry_cols(kind):
    return {"in": 4096, "ab": 20 * 128, "cx": 24 * 128, "wo": 4096}[kind]


MAIN_ENT = _main_entries()
MAIN_OFF = np.concatenate([[0], np.cumsum([_entry_cols(k) for k, _ in MAIN_ENT])]).astype(np.int64)
MAIN_COLS = int(MAIN_OFF[-1])
MEM_COLS = 16 * 4096


def _build_wmain(l, w_in, w_out_a, w_pool, w_out_c, w_out_x, w_o, conv_c_w):
    out = np.empty((128, MAIN_COLS), np.float32)
    pidx = np.arange(128)
    for i, (kind, a) in enumerate(MAIN_ENT):
        o = int(MAIN_OFF[i])
        if kind == "in":
            out[:, o:o + 4096] = _slabs(w_in[l][:, a:a + 128])
        elif kind == "ab":
            g, ol = a // 8, a % 8
            out[:, o:o + 2048] = _slabs(w_out_a[l][:, a * 128:(a + 1) * 128])
            out[:, o + 2048:o + 2560] = _slabs(w_pool[l][g][:, ol * 128:(ol + 1) * 128])
        elif kind == "cx":
            out[:, o:o + 2048] = _slabs(w_out_c[l][:, a * 128:(a + 1) * 128])
            out[:, o + 2048:o + 3072] = _slabs(w_out_x[l][:, a * 128:(a + 1) * 128])
        else:
            out[:, o:o + 4096] = _slabs(w_o[l][:, a * 128:(a + 1) * 128])
    return out


def _build_wmem(l, w_mem_kv):
    out = np.empty((128, MEM_COLS), np.float32)
    w = w_mem_kv[l]
    for oc in range(8):
        out[:, oc * 4096:(oc + 1) * 4096] = _slabs(w[:, oc * 128:(oc + 1) * 128])
    e = 8
    for ch in range(2):
        for q in range(4):
            blk = w[q * 1024:(q + 1) * 1024, 1024 + ch * 512:1024 + (ch + 1) * 512]
            out[:, e * 4096:(e + 1) * 4096] = blk.reshape(8, 128, 512).transpose(1, 0, 2).reshape(128, 4096)
            e += 1
    return out


def _fm(v, n):
    return np.ascontiguousarray(v.reshape(n, 128).T)


def _build_params(l, g_pre, g_post, g_mem, conv_a_w, pool_scale, conv_c_w, conv_c_b, ln_c_g, ln_c_b):
    p = np.zeros((128, NPAR), np.float32)
    p[:, P_GPRE:P_GPRE + 32] = _fm(g_pre[l], 32)
    p[:, P_GPOST:P_GPOST + 32] = _fm(g_post[l], 32)
    p[:, P_GMEM:P_GMEM + 32] = _fm(g_mem[l], 32)
    p[:, P_CAW:P_CAW + 48] = conv_a_w[l].reshape(3, 16, 128).transpose(2, 1, 0).reshape(128, 48)
    p[:, P_PS:P_PS + 16] = _fm(pool_scale[l], 16)
    p[:, P_CCW:P_CCW + 496] = conv_c_w[l].reshape(31, 16, 128).transpose(2, 1, 0).reshape(128, 496)
    p[:, P_CCB:P_CCB + 16] = _fm(conv_c_b[l], 16)
    p[:, P_LNG:P_LNG + 16] = _fm(ln_c_g[l], 16)
    p[:, P_LNB:P_LNB + 16] = _fm(ln_c_b[l], 16)
    return p


ENGS = ("pe", "act", "dve", "pool", "sp")
SAME_SYNC = os.environ.get("MK_SAME_SYNC", "1") == "1"


class Sched:
    def __init__(self):
        self.ops = {e: [] for e in ENGS}
        self.res = {}
        self.waited = {e: {} for e in ENGS}
        self.epoch = 0
        self.dma_cnt = {}
        self.const = set()

    def _need(self, eng, o, ev):
        kind, key, val = ev
        if kind == "c" and key == eng and (eng == "pe" or not SAME_SYNC):
            return
        wk = (kind, key)
        if self.waited[eng].get(wk, -1) >= val:
            return
        self.waited[eng][wk] = val
        o["waits"].append(ev)
        if kind == "c":
            self.ops[key][val]["sig"] = True

    def op(self, eng, fn, reads=(), writes=(), dsem=None):
        o = dict(fn=fn, waits=[], sig=False, dsem=dsem, epoch=self.epoch)
        idx = len(self.ops[eng])
        deps = []
        for r in reads:
            st = self.res.get(r)
            if st and st[0]:
                deps.append(st[0])
        for w in writes:
            st = self.res.get(w)
            if st:
                if st[0]:
                    deps.append(st[0])
                deps += st[1]
        for ev in deps:
            self._need(eng, o, ev)
        self.ops[eng].append(o)
        if dsem:
            c = self.dma_cnt.get(dsem, 0) + 1
            self.dma_cnt[dsem] = c
            ev = ("d", dsem, c * 16)
        else:
            ev = ("c", eng, idx)
        for r in reads:
            if r in self.const:
                continue
            self.res.setdefault(r, [None, []])[1].append(ev)
        for w in writes:
            self.res[w] = [ev, []]
        return ev

    def finalize(self):
        self.nepoch = self.epoch + 1
        for e in ENGS:
            counts = {}
            for o in self.ops[e]:
                if o["sig"]:
                    counts[o["epoch"]] = counts.get(o["epoch"], 0) + 1
                o["cnt"] = counts.get(o["epoch"], 0)

    def emit_engine(self, eng, e, semtab, dsems):
        for o in self.ops[eng]:
            for (kind, key, val) in o["waits"]:
                if kind == "c":
                    t = self.ops[key][val]
                    e.wait_ge(semtab[key][t["epoch"]], t["cnt"])
                else:
                    e.wait_ge(dsems[key], val)
            ins = o["fn"](e)
            if o["sig"]:
                ins.then_inc(semtab[eng][o["epoch"]], 1)
            if o["dsem"]:
                ins.then_inc(dsems[o["dsem"]], 16)


def build_program(depth=DEPTH, ntiles=NTILES):
    nc = bass.Bass("TRN2", target_bir_lowering=False)
    xT = nc.dram_tensor("xT", [D, NTOK], F32, kind="ExternalInput").ap()
    memT = nc.dram_tensor("memT", [D, NMEM], F32, kind="ExternalInput").ap()
    wmain = nc.dram_tensor("wmain", [DEPTH * 128, MAIN_COLS], F32, kind="ExternalInput").ap()
    wmem = nc.dram_tensor("wmem", [DEPTH * 128, MEM_COLS], F32, kind="ExternalInput").ap()
    params = nc.dram_tensor("params", [DEPTH * 128, NPAR], F32, kind="ExternalInput").ap()
    gvin = nc.dram_tensor("gvec", [128, 128], F32, kind="ExternalInput").ap()
    aux = nc.dram_tensor("aux", [128, 128], F32, kind="ExternalInput").ap()
    outT = nc.dram_tensor("outT", [D, NTOK], F32, kind="ExternalOutput").ap()
    ybuf = nc.dram_tensor("ybuf", [D, NT], F32, kind="Internal").ap()
    dcbuf = nc.dram_tensor("dcbuf", [2048, NT], F32, kind="Internal").ap()

    S = Sched()
    es = contextlib.ExitStack()
    with es:
        big = es.enter_context(nc.sbuf_tensor("big", [128, 45056], BF16))
        mg = es.enter_context(nc.sbuf_tensor("mg", [128, 22528], BF16))
        W = [es.enter_context(nc.sbuf_tensor(f"w{i}", [128, 4096], BF16)) for i in range(NBW)]
        T = [es.enter_context(nc.sbuf_tensor(f"t{i}", [128, PADW], F32)) for i in range(NTMP)]
        IO = [es.enter_context(nc.sbuf_tensor(f"io{i}", [128, NT], F32)) for i in range(NIO)]
        rstdp = es.enter_context(nc.sbuf_tensor("rstdp", [128, NT], F32))
        ones32 = es.enter_context(nc.sbuf_tensor("ones32", [128, 128], F32))
        ones16 = es.enter_context(nc.sbuf_tensor("ones16", [128, 128], BF16))
        par = es.enter_context(nc.sbuf_tensor("par", [128, NPAR], F32))
        gvec = es.enter_context(nc.sbuf_tensor("gvecs", [128, 128], F32))
        auxs = es.enter_context(nc.sbuf_tensor("auxs", [128, 128], F32))
        KT = es.enter_context(nc.sbuf_tensor("KT", [128, 8 * 256], BF16))
        VV = es.enter_context(nc.sbuf_tensor("VV", [128, 2 * 1024], BF16))
        acar = es.enter_context(nc.sbuf_tensor("acar", [128, 16 * 4], F32))
        ucar = es.enter_context(nc.sbuf_tensor("ucar", [128, 16 * 16], F32))
        ccar = es.enter_context(nc.sbuf_tensor("ccar", [128, 16 * 32], F32))
        lnA = es.enter_context(nc.sbuf_tensor("lnA", [128, NT], F32))
        lnB = es.enter_context(nc.sbuf_tensor("lnB", [128, NT], F32))
        ps = es.enter_context(nc.psum_tensor("ps", [128, 8, 512], F32))

        S.const.update({("aux",), ("ones",), ("gvec",)})

        def v2(ap):
            return ap.rearrange("p (a b) -> p a b", a=2)

        def hT(kc, st=None):
            if st is None:
                return big[:, kc * NT:(kc + 1) * NT]
            return big[:, kc * NT + st * NS: kc * NT + (st + 1) * NS]

        def s0(j, st=None):
            return hT(32 + j, st)

        def s1(j, st=None):
            return hT(48 + j, st)

        def mgc(kc, st=None):
            if st is None:
                return mg[:, kc * NT:(kc + 1) * NT]
            return mg[:, kc * NT + st * NS: kc * NT + (st + 1) * NS]

        def slot(s):
            return ps[:, 2 * s:2 * s + 2, 0:NS]

        def pcol(c):
            return par[:, c:c + 1]

        def gcol(c):
            return gvec[:, c:c + 1]

        st_ = dict(tmp=0, slot=0, w=0, io=0)

        def tmp():
            i = st_["tmp"]
            st_["tmp"] = (i + 1) % NTMP
            return i

        def io():
            i = st_["io"]
            st_["io"] = (i + 1) % NIO
            return i

        def next_slot():
            s = st_["slot"]
            st_["slot"] = (s + 1) % 3
            return s

        def rk_slot(s):
            return [("ps", 2 * s), ("ps", 2 * s + 1)]

        def load_w(src_ap, ncols):
            s = st_["w"]
            st_["w"] = (s + 1) % NBW
            S.op("pool", lambda e, s=s, src_ap=src_ap, ncols=ncols: e.dma_start(out=W[s][:, 0:ncols], in_=src_ap),
                 reads=[], writes=[("w", s)], dsem=f"w{s}")
            return s

        ent_pos = dict(i=0)

        def next_main(l, kind):
            i = ent_pos["i"]
            k, _ = MAIN_ENT[i]
            assert k == kind, (k, kind, i)
            o = int(MAIN_OFF[i])
            n = _entry_cols(k)
            ent_pos["i"] = i + 1
            return load_w(wmain[l * 128:(l + 1) * 128, o:o + n], n)

        def mm_job(ws, nk, rhs_fn, rhs_keys, slab0=0, sl=None):
            if sl is None:
                sl = next_slot()
            for kc in range(nk):
                for st in range(2):
                    S.op("pe", lambda e, sl=sl, ws=ws, kc=kc, st=st, nk=nk: e.matmul(
                        ps[:, 2 * sl + st, 0:NS], W[ws][:, (slab0 + kc) * 128:(slab0 + kc + 1) * 128],
                        rhs_fn(kc, st), start=(kc == 0), stop=(kc == nk - 1)),
                        reads=[("w", ws), rhs_keys(kc)], writes=[("ps", 2 * sl + st)])
            st_["jobs"] = st_.get("jobs", 0) + 1
            if NOP_CYC > 0 and st_["jobs"] % NOP_EVERY == 0:
                S.op("pe", lambda e: e.nop(cycle_cnt=NOP_CYC))
            return sl

        def proj_job(l):
            ws = next_main(l, "in")
            return mm_job(ws, 32, hT, lambda kc: ("big", kc))

        def act(fn, reads, writes):
            return S.op("act", fn, reads, writes)

        def dve(fn, reads, writes):
            return S.op("dve", fn, reads, writes)

        def ones_reduce(src, key):
            for st in range(2):
                S.op("pe", lambda e, st=st: e.matmul(ps[:, 6 + st, 0:NS], ones32[:], src[:, st * NS:(st + 1) * NS],
                                                     start=True, stop=True), [key, ("ones",)], [("ps", 6 + st)])

        def rstd_from_ps(dst, key, scale):
            sd = tmp()
            act(lambda e, sd=sd: e.activation(out=v2(T[sd][:, 0:NT]), in_=ps[:, 6:8, 0:NS], func=AF.Sqrt,
                                              bias=EPS, scale=scale), [("ps", 6), ("ps", 7)], [("T", sd)])
            dve(lambda e, sd=sd: e.reciprocal(out=dst[:], in_=T[sd][:, 0:NT]), [("T", sd)], [key])

        dve(lambda e: e.memset(ones32[:], 1.0), [], [("ones",)])
        dve(lambda e: e.memset(ones16[:], 1.0), [], [("ones",)])
        S.op("sp", lambda e: e.dma_start(out=auxs[:], in_=aux), [], [("aux",)], dsem="aux")
        S.op("sp", lambda e: e.dma_start(out=gvec[:], in_=gvin), [], [("gvec",)], dsem="gvec")

        def x_src(l):
            return xT if l == 0 else outT

        def load_io(l, ti, kc, src=None, key=None):
            i = io()
            if src is None:
                src = x_src(l)[kc * 128:(kc + 1) * 128, ti * NT:(ti + 1) * NT]
                rk = [("o", ti, kc)] if l > 0 else []
            else:
                rk = [key]
            S.op("sp", lambda e, i=i, src=src: e.dma_start(out=IO[i][:], in_=src), reads=rk, writes=[("IO", i)],
                 dsem=f"lio{i}")
            return i

        def phase1_pieces(l, ti):
            P = []

            def p1(kc):
                def f():
                    i = load_io(l, ti, kc)
                    if kc == 0:
                        act(lambda e, i=i: e.activation(out=lnA[:], in_=IO[i][:], func=AF.Square), [("IO", i)], [("lnA",)])
                    else:
                        q = tmp()
                        act(lambda e, i=i, q=q: e.activation(out=T[q][:, 0:NT], in_=IO[i][:], func=AF.Square),
                            [("IO", i)], [("T", q)])
                        dve(lambda e, q=q: e.tensor_tensor(out=lnA[:], in0=lnA[:], in1=T[q][:, 0:NT], op=ALU.add),
                            [("T", q), ("lnA",)], [("lnA",)])
                return f

            def fin():
                ones_reduce(lnA, ("lnA",))
                rstd_from_ps(lnA, ("lnA",), 1.0 / D)

            def p2(kc):
                def f():
                    i = load_io(l, ti, kc)
                    dve(lambda e, i=i, kc=kc: e.scalar_tensor_tensor(
                        out=hT(kc), in0=IO[i][:], scalar=gcol(l * 64 + kc), in1=lnA[:],
                        op0=ALU.mult, op1=ALU.mult), [("IO", i), ("lnA",), ("gvec",)], [("big", kc)])
                return f

            for kc in range(32):
                P.append(p1(kc))
            P.append(fin)
            for kc in range(32):
                P.append(p2(kc))
            return P

        def final_pieces(l, ti):
            P = []

            def fp(oc):
                def f():
                    a = load_io(l, ti, oc, src=ybuf[oc * 128:(oc + 1) * 128, :], key=("yb", oc))
                    b = load_io(l, ti, oc)
                    dve(lambda e, a=a, oc=oc: e.scalar_tensor_tensor(
                        out=IO[a][:], in0=IO[a][:], scalar=gcol(l * 64 + 32 + oc), in1=rstdp[:], op0=ALU.mult, op1=ALU.mult),
                        [("IO", a), ("rstdp",), ("gvec",)], [("IO", a)])
                    dve(lambda e, a=a, b=b: e.tensor_tensor(out=IO[b][:], in0=IO[b][:], in1=IO[a][:], op=ALU.add),
                        [("IO", a), ("IO", b)], [("IO", b)])
                    S.op("sp", lambda e, b=b, oc=oc: e.dma_start(
                        out=outT[oc * 128:(oc + 1) * 128, ti * NT:(ti + 1) * NT], in_=IO[b][:]),
                        reads=[("IO", b)], writes=[("o", ti, oc)], dsem=f"sio{b}")
                return f

            for oc in range(32):
                P.append(fp(oc))
            return P

        def drain(P, n):
            for _ in range(n):
                if P:
                    P.pop(0)()

        def layer_setup(l):
            S.op("sp", lambda e, l=l: e.dma_start(out=par[:], in_=params[l * 128:(l + 1) * 128, :]),
                 [], [("par",)], dsem="par")
            dve(lambda e: e.memset(acar[:], 0.0), [], [("acar",)])
            dve(lambda e: e.memset(ucar[:], 0.0), [], [("ucar",)])
            dve(lambda e: e.memset(ccar[:], 0.0), [], [("ccar",)])
            for kc in range(32):
                i = tmp()
                S.op("sp", lambda e, i=i, kc=kc: e.dma_start(out=T[i][:, 0:NMEM], in_=memT[kc * 128:(kc + 1) * 128, :]),
                     [], [("T", i)], dsem=f"ld{i}")
                q = tmp()
                act(lambda e, i=i, q=q: e.activation(out=T[q][:, 0:NMEM], in_=T[i][:, 0:NMEM], func=AF.Square),
                    [("T", i)], [("T", q)])
                S.op("pe", lambda e, q=q, kc=kc: e.matmul(ps[:, 6, 0:NMEM], ones32[:], T[q][:, 0:NMEM],
                                                          start=(kc == 0), stop=(kc == 31)),
                     [("T", q), ("ones",)], [("ps", 6)])
            sd = tmp()
            act(lambda e, sd=sd: e.activation(out=T[sd][:, 0:NMEM], in_=ps[:, 6, 0:NMEM], func=AF.Sqrt,
                                              bias=EPS, scale=1.0 / D), [("ps", 6)], [("T", sd)])
            rm = lnB
            dve(lambda e, sd=sd: e.reciprocal(out=rm[:, 0:NMEM], in_=T[sd][:, 0:NMEM]), [("T", sd)], [("lnB",)])

            def mh(kc):
                return big[:, 48 * NT + kc * NMEM: 48 * NT + (kc + 1) * NMEM]
            MHK = [("big", 48 + u) for u in range(16)]
            for kc in range(32):
                i = tmp()
                S.op("sp", lambda e, i=i, kc=kc: e.dma_start(out=T[i][:, 0:NMEM], in_=memT[kc * 128:(kc + 1) * 128, :]),
                     [], [("T", i)], dsem=f"ld{i}")
                dve(lambda e, i=i, kc=kc: e.scalar_tensor_tensor(
                    out=mh(kc), in0=T[i][:, 0:NMEM], scalar=pcol(P_GMEM + kc), in1=rm[:, 0:NMEM],
                    op0=ALU.mult, op1=ALU.mult), [("T", i), ("lnB",), ("par",)],
                    [("big", 48 + (kc * NMEM) // NT), ("big", 48 + (kc * NMEM + NMEM - 1) // NT)])
            for oc in range(8):
                ws = load_w(wmem[l * 128:(l + 1) * 128, oc * 4096:(oc + 1) * 4096], 4096)
                sl = next_slot()
                for kc in range(32):
                    S.op("pe", lambda e, sl=sl, ws=ws, kc=kc: e.matmul(
                        ps[:, 2 * sl, 0:NMEM], W[ws][:, kc * 128:(kc + 1) * 128], mh(kc),
                        start=(kc == 0), stop=(kc == 31)),
                        [("w", ws)] + MHK, [("ps", 2 * sl)])
                act(lambda e, sl=sl, oc=oc: e.activation(out=KT[:, oc * 256:(oc + 1) * 256], in_=ps[:, 2 * sl, 0:NMEM],
                                                         func=AF.Copy), [("ps", 2 * sl)], [("KT",)])
            e_i = 8
            for ch in range(2):
                for q4 in range(4):
                    ws = load_w(wmem[l * 128:(l + 1) * 128, e_i * 4096:(e_i + 1) * 4096], 4096)
                    e_i += 1
                    for i8 in range(8):
                        kc = q4 * 8 + i8
                        for mc in range(2):
                            S.op("pe", lambda e, ws=ws, kc=kc, mc=mc, i8=i8: e.matmul(
                                ps[:, 6 + mc, 0:512], mh(kc)[:, mc * 128:(mc + 1) * 128],
                                W[ws][:, i8 * 512:(i8 + 1) * 512], start=(kc == 0), stop=(kc == 31)),
                                [("w", ws)] + MHK, [("ps", 6 + mc)])
                for mc in range(2):
                    act(lambda e, mc=mc, ch=ch: e.activation(
                        out=VV[:, mc * 1024 + ch * 512: mc * 1024 + (ch + 1) * 512], in_=ps[:, 6 + mc, 0:512],
                        func=AF.Copy), [("ps", 6 + mc)], [("VV",)])

        def front_a(l, ti, pend):
            for j in range(16):
                sl = proj_job(l)
                vS = tmp()
                act(lambda e, sl=sl, vS=vS: e.activation(out=v2(T[vS][:, 0:NT]), in_=slot(sl), func=AF.Copy),
                    rk_slot(sl), [("T", vS)])
                sl = proj_job(l)
                cv = tmp()
                act(lambda e, cv=cv, j=j: e.activation(out=T[cv][:, 0:4], in_=acar[:, j * 4:(j + 1) * 4], func=AF.Copy),
                    [("acar",)], [("T", cv)])
                dve(lambda e, sl=sl, vS=vS, cv=cv: e.tensor_tensor(
                    out=v2(T[cv][:, 4:4 + NT]), in0=slot(sl), in1=v2(T[vS][:, 0:NT]), op=ALU.mult),
                    rk_slot(sl) + [("T", vS), ("T", cv)], [("T", cv)])
                act(lambda e, cv=cv, j=j: e.activation(out=acar[:, j * 4:(j + 1) * 4], in_=T[cv][:, NT:NT + 4], func=AF.Copy),
                    [("T", cv)], [("acar",)])
                acc = tmp()
                dve(lambda e, cv=cv, acc=acc, j=j: e.tensor_scalar(
                    out=T[acc][:, 0:NT], in0=T[cv][:, 2:2 + NT], scalar1=pcol(P_CAW + j * 3 + 0), scalar2=None,
                    op0=ALU.mult), [("T", cv), ("par",)], [("T", acc)])
                for k in (1, 2):
                    dve(lambda e, cv=cv, acc=acc, j=j, k=k: e.scalar_tensor_tensor(
                        out=T[acc][:, 0:NT], in0=T[cv][:, 2 + k:2 + k + NT], scalar=pcol(P_CAW + j * 3 + k),
                        in1=T[acc][:, 0:NT], op0=ALU.mult, op1=ALU.add), [("T", cv), ("T", acc), ("par",)], [("T", acc)])
                sl = proj_job(l)
                dve(lambda e, sl=sl, acc=acc: e.tensor_tensor(
                    out=v2(T[acc][:, 0:NT]), in0=slot(sl), in1=v2(T[acc][:, 0:NT]), op=ALU.mult),
                    rk_slot(sl) + [("T", acc)], [("T", acc)])
                sl = proj_job(l)
                sz = tmp()
                act(lambda e, sl=sl, sz=sz: e.activation(out=v2(T[sz][:, 0:NT]), in_=slot(sl), func=AF.Silu),
                    rk_slot(sl), [("T", sz)])
                dve(lambda e, acc=acc, sz=sz, j=j: e.tensor_tensor(
                    out=s0(j), in0=T[acc][:, 0:NT], in1=T[sz][:, 0:NT], op=ALU.mult),
                    [("T", acc), ("T", sz)], [("big", 32 + j)])
                drain(pend, 2)

        def front_b(l, ti):
            for j in range(16):
                g = j // 4
                wwin = 2 ** (g + 1)
                sl = proj_job(l)
                U = tmp()
                act(lambda e, U=U, j=j: e.activation(out=T[U][:, 0:16], in_=ucar[:, j * 16:(j + 1) * 16], func=AF.Copy),
                    [("ucar",)], [("T", U)])
                act(lambda e, sl=sl, U=U: e.activation(out=v2(T[U][:, 16:16 + NT]), in_=slot(sl), func=AF.Copy),
                    rk_slot(sl) + [("T", U)], [("T", U)])
                act(lambda e, U=U, j=j: e.activation(out=ucar[:, j * 16:(j + 1) * 16], in_=T[U][:, NT:NT + 16], func=AF.Copy),
                    [("T", U)], [("ucar",)])
                pp = [tmp(), tmp()]
                Sb = U
                sh = 1
                for step in range(g + 1):
                    lo = 2 * sh - 1
                    Tn = pp[step % 2]
                    dve(lambda e, Sb=Sb, Tn=Tn, lo=lo, sh=sh: e.tensor_tensor(
                        out=T[Tn][:, lo:16 + NT], in0=T[Sb][:, lo:16 + NT], in1=T[Sb][:, lo - sh:16 + NT - sh],
                        op=ALU.add), [("T", Sb)], [("T", Tn)])
                    Sb = Tn
                    sh *= 2
                P = tmp()
                dve(lambda e, Sb=Sb, U=U, P=P, wwin=wwin: e.scalar_tensor_tensor(
                    out=T[P][:, 0:NT], in0=T[Sb][:, 16:16 + NT], scalar=1.0 / wwin, in1=T[U][:, 16:16 + NT],
                    op0=ALU.mult, op1=ALU.subtract), [("T", Sb), ("T", U)], [("T", P)])
                if ti == 0:
                    fx = pp[(g + 1) % 2]
                    dve(lambda e, Sb=Sb, fx=fx, g=g: e.tensor_tensor(
                        out=T[fx][:, 0:16], in0=T[Sb][:, 16 + HALO:16 + HALO + 16], in1=auxs[:, g * 16:(g + 1) * 16],
                        op=ALU.mult), [("T", Sb), ("aux",)], [("T", fx)])
                    dve(lambda e, U=U, fx=fx, P=P: e.tensor_tensor(
                        out=T[P][:, HALO:HALO + 16], in0=T[fx][:, 0:16], in1=T[U][:, 16 + HALO:16 + HALO + 16],
                        op=ALU.subtract), [("T", fx), ("T", U), ("T", P)], [("T", P)])
                sl = proj_job(l)
                sz = tmp()
                act(lambda e, sl=sl, sz=sz: e.activation(out=v2(T[sz][:, 0:NT]), in_=slot(sl), func=AF.Silu),
                    rk_slot(sl), [("T", sz)])
                dve(lambda e, P=P, sz=sz, j=j: e.scalar_tensor_tensor(
                    out=s1(j), in0=T[P][:, 0:NT], scalar=pcol(P_PS + j), in1=T[sz][:, 0:NT],
                    op0=ALU.mult, op1=ALU.mult), [("T", P), ("T", sz), ("par",)], [("big", 48 + j)])

        def back(l, first, nk_y, rhs_y, keys_y, kind, hook=None):
            for oc in range(32):
                sl = proj_job(l)
                sg1 = tmp()
                act(lambda e, sl=sl, sg1=sg1: e.activation(out=v2(T[sg1][:, 0:NT]), in_=slot(sl), func=AF.Sigmoid),
                    rk_slot(sl), [("T", sg1)])
                sl = proj_job(l)
                sg2 = tmp()
                act(lambda e, sl=sl, sg2=sg2: e.activation(out=v2(T[sg2][:, 0:NT]), in_=slot(sl), func=AF.Sigmoid),
                    rk_slot(sl), [("T", sg2)])
                ws = next_main(l, kind)
                sx = mm_job(ws, 16, s0, lambda kc: ("big", 32 + kc), slab0=0)
                sy = mm_job(ws, nk_y, lambda kc, st, oc=oc: rhs_y(oc, kc, st), lambda kc, oc=oc: keys_y(oc, kc), slab0=16)
                dve(lambda e, sx=sx, sg1=sg1: e.tensor_tensor(
                    out=v2(T[sg1][:, 0:NT]), in0=slot(sx), in1=v2(T[sg1][:, 0:NT]), op=ALU.mult),
                    rk_slot(sx) + [("T", sg1)], [("T", sg1)])
                dve(lambda e, sy=sy, sg2=sg2: e.tensor_tensor(
                    out=v2(T[sg2][:, 0:NT]), in0=slot(sy), in1=v2(T[sg2][:, 0:NT]), op=ALU.mult),
                    rk_slot(sy) + [("T", sg2)], [("T", sg2)])
                if first:
                    dve(lambda e, sg1=sg1, sg2=sg2, oc=oc: e.tensor_tensor(
                        out=mgc(oc), in0=T[sg1][:, 0:NT], in1=T[sg2][:, 0:NT], op=ALU.add),
                        [("T", sg1), ("T", sg2)], [("mg", oc)])
                else:
                    dve(lambda e, sg1=sg1, sg2=sg2: e.tensor_tensor(
                        out=T[sg1][:, 0:NT], in0=T[sg1][:, 0:NT], in1=T[sg2][:, 0:NT], op=ALU.add),
                        [("T", sg1), ("T", sg2)], [("T", sg1)])
                    dve(lambda e, sg1=sg1, oc=oc: e.tensor_tensor(
                        out=mgc(oc), in0=mgc(oc), in1=T[sg1][:, 0:NT], op=ALU.add),
                        [("T", sg1), ("mg", oc)], [("mg", oc)])
                if hook is not None:
                    hook(oc)

        def conv_chunk(l, j):
            sl = proj_job(l)
            sg = tmp()
            act(lambda e, sl=sl, sg=sg: e.activation(out=v2(T[sg][:, 0:NT]), in_=slot(sl), func=AF.Sigmoid),
                rk_slot(sl), [("T", sg)])
            sl = proj_job(l)
            gc = tmp()
            act(lambda e, gc=gc, j=j: e.activation(out=T[gc][:, 0:32], in_=ccar[:, j * 32:(j + 1) * 32], func=AF.Copy),
                [("ccar",)], [("T", gc)])
            dve(lambda e, sl=sl, sg=sg, gc=gc: e.tensor_tensor(
                out=v2(T[gc][:, 32:32 + NT]), in0=slot(sl), in1=v2(T[sg][:, 0:NT]), op=ALU.mult),
                rk_slot(sl) + [("T", sg), ("T", gc)], [("T", gc)])
            act(lambda e, gc=gc, j=j: e.activation(out=ccar[:, j * 32:(j + 1) * 32], in_=T[gc][:, NT:NT + 32], func=AF.Copy),
                [("T", gc)], [("ccar",)])
            acc = tmp()
            dve(lambda e, gc=gc, acc=acc, j=j: e.tensor_scalar(
                out=T[acc][:, 0:NT], in0=T[gc][:, 2:2 + NT], scalar1=pcol(P_CCW + j * 31), scalar2=pcol(P_CCB + j),
                op0=ALU.mult, op1=ALU.add), [("T", gc), ("par",)], [("T", acc)])
            for k in range(1, 31):
                dve(lambda e, gc=gc, acc=acc, j=j, k=k: e.scalar_tensor_tensor(
                    out=T[acc][:, 0:NT], in0=T[gc][:, 2 + k:2 + k + NT], scalar=pcol(P_CCW + j * 31 + k),
                    in1=T[acc][:, 0:NT], op0=ALU.mult, op1=ALU.add), [("T", gc), ("T", acc), ("par",)], [("T", acc)])
            S.op("sp", lambda e, acc=acc, j=j: e.dma_start(out=dcbuf[j * 128:(j + 1) * 128, :], in_=T[acc][:, 0:NT]),
                 reads=[("T", acc)], writes=[("dc", j)], dsem=f"st{acc}")
            if j == 0:
                act(lambda e, acc=acc: e.activation(out=lnB[:], in_=T[acc][:, 0:NT], func=AF.Square),
                    [("T", acc)], [("lnB",)])
                dve(lambda e, acc=acc: e.tensor_copy(out=lnA[:], in_=T[acc][:, 0:NT]), [("T", acc)], [("lnA",)])
            else:
                sq = tmp()
                act(lambda e, acc=acc, sq=sq: e.activation(out=T[sq][:, 0:NT], in_=T[acc][:, 0:NT], func=AF.Square),
                    [("T", acc)], [("T", sq)])
                dve(lambda e, acc=acc: e.tensor_tensor(out=lnA[:], in0=lnA[:], in1=T[acc][:, 0:NT], op=ALU.add),
                    [("T", acc), ("lnA",)], [("lnA",)])
                dve(lambda e, sq=sq: e.tensor_tensor(out=lnB[:], in0=lnB[:], in1=T[sq][:, 0:NT], op=ALU.add),
                    [("T", sq), ("lnB",)], [("lnB",)])

        def front_c_x(l, ti):
            for i8 in range(8):
                sl = proj_job(l)
                act(lambda e, sl=sl, i8=i8: e.activation(out=v2(s1(8 + i8)), in_=slot(sl), func=AF.Copy),
                    rk_slot(sl), [("big", 48 + 8 + i8)])

            ones_reduce(lnA, ("lnA",))
            mu = tmp()
            dve(lambda e, mu=mu: e.tensor_scalar(out=v2(T[mu][:, 0:NT]), in0=ps[:, 6:8, 0:NS], scalar1=1.0 / 2048,
                                                 scalar2=None, op0=ALU.mult), [("ps", 6), ("ps", 7)], [("T", mu)])
            ones_reduce(lnB, ("lnB",))
            msq = tmp()
            dve(lambda e, mu=mu, msq=msq: e.tensor_tensor(out=T[msq][:, 0:NT], in0=T[mu][:, 0:NT], in1=T[mu][:, 0:NT],
                                                          op=ALU.mult), [("T", mu)], [("T", msq)])
            var = tmp()
            dve(lambda e, msq=msq, var=var: e.scalar_tensor_tensor(
                out=v2(T[var][:, 0:NT]), in0=ps[:, 6:8, 0:NS], scalar=1.0 / 2048, in1=v2(T[msq][:, 0:NT]),
                op0=ALU.mult, op1=ALU.subtract), [("ps", 6), ("ps", 7), ("T", msq)], [("T", var)])
            act(lambda e, var=var: e.activation(out=T[var][:, 0:NT], in_=T[var][:, 0:NT], func=AF.Sqrt, bias=EPS, scale=1.0),
                [("T", var)], [("T", var)])
            dve(lambda e, var=var: e.reciprocal(out=lnA[:], in_=T[var][:, 0:NT]), [("T", var)], [("lnA",)])
            dve(lambda e, mu=mu: e.scalar_tensor_tensor(out=lnB[:], in0=T[mu][:, 0:NT], scalar=-1.0, in1=lnA[:],
                                                        op0=ALU.mult, op1=ALU.mult), [("T", mu), ("lnA",)], [("lnB",)])

            def z_job(j):
                sl = proj_job(l)
                sz = tmp()
                act(lambda e, sl=sl, sz=sz: e.activation(out=v2(T[sz][:, 0:NT]), in_=slot(sl), func=AF.Silu),
                    rk_slot(sl), [("T", sz)])
                di = load_io(l, ti, j, src=dcbuf[j * 128:(j + 1) * 128, :], key=("dc", j))
                tn = tmp()
                dve(lambda e, tn=tn, di=di: e.tensor_tensor(out=T[tn][:, 0:NT], in0=IO[di][:], in1=lnA[:], op=ALU.mult),
                    [("IO", di), ("lnA",)], [("T", tn)])
                dve(lambda e, tn=tn: e.tensor_tensor(out=T[tn][:, 0:NT], in0=T[tn][:, 0:NT], in1=lnB[:], op=ALU.add),
                    [("T", tn), ("lnB",)], [("T", tn)])
                act(lambda e, tn=tn, j=j: e.activation(out=T[tn][:, 0:NT], in_=T[tn][:, 0:NT], func=AF.Silu,
                                                       bias=pcol(P_LNB + j), scale=pcol(P_LNG + j)),
                    [("T", tn), ("par",)], [("T", tn)])
                dve(lambda e, tn=tn, sz=sz, j=j: e.tensor_tensor(out=s0(j), in0=T[tn][:, 0:NT], in1=T[sz][:, 0:NT],
                                                                 op=ALU.mult), [("T", tn), ("T", sz)], [("big", 32 + j)])

            Ebuf = {}

            def scores(h):
                sls = [next_slot(), next_slot()]
                for mc in range(2):
                    for st in range(2):
                        for dc in range(2):
                            S.op("pe", lambda e, h=h, mc=mc, st=st, dc=dc, sl=sls[mc]: e.matmul(
                                ps[:, 2 * sl + st, 0:NS],
                                KT[:, (h * 2 + dc) * 256 + mc * 128:(h * 2 + dc) * 256 + (mc + 1) * 128],
                                s1(8 + h * 2 + dc, st), start=(dc == 0), stop=(dc == 1)),
                                [("KT",), ("big", 48 + 8 + h * 2 + dc)], [("ps", 2 * sls[mc] + st)])
                E = tmp()
                Eb = T[E].bitcast(BF16)
                for mc in range(2):
                    act(lambda e, mc=mc, sl=sls[mc], Eb=Eb: e.activation(
                        out=v2(Eb[:, mc * NT:(mc + 1) * NT]), in_=slot(sl), func=AF.Exp, scale=1.0 / 16.0),
                        rk_slot(sls[mc]) + [("T", E)], [("T", E)])
                Ebuf[h] = (E, Eb)

            def attv(h):
                E, Eb = Ebuf[h]
                for st in range(2):
                    for mc in range(2):
                        S.op("pe", lambda e, st=st, mc=mc, Eb=Eb: e.matmul(
                            ps[:, 6 + st, 0:NS], ones16[:], Eb[:, mc * NT + st * NS: mc * NT + (st + 1) * NS],
                            start=(mc == 0), stop=(mc == 1)), [("T", E), ("ones",)], [("ps", 6 + st)])
                rd = tmp()
                dve(lambda e, rd=rd: e.reciprocal(out=v2(T[rd][:, 0:NT]), in_=ps[:, 6:8, 0:NS]),
                    [("ps", 6), ("ps", 7)], [("T", rd)])
                for dcp in range(2):
                    sl = next_slot()
                    for st in range(2):
                        for mc in range(2):
                            S.op("pe", lambda e, sl=sl, st=st, mc=mc, h=h, dcp=dcp, Eb=Eb: e.matmul(
                                ps[:, 2 * sl + st, 0:NS],
                                VV[:, mc * 1024 + (h * 2 + dcp) * 128: mc * 1024 + (h * 2 + dcp + 1) * 128],
                                Eb[:, mc * NT + st * NS: mc * NT + (st + 1) * NS], start=(mc == 0), stop=(mc == 1)),
                                [("T", E), ("VV",)], [("ps", 2 * sl + st)])
                    dve(lambda e, sl=sl, rd=rd, h=h, dcp=dcp: e.tensor_tensor(
                        out=v2(s1(h * 2 + dcp)), in0=slot(sl), in1=v2(T[rd][:, 0:NT]), op=ALU.mult),
                        rk_slot(sl) + [("T", rd)], [("big", 48 + h * 2 + dcp)])

            steps = []
            for h in range(4):
                steps += [lambda h=h: scores(h), lambda h=h: attv(h)]
            for j in range(16):
                z_job(j)
                if steps:
                    steps.pop(0)()

        def wo_phase(l, ti, nxt):
            for oc in range(32):
                ws = next_main(l, "wo")
                sl = mm_job(ws, 32, mgc, lambda kc: ("mg", kc))
                t1 = tmp()
                act(lambda e, sl=sl, t1=t1: e.activation(out=v2(T[t1][:, 0:NT]), in_=slot(sl), func=AF.Copy),
                    rk_slot(sl), [("T", t1)])
                S.op("sp", lambda e, t1=t1, oc=oc: e.dma_start(out=ybuf[oc * 128:(oc + 1) * 128, :], in_=T[t1][:, 0:NT]),
                     reads=[("T", t1)], writes=[("yb", oc)], dsem=f"st{t1}")
                if oc == 0:
                    act(lambda e, sl=sl: e.activation(out=v2(lnB[:]), in_=slot(sl), func=AF.Square),
                        rk_slot(sl), [("lnB",)])
                else:
                    t2 = tmp()
                    act(lambda e, sl=sl, t2=t2: e.activation(out=v2(T[t2][:, 0:NT]), in_=slot(sl), func=AF.Square),
                        rk_slot(sl), [("T", t2)])
                    dve(lambda e, t2=t2: e.tensor_tensor(out=lnB[:], in0=lnB[:], in1=T[t2][:, 0:NT], op=ALU.add),
                        [("T", t2), ("lnB",)], [("lnB",)])
                drain(nxt, 3 if oc == 0 else 2)
            drain(nxt, len(nxt))
            assert ent_pos["i"] == len(MAIN_ENT)
            ones_reduce(lnB, ("lnB",))
            rstd_from_ps(rstdp, ("rstdp",), 1.0 / D)
            if ti == 0:
                dve(lambda e: e.tensor_tensor(out=rstdp[:, 0:HALO], in0=rstdp[:, 0:HALO], in1=auxs[:, 64:64 + HALO],
                                              op=ALU.mult), [("rstdp",), ("aux",)], [("rstdp",)])

        seq = [(l, ti) for l in range(depth) for ti in range(ntiles)]
        pend_final = []
        drain_all = lambda P: drain(P, len(P))
        drain_all(phase1_pieces(*seq[0]))
        for n, (l, ti) in enumerate(seq):
            if ti == 0:
                layer_setup(l)
            ent_pos["i"] = 0
            front_a(l, ti, pend_final)
            drain_all(pend_final)
            front_b(l, ti)
            back(l, True, 4, lambda oc, kc, st: s1(4 * (oc // 8) + kc, st), lambda oc, kc: ("big", 48 + 4 * (oc // 8) + kc), "ab",
                 hook=lambda oc, l=l: conv_chunk(l, oc // 2) if oc % 2 == 1 else None)
            front_c_x(l, ti)
            back(l, False, 8, lambda oc, kc, st: s1(kc, st), lambda oc, kc: ("big", 48 + kc), "cx")
            nxt = phase1_pieces(*seq[n + 1]) if n + 1 < len(seq) else []
            wo_phase(l, ti, nxt)
            pend_final = final_pieces(l, ti)
            S.epoch += 1
        drain_all(pend_final)

        S.finalize()
        dsem_names = sorted(S.dma_cnt.keys())
        dsems = {n: es.enter_context(nc.semaphore("d_" + n)) for n in dsem_names}
        semtab = {e: [es.enter_context(nc.semaphore(f"c_{e}_{k}")) for k in range(S.nepoch)] for e in ENGS}
        block = es.enter_context(nc.Block())

        @block.tensor
        def _(e):
            S.emit_engine("pe", e, semtab, dsems)

        @block.scalar
        def _(e):
            S.emit_engine("act", e, semtab, dsems)

        @block.vector
        def _(e):
            S.emit_engine("dve", e, semtab, dsems)

        @block.gpsimd
        def _(e):
            S.emit_engine("pool", e, semtab, dsems)

        @block.sync
        def _(e):
            S.emit_engine("sp", e, semtab, dsems)
            for n in dsem_names:
                if n.startswith("st") or n.startswith("sio"):
                    e.wait_ge(dsems[n], S.dma_cnt[n] * 16)
    stats = {e: len(S.ops[e]) for e in ENGS}
    return nc, stats


def _prep_inputs(x, mem, g_pre, g_post, g_mem, w_in, conv_a_w, w_out_a, pool_scale, w_pool, conv_c_w, conv_c_b,
                 ln_c_g, ln_c_b, w_out_c, w_mem_kv, w_out_x, w_o, cores=None):
    f = lambda a: np.asarray(a, dtype=np.float32)
    x, mem = f(x), f(mem)
    w_in, w_out_a, w_pool, w_out_c, w_out_x, w_o, w_mem_kv = map(f, (w_in, w_out_a, w_pool, w_out_c, w_out_x, w_o, w_mem_kv))
    conv_c_w = f(conv_c_w)
    wmain = np.concatenate([_build_wmain(l, w_in, w_out_a, w_pool, w_out_c, w_out_x, w_o, conv_c_w) for l in range(DEPTH)], axis=0)
    wmem = np.concatenate([_build_wmem(l, w_mem_kv) for l in range(DEPTH)], axis=0)
    params = np.concatenate([_build_params(l, f(g_pre), f(g_post), f(g_mem), f(conv_a_w), f(pool_scale), conv_c_w,
                                           f(conv_c_b), f(ln_c_g), f(ln_c_b)) for l in range(DEPTH)], axis=0)
    gv = np.zeros((128, 128), np.float32)
    for l in range(DEPTH):
        gv[:, l * 64:l * 64 + 32] = _fm(f(g_pre)[l], 32)
        gv[:, l * 64 + 32:l * 64 + 64] = _fm(f(g_post)[l], 32)
    in_maps = []
    for c in (range(NCORES) if cores is None else cores):
        b, sc = c // 4, c % 4
        t0 = sc * TOK
        xt = np.zeros((D, NTOK), np.float32)
        if t0 == 0:
            xt[:, HALO:] = x[b, 0:TOK, :].T
        else:
            xt[:, :] = x[b, t0 - HALO:t0 + TOK, :].T
        auxa = np.zeros((128, 128), np.float32)
        for g in range(4):
            w = 2 ** (g + 1)
            for t in range(16):
                cnt = min(t + 1, w) if t0 == 0 else w
                auxa[:, g * 16 + t] = 1.0 / cnt
        auxa[:, 64:128] = 0.0 if t0 == 0 else 1.0
        in_maps.append({"xT": xt, "memT": np.ascontiguousarray(mem[b].T), "wmain": wmain, "wmem": wmem,
                        "params": params, "gvec": gv, "aux": auxa})
    return in_maps


_PROG = {}


def kernel(**inputs):
    in_maps = _prep_inputs(**inputs)
    if "nc" not in _PROG:
        _PROG["nc"], _ = build_program()
    nc = _PROG["nc"]
    res = run_bass_kernel_spmd(nc, in_maps, core_ids=list(range(NCORES)))
    B = inputs["x"].shape[0]
    out = np.empty((B, SEQ, D), np.float32)
    for c in range(NCORES):
        b, sc = c // 4, c % 4
        o = np.asarray(res.results[c]["outT"])
        out[b, sc * TOK:(sc + 1) * TOK, :] = o[:, HALO:].T
    return out
```

```python
import os
import contextlib
import numpy as np
import concourse.bass as bass
import concourse.mybir as mybir
from concourse.bass_utils import run_bass_kernel_spmd

F32 = mybir.dt.float32
BF16 = mybir.dt.bfloat16
ALU = mybir.AluOpType
AF = mybir.ActivationFunctionType

D = 4096
NCORES = 8
SEQ = 8192
TOK = 2048
HALO = 64
NTOK = TOK + HALO
NT = 704
NS = 352
NTILES = 3
DEPTH = 2
NMEM = 256
EPS = 1e-6
PADW = 736
NBW = 3
NTMP = 6
NOP_CYC = int(os.environ.get("MK_NOP_CYC", "0"))
NOP_EVERY = int(os.environ.get("MK_NOP_EVERY", "1"))
NIO = 3

C_V, C_B, C_C, C_Z = 0, 2048, 4096, 6144
C_U, C_ZB = 8192, 10240
C_AC, C_GT, C_ZC = 12288, 14336, 16384
C_Q = 18432
C_G = 19456

P_GPRE, P_GPOST, P_GMEM = 0, 32, 64
P_CAW = 96
P_PS = 144
P_CCW = 160
P_CCB = 656
P_LNG = 672
P_LNB = 688
NPAR = 704


def _slabs(wcols):
    k = wcols.shape[0] // 128
    return wcols.reshape(k, 128, 128).transpose(1, 0, 2).reshape(128, k * 128)


def _main_entries():
    ent = []
    for j in range(16):
        ent += [("in", C_V + j * 128), ("in", C_C + j * 128), ("in", C_B + j * 128), ("in", C_Z + j * 128)]
    for j in range(16):
        ent += [("in", C_U + j * 128), ("in", C_ZB + j * 128)]
    for oc in range(32):
        ent += [("in", C_G + oc * 128), ("in", C_G + 4096 + oc * 128), ("ab", oc)]
        if oc % 2 == 1:
            j = oc // 2
            ent += [("in", C_GT + j * 128), ("in", C_AC + j * 128)]
    for i in range(8):
        ent += [("in", C_Q + i * 128)]
    for j in range(16):
        ent += [("in", C_ZC + j * 128)]
    for oc in range(32):
        ent += [("in", C_G + 8192 + oc * 128), ("in", C_G + 12288 + oc * 128), ("cx", oc)]
    for oc in range(32):
        ent += [("wo", oc)]
    return ent


def _entry_cols(kind):
    return {"in": 4096, "ab": 20 * 128, "cx": 24 * 128, "wo": 4096}[kind]


MAIN_ENT = _main_entries()
MAIN_OFF = np.concatenate([[0], np.cumsum([_entry_cols(k) for k, _ in MAIN_ENT])]).astype(np.int64)
MAIN_COLS = int(MAIN_OFF[-1])
MEM_COLS = 16 * 4096


def _build_wmain(l, w_in, w_out_a, w_pool, w_out_c, w_out_x, w_o, conv_c_w):
    out = np.empty((128, MAIN_COLS), np.float32)
    pidx = np.arange(128)
    for i, (kind, a) in enumerate(MAIN_ENT):
        o = int(MAIN_OFF[i])
        if kind == "in":
            out[:, o:o + 4096] = _slabs(w_in[l][:, a:a + 128])
        elif kind == "ab":
            g, ol = a // 8, a % 8
            out[:, o:o + 2048] = _slabs(w_out_a[l][:, a * 128:(a + 1) * 128])
            out[:, o + 2048:o + 2560] = _slabs(w_pool[l][g][:, ol * 128:(ol + 1) * 128])
        elif kind == "cx":
            out[:, o:o + 2048] = _slabs(w_out_c[l][:, a * 128:(a + 1) * 128])
            out[:, o + 2048:o + 3072] = _slabs(w_out_x[l][:, a * 128:(a + 1) * 128])
        else:
            out[:, o:o + 4096] = _slabs(w_o[l][:, a * 128:(a + 1) * 128])
    return out


def _build_wmem(l, w_mem_kv):
    out = np.empty((128, MEM_COLS), np.float32)
    w = w_mem_kv[l]
    for oc in range(8):
        out[:, oc * 4096:(oc + 1) * 4096] = _slabs(w[:, oc * 128:(oc + 1) * 128])
    e = 8
    for ch in range(2):
        for q in range(4):
            blk = w[q * 1024:(q + 1) * 1024, 1024 + ch * 512:1024 + (ch + 1) * 512]
            out[:, e * 4096:(e + 1) * 4096] = blk.reshape(8, 128, 512).transpose(1, 0, 2).reshape(128, 4096)
            e += 1
    return out


def _fm(v, n):
    return np.ascontiguousarray(v.reshape(n, 128).T)


def _build_params(l, g_pre, g_post, g_mem, conv_a_w, pool_scale, conv_c_w, conv_c_b, ln_c_g, ln_c_b):
    p = np.zeros((128, NPAR), np.float32)
    p[:, P_GPRE:P_GPRE + 32] = _fm(g_pre[l], 32)
    p[:, P_GPOST:P_GPOST + 32] = _fm(g_post[l], 32)
    p[:, P_GMEM:P_GMEM + 32] = _fm(g_mem[l], 32)
    p[:, P_CAW:P_CAW + 48] = conv_a_w[l].reshape(3, 16, 128).transpose(2, 1, 0).reshape(128, 48)
    p[:, P_PS:P_PS + 16] = _fm(pool_scale[l], 16)
    p[:, P_CCW:P_CCW + 496] = conv_c_w[l].reshape(31, 16, 128).transpose(2, 1, 0).reshape(128, 496)
    p[:, P_CCB:P_CCB + 16] = _fm(conv_c_b[l], 16)
    p[:, P_LNG:P_LNG + 16] = _fm(ln_c_g[l], 16)
    p[:, P_LNB:P_LNB + 16] = _fm(ln_c_b[l], 16)
    return p


ENGS = ("pe", "act", "dve", "pool", "sp")
SAME_SYNC = os.environ.get("MK_SAME_SYNC", "1") == "1"


class Sched:
    def __init__(self):
        self.ops = {e: [] for e in ENGS}
        self.res = {}
        self.waited = {e: {} for e in ENGS}
        self.epoch = 0
        self.dma_cnt = {}
        self.const = set()

    def _need(self, eng, o, ev):
        kind, key, val = ev
        if kind == "c" and key == eng and (eng == "pe" or not SAME_SYNC):
            return
        wk = (kind, key)
        if self.waited[eng].get(wk, -1) >= val:
            return
        self.waited[eng][wk] = val
        o["waits"].append(ev)
        if kind == "c":
            self.ops[key][val]["sig"] = True

    def op(self, eng, fn, reads=(), writes=(), dsem=None):
        o = dict(fn=fn, waits=[], sig=False, dsem=dsem, epoch=self.epoch)
        idx = len(self.ops[eng])
        deps = []
        for r in reads:
            st = self.res.get(r)
            if st and st[0]:
                deps.append(st[0])
        for w in writes:
            st = self.res.get(w)
            if st:
                if st[0]:
                    deps.append(st[0])
                deps += st[1]
        for ev in deps:
            self._need(eng, o, ev)
        self.ops[eng].append(o)
        if dsem:
            c = self.dma_cnt.get(dsem, 0) + 1
            self.dma_cnt[dsem] = c
            ev = ("d", dsem, c * 16)
        else:
            ev = ("c", eng, idx)
        for r in reads:
            if r in self.const:
                continue
            self.res.setdefault(r, [None, []])[1].append(ev)
        for w in writes:
            self.res[w] = [ev, []]
        return ev

    def finalize(self):
        self.nepoch = self.epoch + 1
        for e in ENGS:
            counts = {}
            for o in self.ops[e]:
                if o["sig"]:
                    counts[o["epoch"]] = counts.get(o["epoch"], 0) + 1
                o["cnt"] = counts.get(o["epoch"], 0)

    def emit_engine(self, eng, e, semtab, dsems):
        for o in self.ops[eng]:
            for (kind, key, val) in o["waits"]:
                if kind == "c":
                    t = self.ops[key][val]
                    e.wait_ge(semtab[key][t["epoch"]], t["cnt"])
                else:
                    e.wait_ge(dsems[key], val)
            ins = o["fn"](e)
            if o["sig"]:
                ins.then_inc(semtab[eng][o["epoch"]], 1)
            if o["dsem"]:
                ins.then_inc(dsems[o["dsem"]], 16)


def build_program(depth=DEPTH, ntiles=NTILES):
    nc = bass.Bass("TRN2", target_bir_lowering=False)
    xT = nc.dram_tensor("xT", [D, NTOK], F32, kind="ExternalInput").ap()
    memT = nc.dram_tensor("memT", [D, NMEM], F32, kind="ExternalInput").ap()
    wmain = nc.dram_tensor("wmain", [DEPTH * 128, MAIN_COLS], F32, kind="ExternalInput").ap()
    wmem = nc.dram_tensor("wmem", [DEPTH * 128, MEM_COLS], F32, kind="ExternalInput").ap()
    params = nc.dram_tensor("params", [DEPTH * 128, NPAR], F32, kind="ExternalInput").ap()
    gvin = nc.dram_tensor("gvec", [128, 128], F32, kind="ExternalInput").ap()
    aux = nc.dram_tensor("aux", [128, 128], F32, kind="ExternalInput").ap()
    outT = nc.dram_tensor("outT", [D, NTOK], F32, kind="ExternalOutput").ap()
    ybuf = nc.dram_tensor("ybuf", [D, NT], F32, kind="Internal").ap()
    dcbuf = nc.dram_tensor("dcbuf", [2048, NT], F32, kind="Internal").ap()

    S = Sched()
    es = contextlib.ExitStack()
    with es:
        big = es.enter_context(nc.sbuf_tensor("big", [128, 45056], BF16))
        mg = es.enter_context(nc.sbuf_tensor("mg", [128, 22528], BF16))
        W = [es.enter_context(nc.sbuf_tensor(f"w{i}", [128, 4096], BF16)) for i in range(NBW)]
        T = [es.enter_context(nc.sbuf_tensor(f"t{i}", [128, PADW], F32)) for i in range(NTMP)]
        IO = [es.enter_context(nc.sbuf_tensor(f"io{i}", [128, NT], F32)) for i in range(NIO)]
        rstdp = es.enter_context(nc.sbuf_tensor("rstdp", [128, NT], F32))
        ones32 = es.enter_context(nc.sbuf_tensor("ones32", [128, 128], F32))
        ones16 = es.enter_context(nc.sbuf_tensor("ones16", [128, 128], BF16))
        par = es.enter_context(nc.sbuf_tensor("par", [128, NPAR], F32))
        gvec = es.enter_context(nc.sbuf_tensor("gvecs", [128, 128], F32))
        auxs = es.enter_context(nc.sbuf_tensor("auxs", [128, 128], F32))
        KT = es.enter_context(nc.sbuf_tensor("KT", [128, 8 * 256], BF16))
        VV = es.enter_context(nc.sbuf_tensor("VV", [128, 2 * 1024], BF16))
        acar = es.enter_context(nc.sbuf_tensor("acar", [128, 16 * 4], F32))
        ucar = es.enter_context(nc.sbuf_tensor("ucar", [128, 16 * 16], F32))
        ccar = es.enter_context(nc.sbuf_tensor("ccar", [128, 16 * 32], F32))
        lnA = es.enter_context(nc.sbuf_tensor("lnA", [128, NT], F32))
        lnB = es.enter_context(nc.sbuf_tensor("lnB", [128, NT], F32))
        ps = es.enter_context(nc.psum_tensor("ps", [128, 8, 512], F32))

        S.const.update({("aux",), ("ones",), ("gvec",)})

        def v2(ap):
            return ap.rearrange("p (a b) -> p a b", a=2)

        def hT(kc, st=None):
            if st is None:
                return big[:, kc * NT:(kc + 1) * NT]
            return big[:, kc * NT + st * NS: kc * NT + (st + 1) * NS]

        def s0(j, st=None):
            return hT(32 + j, st)

        def s1(j, st=None):
            return hT(48 + j, st)

        def mgc(kc, st=None):
            if st is None:
                return mg[:, kc * NT:(kc + 1) * NT]
            return mg[:, kc * NT + st * NS: kc * NT + (st + 1) * NS]

        def slot(s):
            return ps[:, 2 * s:2 * s + 2, 0:NS]

        def pcol(c):
            return par[:, c:c + 1]

        def gcol(c):
            return gvec[:, c:c + 1]

        st_ = dict(tmp=0, slot=0, w=0, io=0)

        def tmp():
            i = st_["tmp"]
            st_["tmp"] = (i + 1) % NTMP
            return i

        def io():
            i = st_["io"]
            st_["io"] = (i + 1) % NIO
            return i

        def next_slot():
            s = st_["slot"]
            st_["slot"] = (s + 1) % 3
            return s

        def rk_slot(s):
            return [("ps", 2 * s), ("ps", 2 * s + 1)]

        def load_w(src_ap, ncols):
            s = st_["w"]
            st_["w"] = (s + 1) % NBW
            S.op("pool", lambda e, s=s, src_ap=src_ap, ncols=ncols: e.dma_start(out=W[s][:, 0:ncols], in_=src_ap),
                 reads=[], writes=[("w", s)], dsem=f"w{s}")
            return s

        ent_pos = dict(i=0)

        def next_main(l, kind):
            i = ent_pos["i"]
            k, _ = MAIN_ENT[i]
            assert k == kind, (k, kind, i)
            o = int(MAIN_OFF[i])
            n = _entry_cols(k)
            ent_pos["i"] = i + 1
            return load_w(wmain[l * 128:(l + 1) * 128, o:o + n], n)

        def mm_job(ws, nk, rhs_fn, rhs_keys, slab0=0, sl=None):
            if sl is None:
                sl = next_slot()
            for kc in range(nk):
                for st in range(2):
                    S.op("pe", lambda e, sl=sl, ws=ws, kc=kc, st=st, nk=nk: e.matmul(
                        ps[:, 2 * sl + st, 0:NS], W[ws][:, (slab0 + kc) * 128:(slab0 + kc + 1) * 128],
                        rhs_fn(kc, st), start=(kc == 0), stop=(kc == nk - 1)),
                        reads=[("w", ws), rhs_keys(kc)], writes=[("ps", 2 * sl + st)])
            st_["jobs"] = st_.get("jobs", 0) + 1
            if NOP_CYC > 0 and st_["jobs"] % NOP_EVERY == 0:
                S.op("pe", lambda e: e.nop(cycle_cnt=NOP_CYC))
            return sl

        def proj_job(l):
            ws = next_main(l, "in")
            return mm_job(ws, 32, hT, lambda kc: ("big", kc))

        def act(fn, reads, writes):
            return S.op("act", fn, reads, writes)

        def dve(fn, reads, writes):
            return S.op("dve", fn, reads, writes)

        def ones_reduce(src, key):
            for st in range(2):
                S.op("pe", lambda e, st=st: e.matmul(ps[:, 6 + st, 0:NS], ones32[:], src[:, st * NS:(st + 1) * NS],
                                                     start=True, stop=True), [key, ("ones",)], [("ps", 6 + st)])

        def rstd_from_ps(dst, key, scale):
            sd = tmp()
            act(lambda e, sd=sd: e.activation(out=v2(T[sd][:, 0:NT]), in_=ps[:, 6:8, 0:NS], func=AF.Sqrt,
                                              bias=EPS, scale=scale), [("ps", 6), ("ps", 7)], [("T", sd)])
            dve(lambda e, sd=sd: e.reciprocal(out=dst[:], in_=T[sd][:, 0:NT]), [("T", sd)], [key])

        dve(lambda e: e.memset(ones32[:], 1.0), [], [("ones",)])
        dve(lambda e: e.memset(ones16[:], 1.0), [], [("ones",)])
        S.op("sp", lambda e: e.dma_start(out=auxs[:], in_=aux), [], [("aux",)], dsem="aux")
        S.op("sp", lambda e: e.dma_start(out=gvec[:], in_=gvin), [], [("gvec",)], dsem="gvec")

        def x_src(l):
            return xT if l == 0 else outT

        def load_io(l, ti, kc, src=None, key=None):
            i = io()
            if src is None:
                src = x_src(l)[kc * 128:(kc + 1) * 128, ti * NT:(ti + 1) * NT]
                rk = [("o", ti, kc)] if l > 0 else []
            else:
                rk = [key]
            S.op("sp", lambda e, i=i, src=src: e.dma_start(out=IO[i][:], in_=src), reads=rk, writes=[("IO", i)],
                 dsem=f"lio{i}")
            return i

        def phase1_pieces(l, ti):
            P = []

            def p1(kc):
                def f():
                    i = load_io(l, ti, kc)
                    if kc == 0:
                        act(lambda e, i=i: e.activation(out=lnA[:], in_=IO[i][:], func=AF.Square), [("IO", i)], [("lnA",)])
                    else:
                        q = tmp()
                        act(lambda e, i=i, q=q: e.activation(out=T[q][:, 0:NT], in_=IO[i][:], func=AF.Square),
                            [("IO", i)], [("T", q)])
                        dve(lambda e, q=q: e.tensor_tensor(out=lnA[:], in0=lnA[:], in1=T[q][:, 0:NT], op=ALU.add),
                            [("T", q), ("lnA",)], [("lnA",)])
                return f

            def fin():
                ones_reduce(lnA, ("lnA",))
                rstd_from_ps(lnA, ("lnA",), 1.0 / D)

            def p2(kc):
                def f():
                    i = load_io(l, ti, kc)
                    dve(lambda e, i=i, kc=kc: e.scalar_tensor_tensor(
                        out=hT(kc), in0=IO[i][:], scalar=gcol(l * 64 + kc), in1=lnA[:],
                        op0=ALU.mult, op1=ALU.mult), [("IO", i), ("lnA",), ("gvec",)], [("big", kc)])
                return f

            for kc in range(32):
                P.append(p1(kc))
            P.append(fin)
            for kc in range(32):
                P.append(p2(kc))
            return P

        def final_pieces(l, ti):
            P = []

            def fp(oc):
                def f():
                    a = load_io(l, ti, oc, src=ybuf[oc * 128:(oc + 1) * 128, :], key=("yb", oc))
                    b = load_io(l, ti, oc)
                    dve(lambda e, a=a, oc=oc: e.scalar_tensor_tensor(
                        out=IO[a][:], in0=IO[a][:], scalar=gcol(l * 64 + 32 + oc), in1=rstdp[:], op0=ALU.mult, op1=ALU.mult),
                        [("IO", a), ("rstdp",), ("gvec",)], [("IO", a)])
                    dve(lambda e, a=a, b=b: e.tensor_tensor(out=IO[b][:], in0=IO[b][:], in1=IO[a][:], op=ALU.add),
                        [("IO", a), ("IO", b)], [("IO", b)])
                    S.op("sp", lambda e, b=b, oc=oc: e.dma_start(
                        out=outT[oc * 128:(oc + 1) * 128, ti * NT:(ti + 1) * NT], in_=IO[b][:]),
                        reads=[("IO", b)], writes=[("o", ti, oc)], dsem=f"sio{b}")
                return f

            for oc in range(32):
                P.append(fp(oc))
            return P

        def drain(P, n):
            for _ in range(n):
                if P:
                    P.pop(0)()

        def layer_setup(l):
            S.op("sp", lambda e, l=l: e.dma_start(out=par[:], in_=params[l * 128:(l + 1) * 128, :]),
                 [], [("par",)], dsem="par")
            dve(lambda e: e.memset(acar[:], 0.0), [], [("acar",)])
            dve(lambda e: e.memset(ucar[:], 0.0), [], [("ucar",)])
            dve(lambda e: e.memset(ccar[:], 0.0), [], [("ccar",)])
            for kc in range(32):
                i = tmp()
                S.op("sp", lambda e, i=i, kc=kc: e.dma_start(out=T[i][:, 0:NMEM], in_=memT[kc * 128:(kc + 1) * 128, :]),
                     [], [("T", i)], dsem=f"ld{i}")
                q = tmp()
                act(lambda e, i=i, q=q: e.activation(out=T[q][:, 0:NMEM], in_=T[i][:, 0:NMEM], func=AF.Square),
                    [("T", i)], [("T", q)])
                S.op("pe", lambda e, q=q, kc=kc: e.matmul(ps[:, 6, 0:NMEM], ones32[:], T[q][:, 0:NMEM],
                                                          start=(kc == 0), stop=(kc == 31)),
                     [("T", q), ("ones",)], [("ps", 6)])
            sd = tmp()
            act(lambda e, sd=sd: e.activation(out=T[sd][:, 0:NMEM], in_=ps[:, 6, 0:NMEM], func=AF.Sqrt,
                                              bias=EPS, scale=1.0 / D), [("ps", 6)], [("T", sd)])
            rm = lnB
            dve(lambda e, sd=sd: e.reciprocal(out=rm[:, 0:NMEM], in_=T[sd][:, 0:NMEM]), [("T", sd)], [("lnB",)])

            def mh(kc):
                return big[:, 48 * NT + kc * NMEM: 48 * NT + (kc + 1) * NMEM]
            MHK = [("big", 48 + u) for u in range(16)]
            for kc in range(32):
                i = tmp()
                S.op("sp", lambda e, i=i, kc=kc: e.dma_start(out=T[i][:, 0:NMEM], in_=memT[kc * 128:(kc + 1) * 128, :]),
                     [], [("T", i)], dsem=f"ld{i}")
                dve(lambda e, i=i, kc=kc: e.scalar_tensor_tensor(
                    out=mh(kc), in0=T[i][:, 0:NMEM], scalar=pcol(P_GMEM + kc), in1=rm[:, 0:NMEM],
                    op0=ALU.mult, op1=ALU.mult), [("T", i), ("lnB",), ("par",)],
                    [("big", 48 + (kc * NMEM) // NT), ("big", 48 + (kc * NMEM + NMEM - 1) // NT)])
            for oc in range(8):
                ws = load_w(wmem[l * 128:(l + 1) * 128, oc * 4096:(oc + 1) * 4096], 4096)
                sl = next_slot()
                for kc in range(32):
                    S.op("pe", lambda e, sl=sl, ws=ws, kc=kc: e.matmul(
                        ps[:, 2 * sl, 0:NMEM], W[ws][:, kc * 128:(kc + 1) * 128], mh(kc),
                        start=(kc == 0), stop=(kc == 31)),
                        [("w", ws)] + MHK, [("ps", 2 * sl)])
                act(lambda e, sl=sl, oc=oc: e.activation(out=KT[:, oc * 256:(oc + 1) * 256], in_=ps[:, 2 * sl, 0:NMEM],
                                                         func=AF.Copy), [("ps", 2 * sl)], [("KT",)])
            e_i = 8
            for ch in range(2):
                for q4 in range(4):
                    ws = load_w(wmem[l * 128:(l + 1) * 128, e_i * 4096:(e_i + 1) * 4096], 4096)
                    e_i += 1
                    for i8 in range(8):
                        kc = q4 * 8 + i8
                        for mc in range(2):
                            S.op("pe", lambda e, ws=ws, kc=kc, mc=mc, i8=i8: e.matmul(
                                ps[:, 6 + mc, 0:512], mh(kc)[:, mc * 128:(mc + 1) * 128],
                                W[ws][:, i8 * 512:(i8 + 1) * 512], start=(kc == 0), stop=(kc == 31)),
                                [("w", ws)] + MHK, [("ps", 6 + mc)])
                for mc in range(2):
                    act(lambda e, mc=mc, ch=ch: e.activation(
                        out=VV[:, mc * 1024 + ch * 512: mc * 1024 + (ch + 1) * 512], in_=ps[:, 6 + mc, 0:512],
                        func=AF.Copy), [("ps", 6 + mc)], [("VV",)])

        def front_a(l, ti, pend):
            for j in range(16):
                sl = proj_job(l)
                vS = tmp()
                act(lambda e, sl=sl, vS=vS: e.activation(out=v2(T[vS][:, 0:NT]), in_=slot(sl), func=AF.Copy),
                    rk_slot(sl), [("T", vS)])
                sl = proj_job(l)
                cv = tmp()
                act(lambda e, cv=cv, j=j: e.activation(out=T[cv][:, 0:4], in_=acar[:, j * 4:(j + 1) * 4], func=AF.Copy),
                    [("acar",)], [("T", cv)])
                dve(lambda e, sl=sl, vS=vS, cv=cv: e.tensor_tensor(
                    out=v2(T[cv][:, 4:4 + NT]), in0=slot(sl), in1=v2(T[vS][:, 0:NT]), op=ALU.mult),
                    rk_slot(sl) + [("T", vS), ("T", cv)], [("T", cv)])
                act(lambda e, cv=cv, j=j: e.activation(out=acar[:, j * 4:(j + 1) * 4], in_=T[cv][:, NT:NT + 4], func=AF.Copy),
                    [("T", cv)], [("acar",)])
                acc = tmp()
                dve(lambda e, cv=cv, acc=acc, j=j: e.tensor_scalar(
                    out=T[acc][:, 0:NT], in0=T[cv][:, 2:2 + NT], scalar1=pcol(P_CAW + j * 3 + 0), scalar2=None,
                    op0=ALU.mult), [("T", cv), ("par",)], [("T", acc)])
                for k in (1, 2):
                    dve(lambda e, cv=cv, acc=acc, j=j, k=k: e.scalar_tensor_tensor(
                        out=T[acc][:, 0:NT], in0=T[cv][:, 2 + k:2 + k + NT], scalar=pcol(P_CAW + j * 3 + k),
                        in1=T[acc][:, 0:NT], op0=ALU.mult, op1=ALU.add), [("T", cv), ("T", acc), ("par",)], [("T", acc)])
                sl = proj_job(l)
                dve(lambda e, sl=sl, acc=acc: e.tensor_tensor(
                    out=v2(T[acc][:, 0:NT]), in0=slot(sl), in1=v2(T[acc][:, 0:NT]), op=ALU.mult),
                    rk_slot(sl) + [("T", acc)], [("T", acc)])
                sl = proj_job(l)
                sz = tmp()
                act(lambda e, sl=sl, sz=sz: e.activation(out=v2(T[sz][:, 0:NT]), in_=slot(sl), func=AF.Silu),
                    rk_slot(sl), [("T", sz)])
                dve(lambda e, acc=acc, sz=sz, j=j: e.tensor_tensor(
                    out=s0(j), in0=T[acc][:, 0:NT], in1=T[sz][:, 0:NT], op=ALU.mult),
                    [("T", acc), ("T", sz)], [("big", 32 + j)])
                drain(pend, 2)

        def front_b(l, ti):
            for j in range(16):
                g = j // 4
                wwin = 2 ** (g + 1)
                sl = proj_job(l)
                U = tmp()
                act(lambda e, U=U, j=j: e.activation(out=T[U][:, 0:16], in_=ucar[:, j * 16:(j + 1) * 16], func=AF.Copy),
                    [("ucar",)], [("T", U)])
                act(lambda e, sl=sl, U=U: e.activation(out=v2(T[U][:, 16:16 + NT]), in_=slot(sl), func=AF.Copy),
                    rk_slot(sl) + [("T", U)], [("T", U)])
                act(lambda e, U=U, j=j: e.activation(out=ucar[:, j * 16:(j + 1) * 16], in_=T[U][:, NT:NT + 16], func=AF.Copy),
                    [("T", U)], [("ucar",)])
                pp = [tmp(), tmp()]
                Sb = U
                sh = 1
                for step in range(g + 1):
                    lo = 2 * sh - 1
                    Tn = pp[step % 2]
                    dve(lambda e, Sb=Sb, Tn=Tn, lo=lo, sh=sh: e.tensor_tensor(
                        out=T[Tn][:, lo:16 + NT], in0=T[Sb][:, lo:16 + NT], in1=T[Sb][:, lo - sh:16 + NT - sh],
                        op=ALU.add), [("T", Sb)], [("T", Tn)])
                    Sb = Tn
                    sh *= 2
                P = tmp()
                dve(lambda e, Sb=Sb, U=U, P=P, wwin=wwin: e.scalar_tensor_tensor(
                    out=T[P][:, 0:NT], in0=T[Sb][:, 16:16 + NT], scalar=1.0 / wwin, in1=T[U][:, 16:16 + NT],
                    op0=ALU.mult, op1=ALU.subtract), [("T", Sb), ("T", U)], [("T", P)])
                if ti == 0:
                    fx = pp[(g + 1) % 2]
                    dve(lambda e, Sb=Sb, fx=fx, g=g: e.tensor_tensor(
                        out=T[fx][:, 0:16], in0=T[Sb][:, 16 + HALO:16 + HALO + 16], in1=auxs[:, g * 16:(g + 1) * 16],
                        op=ALU.mult), [("T", Sb), ("aux",)], [("T", fx)])
                    dve(lambda e, U=U, fx=fx, P=P: e.tensor_tensor(
                        out=T[P][:, HALO:HALO + 16], in0=T[fx][:, 0:16], in1=T[U][:, 16 + HALO:16 + HALO + 16],
                        op=ALU.subtract), [("T", fx), ("T", U), ("T", P)], [("T", P)])
                sl = proj_job(l)
                sz = tmp()
                act(lambda e, sl=sl, sz=sz: e.activation(out=v2(T[sz][:, 0:NT]), in_=slot(sl), func=AF.Silu),
                    rk_slot(sl), [("T", sz)])
                dve(lambda e, P=P, sz=sz, j=j: e.scalar_tensor_tensor(
                    out=s1(j), in0=T[P][:, 0:NT], scalar=pcol(P_PS + j), in1=T[sz][:, 0:NT],
                    op0=ALU.mult, op1=ALU.mult), [("T", P), ("T", sz), ("par",)], [("big", 48 + j)])

        def back(l, first, nk_y, rhs_y, keys_y, kind, hook=None, mid=None):
            for oc in range(32):
                sl = proj_job(l)
                sg1 = tmp()
                act(lambda e, sl=sl, sg1=sg1: e.activation(out=v2(T[sg1][:, 0:NT]), in_=slot(sl), func=AF.Sigmoid),
                    rk_slot(sl), [("T", sg1)])
                if mid is not None:
                    mid()
                sl = proj_job(l)
                sg2 = tmp()
                act(lambda e, sl=sl, sg2=sg2: e.activation(out=v2(T[sg2][:, 0:NT]), in_=slot(sl), func=AF.Sigmoid),
                    rk_slot(sl), [("T", sg2)])
                if mid is not None:
                    mid()
                ws = next_main(l, kind)
                sx = mm_job(ws, 16, s0, lambda kc: ("big", 32 + kc), slab0=0)
                sy = mm_job(ws, nk_y, lambda kc, st, oc=oc: rhs_y(oc, kc, st), lambda kc, oc=oc: keys_y(oc, kc), slab0=16)
                dve(lambda e, sx=sx, sg1=sg1: e.tensor_tensor(
                    out=v2(T[sg1][:, 0:NT]), in0=slot(sx), in1=v2(T[sg1][:, 0:NT]), op=ALU.mult),
                    rk_slot(sx) + [("T", sg1)], [("T", sg1)])
                dve(lambda e, sy=sy, sg2=sg2: e.tensor_tensor(
                    out=v2(T[sg2][:, 0:NT]), in0=slot(sy), in1=v2(T[sg2][:, 0:NT]), op=ALU.mult),
                    rk_slot(sy) + [("T", sg2)], [("T", sg2)])
                if first:
                    dve(lambda e, sg1=sg1, sg2=sg2, oc=oc: e.tensor_tensor(
                        out=mgc(oc), in0=T[sg1][:, 0:NT], in1=T[sg2][:, 0:NT], op=ALU.add),
                        [("T", sg1), ("T", sg2)], [("mg", oc)])
                else:
                    dve(lambda e, sg1=sg1, sg2=sg2: e.tensor_tensor(
                        out=T[sg1][:, 0:NT], in0=T[sg1][:, 0:NT], in1=T[sg2][:, 0:NT], op=ALU.add),
                        [("T", sg1), ("T", sg2)], [("T", sg1)])
                    dve(lambda e, sg1=sg1, oc=oc: e.tensor_tensor(
                        out=mgc(oc), in0=mgc(oc), in1=T[sg1][:, 0:NT], op=ALU.add),
                        [("T", sg1), ("mg", oc)], [("mg", oc)])
                if hook is not None:
                    hook(oc)

        def conv_chunk(l, j):
            tmp()
            sl = proj_job(l)
            sg = tmp()
            act(lambda e, sl=sl, sg=sg: e.activation(out=v2(T[sg][:, 0:NT]), in_=slot(sl), func=AF.Sigmoid),
                rk_slot(sl), [("T", sg)])
            sl = proj_job(l)
            gc = tmp()
            act(lambda e, gc=gc, j=j: e.activation(out=T[gc][:, 0:32], in_=ccar[:, j * 32:(j + 1) * 32], func=AF.Copy),
                [("ccar",)], [("T", gc)])
            dve(lambda e, sl=sl, sg=sg, gc=gc: e.tensor_tensor(
                out=v2(T[gc][:, 32:32 + NT]), in0=slot(sl), in1=v2(T[sg][:, 0:NT]), op=ALU.mult),
                rk_slot(sl) + [("T", sg), ("T", gc)], [("T", gc)])
            act(lambda e, gc=gc, j=j: e.activation(out=ccar[:, j * 32:(j + 1) * 32], in_=T[gc][:, NT:NT + 32], func=AF.Copy),
                [("T", gc)], [("ccar",)])
            acc = tmp()

            def taps(k0, k1):
                for k in range(k0, k1):
                    if k == 0:
                        dve(lambda e, gc=gc, acc=acc, j=j: e.tensor_scalar(
                            out=T[acc][:, 0:NT], in0=T[gc][:, 2:2 + NT], scalar1=pcol(P_CCW + j * 31), scalar2=pcol(P_CCB + j),
                            op0=ALU.mult, op1=ALU.add), [("T", gc), ("par",)], [("T", acc)])
                    else:
                        dve(lambda e, gc=gc, acc=acc, j=j, k=k: e.scalar_tensor_tensor(
                            out=T[acc][:, 0:NT], in0=T[gc][:, 2 + k:2 + k + NT], scalar=pcol(P_CCW + j * 31 + k),
                            in1=T[acc][:, 0:NT], op0=ALU.mult, op1=ALU.add), [("T", gc), ("T", acc), ("par",)], [("T", acc)])

            def tail():
                S.op("sp", lambda e, acc=acc, j=j: e.dma_start(out=dcbuf[j * 128:(j + 1) * 128, :], in_=T[acc][:, 0:NT]),
                     reads=[("T", acc)], writes=[("dc", j)], dsem=f"st{acc}")
                if j == 0:
                    act(lambda e, acc=acc: e.activation(out=lnB[:], in_=T[acc][:, 0:NT], func=AF.Square),
                        [("T", acc)], [("lnB",)])
                    dve(lambda e, acc=acc: e.tensor_copy(out=lnA[:], in_=T[acc][:, 0:NT]), [("T", acc)], [("lnA",)])
                else:
                    sq = tmp()
                    act(lambda e, acc=acc, sq=sq: e.activation(out=T[sq][:, 0:NT], in_=T[acc][:, 0:NT], func=AF.Square),
                        [("T", acc)], [("T", sq)])
                    dve(lambda e, acc=acc: e.tensor_tensor(out=lnA[:], in0=lnA[:], in1=T[acc][:, 0:NT], op=ALU.add),
                        [("T", acc), ("lnA",)], [("lnA",)])
                    dve(lambda e, sq=sq: e.tensor_tensor(out=lnB[:], in0=lnB[:], in1=T[sq][:, 0:NT], op=ALU.add),
                        [("T", sq), ("lnB",)], [("lnB",)])

            taps(0, 6)
            return [lambda: taps(6, 12), lambda: taps(12, 18), lambda: taps(18, 24),
                    lambda: taps(24, 31), tail]

        def front_c_x(l, ti):
            for i8 in range(8):
                sl = proj_job(l)
                act(lambda e, sl=sl, i8=i8: e.activation(out=v2(s1(8 + i8)), in_=slot(sl), func=AF.Copy),
                    rk_slot(sl), [("big", 48 + 8 + i8)])

            ones_reduce(lnA, ("lnA",))
            mu = tmp()
            dve(lambda e, mu=mu: e.tensor_scalar(out=v2(T[mu][:, 0:NT]), in0=ps[:, 6:8, 0:NS], scalar1=1.0 / 2048,
                                                 scalar2=None, op0=ALU.mult), [("ps", 6), ("ps", 7)], [("T", mu)])
            ones_reduce(lnB, ("lnB",))
            msq = tmp()
            dve(lambda e, mu=mu, msq=msq: e.tensor_tensor(out=T[msq][:, 0:NT], in0=T[mu][:, 0:NT], in1=T[mu][:, 0:NT],
                                                          op=ALU.mult), [("T", mu)], [("T", msq)])
            var = tmp()
            dve(lambda e, msq=msq, var=var: e.scalar_tensor_tensor(
                out=v2(T[var][:, 0:NT]), in0=ps[:, 6:8, 0:NS], scalar=1.0 / 2048, in1=v2(T[msq][:, 0:NT]),
                op0=ALU.mult, op1=ALU.subtract), [("ps", 6), ("ps", 7), ("T", msq)], [("T", var)])
            act(lambda e, var=var: e.activation(out=T[var][:, 0:NT], in_=T[var][:, 0:NT], func=AF.Sqrt, bias=EPS, scale=1.0),
                [("T", var)], [("T", var)])
            dve(lambda e, var=var: e.reciprocal(out=lnA[:], in_=T[var][:, 0:NT]), [("T", var)], [("lnA",)])
            dve(lambda e, mu=mu: e.scalar_tensor_tensor(out=lnB[:], in0=T[mu][:, 0:NT], scalar=-1.0, in1=lnA[:],
                                                        op0=ALU.mult, op1=ALU.mult), [("T", mu), ("lnA",)], [("lnB",)])

            def z_job(j):
                sl = proj_job(l)
                sz = tmp()
                act(lambda e, sl=sl, sz=sz: e.activation(out=v2(T[sz][:, 0:NT]), in_=slot(sl), func=AF.Silu),
                    rk_slot(sl), [("T", sz)])
                di = load_io(l, ti, j, src=dcbuf[j * 128:(j + 1) * 128, :], key=("dc", j))
                tn = tmp()
                dve(lambda e, tn=tn, di=di: e.tensor_tensor(out=T[tn][:, 0:NT], in0=IO[di][:], in1=lnA[:], op=ALU.mult),
                    [("IO", di), ("lnA",)], [("T", tn)])
                dve(lambda e, tn=tn: e.tensor_tensor(out=T[tn][:, 0:NT], in0=T[tn][:, 0:NT], in1=lnB[:], op=ALU.add),
                    [("T", tn), ("lnB",)], [("T", tn)])
                act(lambda e, tn=tn, j=j: e.activation(out=T[tn][:, 0:NT], in_=T[tn][:, 0:NT], func=AF.Silu,
                                                       bias=pcol(P_LNB + j), scale=pcol(P_LNG + j)),
                    [("T", tn), ("par",)], [("T", tn)])
                dve(lambda e, tn=tn, sz=sz, j=j: e.tensor_tensor(out=s0(j), in0=T[tn][:, 0:NT], in1=T[sz][:, 0:NT],
                                                                 op=ALU.mult), [("T", tn), ("T", sz)], [("big", 32 + j)])

            Ebuf = {}

            def scores(h):
                sls = [next_slot(), next_slot()]
                for mc in range(2):
                    for st in range(2):
                        for dc in range(2):
                            S.op("pe", lambda e, h=h, mc=mc, st=st, dc=dc, sl=sls[mc]: e.matmul(
                                ps[:, 2 * sl + st, 0:NS],
                                KT[:, (h * 2 + dc) * 256 + mc * 128:(h * 2 + dc) * 256 + (mc + 1) * 128],
                                s1(8 + h * 2 + dc, st), start=(dc == 0), stop=(dc == 1)),
                                [("KT",), ("big", 48 + 8 + h * 2 + dc)], [("ps", 2 * sls[mc] + st)])
                E = tmp()
                Eb = T[E].bitcast(BF16)
                for mc in range(2):
                    act(lambda e, mc=mc, sl=sls[mc], Eb=Eb: e.activation(
                        out=v2(Eb[:, mc * NT:(mc + 1) * NT]), in_=slot(sl), func=AF.Exp, scale=1.0 / 16.0),
                        rk_slot(sls[mc]) + [("T", E)], [("T", E)])
                Ebuf[h] = (E, Eb)

            def attv(h):
                E, Eb = Ebuf[h]
                for st in range(2):
                    for mc in range(2):
                        S.op("pe", lambda e, st=st, mc=mc, Eb=Eb: e.matmul(
                            ps[:, 6 + st, 0:NS], ones16[:], Eb[:, mc * NT + st * NS: mc * NT + (st + 1) * NS],
                            start=(mc == 0), stop=(mc == 1)), [("T", E), ("ones",)], [("ps", 6 + st)])
                rd = tmp()
                dve(lambda e, rd=rd: e.reciprocal(out=v2(T[rd][:, 0:NT]), in_=ps[:, 6:8, 0:NS]),
                    [("ps", 6), ("ps", 7)], [("T", rd)])
                for dcp in range(2):
                    sl = next_slot()
                    for st in range(2):
                        for mc in range(2):
                            S.op("pe", lambda e, sl=sl, st=st, mc=mc, h=h, dcp=dcp, Eb=Eb: e.matmul(
                                ps[:, 2 * sl + st, 0:NS],
                                VV[:, mc * 1024 + (h * 2 + dcp) * 128: mc * 1024 + (h * 2 + dcp + 1) * 128],
                                Eb[:, mc * NT + st * NS: mc * NT + (st + 1) * NS], start=(mc == 0), stop=(mc == 1)),
                                [("T", E), ("VV",)], [("ps", 2 * sl + st)])
                    dve(lambda e, sl=sl, rd=rd, h=h, dcp=dcp: e.tensor_tensor(
                        out=v2(s1(h * 2 + dcp)), in0=slot(sl), in1=v2(T[rd][:, 0:NT]), op=ALU.mult),
                        rk_slot(sl) + [("T", rd)], [("big", 48 + h * 2 + dcp)])

            steps = []
            for h in range(4):
                steps += [lambda h=h: scores(h), lambda h=h: attv(h)]
            for j in range(16):
                z_job(j)
                if steps:
                    steps.pop(0)()

        def wo_phase(l, ti, nxt):
            for oc in range(32):
                ws = next_main(l, "wo")
                sl = mm_job(ws, 32, mgc, lambda kc: ("mg", kc))
                t1 = tmp()
                act(lambda e, sl=sl, t1=t1: e.activation(out=v2(T[t1][:, 0:NT]), in_=slot(sl), func=AF.Copy),
                    rk_slot(sl), [("T", t1)])
                S.op("sp", lambda e, t1=t1, oc=oc: e.dma_start(out=ybuf[oc * 128:(oc + 1) * 128, :], in_=T[t1][:, 0:NT]),
                     reads=[("T", t1)], writes=[("yb", oc)], dsem=f"st{t1}")
                if oc == 0:
                    act(lambda e, sl=sl: e.activation(out=v2(lnB[:]), in_=slot(sl), func=AF.Square),
                        rk_slot(sl), [("lnB",)])
                else:
                    t2 = tmp()
                    act(lambda e, sl=sl, t2=t2: e.activation(out=v2(T[t2][:, 0:NT]), in_=slot(sl), func=AF.Square),
                        rk_slot(sl), [("T", t2)])
                    dve(lambda e, t2=t2: e.tensor_tensor(out=lnB[:], in0=lnB[:], in1=T[t2][:, 0:NT], op=ALU.add),
                        [("T", t2), ("lnB",)], [("lnB",)])
                drain(nxt, 3 if oc == 0 else 2)
            drain(nxt, len(nxt))
            assert ent_pos["i"] == len(MAIN_ENT)
            ones_reduce(lnB, ("lnB",))
            rstd_from_ps(rstdp, ("rstdp",), 1.0 / D)
            if ti == 0:
                dve(lambda e: e.tensor_tensor(out=rstdp[:, 0:HALO], in0=rstdp[:, 0:HALO], in1=auxs[:, 64:64 + HALO],
                                              op=ALU.mult), [("rstdp",), ("aux",)], [("rstdp",)])

        seq = [(l, ti) for l in range(depth) for ti in range(ntiles)]
        pend_final = []
        drain_all = lambda P: drain(P, len(P))
        drain_all(phase1_pieces(*seq[0]))
        for n, (l, ti) in enumerate(seq):
            if ti == 0:
                layer_setup(l)
            ent_pos["i"] = 0
            front_a(l, ti, pend_final)
            drain_all(pend_final)
            front_b(l, ti)
            pend_conv = []

            def conv_hook(oc, l=l, pend_conv=pend_conv):
                if oc % 2 == 1:
                    drain(pend_conv, len(pend_conv))
                    pend_conv.extend(conv_chunk(l, oc // 2))
                else:
                    drain(pend_conv, 1)

            back(l, True, 4, lambda oc, kc, st: s1(4 * (oc // 8) + kc, st), lambda oc, kc: ("big", 48 + 4 * (oc // 8) + kc), "ab",
                 hook=conv_hook, mid=lambda pend_conv=pend_conv: drain(pend_conv, 1))
            drain(pend_conv, len(pend_conv))
            front_c_x(l, ti)
            back(l, False, 8, lambda oc, kc, st: s1(kc, st), lambda oc, kc: ("big", 48 + kc), "cx")
            nxt = phase1_pieces(*seq[n + 1]) if n + 1 < len(seq) else []
            wo_phase(l, ti, nxt)
            pend_final = final_pieces(l, ti)
            S.epoch += 1
        drain_all(pend_final)

        S.finalize()
        dsem_names = sorted(S.dma_cnt.keys())
        dsems = {n: es.enter_context(nc.semaphore("d_" + n)) for n in dsem_names}
        semtab = {e: [es.enter_context(nc.semaphore(f"c_{e}_{k}")) for k in range(S.nepoch)] for e in ENGS}
        block = es.enter_context(nc.Block())

        @block.tensor
        def _(e):
            S.emit_engine("pe", e, semtab, dsems)

        @block.scalar
        def _(e):
            S.emit_engine("act", e, semtab, dsems)

        @block.vector
        def _(e):
            S.emit_engine("dve", e, semtab, dsems)

        @block.gpsimd
        def _(e):
            S.emit_engine("pool", e, semtab, dsems)

        @block.sync
        def _(e):
            S.emit_engine("sp", e, semtab, dsems)
            for n in dsem_names:
                if n.startswith("st") or n.startswith("sio"):
                    e.wait_ge(dsems[n], S.dma_cnt[n] * 16)
    stats = {e: len(S.ops[e]) for e in ENGS}
    return nc, stats


def _prep_inputs(x, mem, g_pre, g_post, g_mem, w_in, conv_a_w, w_out_a, pool_scale, w_pool, conv_c_w, conv_c_b,
                 ln_c_g, ln_c_b, w_out_c, w_mem_kv, w_out_x, w_o, cores=None):
    f = lambda a: np.asarray(a, dtype=np.float32)
    x, mem = f(x), f(mem)
    w_in, w_out_a, w_pool, w_out_c, w_out_x, w_o, w_mem_kv = map(f, (w_in, w_out_a, w_pool, w_out_c, w_out_x, w_o, w_mem_kv))
    conv_c_w = f(conv_c_w)
    wmain = np.concatenate([_build_wmain(l, w_in, w_out_a, w_pool, w_out_c, w_out_x, w_o, conv_c_w) for l in range(DEPTH)], axis=0)
    wmem = np.concatenate([_build_wmem(l, w_mem_kv) for l in range(DEPTH)], axis=0)
    params = np.concatenate([_build_params(l, f(g_pre), f(g_post), f(g_mem), f(conv_a_w), f(pool_scale), conv_c_w,
                                           f(conv_c_b), f(ln_c_g), f(ln_c_b)) for l in range(DEPTH)], axis=0)
    gv = np.zeros((128, 128), np.float32)
    for l in range(DEPTH):
        gv[:, l * 64:l * 64 + 32] = _fm(f(g_pre)[l], 32)
        gv[:, l * 64 + 32:l * 64 + 64] = _fm(f(g_post)[l], 32)
    in_maps = []
    for c in (range(NCORES) if cores is None else cores):
        b, sc = c // 4, c % 4
        t0 = sc * TOK
        xt = np.zeros((D, NTOK), np.float32)
        if t0 == 0:
            xt[:, HALO:] = x[b, 0:TOK, :].T
        else:
            xt[:, :] = x[b, t0 - HALO:t0 + TOK, :].T
        auxa = np.zeros((128, 128), np.float32)
        for g in range(4):
            w = 2 ** (g + 1)
            for t in range(16):
                cnt = min(t + 1, w) if t0 == 0 else w
                auxa[:, g * 16 + t] = 1.0 / cnt
        auxa[:, 64:128] = 0.0 if t0 == 0 else 1.0
        in_maps.append({"xT": xt, "memT": np.ascontiguousarray(mem[b].T), "wmain": wmain, "wmem": wmem,
                        "params": params, "gvec": gv, "aux": auxa})
    return in_maps


_PROG = {}


def kernel(**inputs):
    in_maps = _prep_inputs(**inputs)
    if "nc" not in _PROG:
        _PROG["nc"], _ = build_program()
    nc = _PROG["nc"]
    res = run_bass_kernel_spmd(nc, in_maps, core_ids=list(range(NCORES)))
    B = inputs["x"].shape[0]
    out = np.empty((B, SEQ, D), np.float32)
    for c in range(NCORES):
        b, sc = c // 4, c % 4
        o = np.asarray(res.results[c]["outT"])
        out[b, sc * TOK:(sc + 1) * TOK, :] = o[:, HALO:].T
    return out
```

```python
import os
import contextlib
import numpy as np
import concourse.bass as bass
import concourse.mybir as mybir
from concourse.bass_utils import run_bass_kernel_spmd

F32 = mybir.dt.float32
BF16 = mybir.dt.bfloat16
ALU = mybir.AluOpType
AF = mybir.ActivationFunctionType

D = 4096
NCORES = 8
SEQ = 8192
TOK = 2048
HALO = 64
NTOK = TOK + HALO
NT = 704
NS = 352
NTILES = 3
DEPTH = 2
NMEM = 256
EPS = 1e-6
PADW = 736
NBW = 3
NTMP = 6
NOP_CYC = int(os.environ.get("MK_NOP_CYC", "0"))
NOP_EVERY = int(os.environ.get("MK_NOP_EVERY", "1"))
NIO = 4

C_V, C_B, C_C, C_Z = 0, 2048, 4096, 6144
C_U, C_ZB = 8192, 10240
C_AC, C_GT, C_ZC = 12288, 14336, 16384
C_Q = 18432
C_G = 19456

P_GMEM = 0
P_CAW = 32
P_PS = 80
P_CCW = 96
P_CCB = 592
P_LNG = 608
P_LNB = 624
NPAR = 640


def _slabs(wcols):
    k = wcols.shape[0] // 128
    return wcols.reshape(k, 128, 128).transpose(1, 0, 2).reshape(128, k * 128)


def _main_entries():
    ent = []
    for j in range(16):
        ent += [("in", C_V + j * 128), ("in", C_C + j * 128), ("in", C_B + j * 128), ("in", C_Z + j * 128)]
    for j in range(16):
        ent += [("in", C_U + j * 128), ("in", C_ZB + j * 128)]
    for oc in range(32):
        ent += [("in", C_G + oc * 128), ("in", C_G + 4096 + oc * 128), ("ab", oc)]
        if oc % 2 == 1:
            j = oc // 2
            ent += [("in", C_GT + j * 128), ("in", C_AC + j * 128)]
    for i in range(8):
        ent += [("in", C_Q + i * 128)]
    for j in range(16):
        ent += [("in", C_ZC + j * 128)]
    for oc in range(32):
        ent += [("in", C_G + 8192 + oc * 128), ("in", C_G + 12288 + oc * 128), ("cx", oc)]
    for oc in range(32):
        ent += [("wo", oc)]
    return ent


def _entry_cols(kind):
    return {"in": 4096, "ab": 20 * 128, "cx": 24 * 128, "wo": 4096}[kind]


MAIN_ENT = _main_entries()
MAIN_OFF = np.concatenate([[0], np.cumsum([_entry_cols(k) for k, _ in MAIN_ENT])]).astype(np.int64)
MAIN_COLS = int(MAIN_OFF[-1])
MEM_COLS = 16 * 4096


def _build_wmain(l, w_in, w_out_a, w_pool, w_out_c, w_out_x, w_o, conv_c_w):
    out = np.empty((128, MAIN_COLS), np.float32)
    pidx = np.arange(128)
    for i, (kind, a) in enumerate(MAIN_ENT):
        o = int(MAIN_OFF[i])
        if kind == "in":
            out[:, o:o + 4096] = _slabs(w_in[l][:, a:a + 128])
        elif kind == "ab":
            g, ol = a // 8, a % 8
            out[:, o:o + 2048] = _slabs(w_out_a[l][:, a * 128:(a + 1) * 128])
            out[:, o + 2048:o + 2560] = _slabs(w_pool[l][g][:, ol * 128:(ol + 1) * 128])
        elif kind == "cx":
            out[:, o:o + 2048] = _slabs(w_out_c[l][:, a * 128:(a + 1) * 128])
            out[:, o + 2048:o + 3072] = _slabs(w_out_x[l][:, a * 128:(a + 1) * 128])
        else:
            out[:, o:o + 4096] = _slabs(w_o[l][:, a * 128:(a + 1) * 128])
    return out


def _build_wmem(l, w_mem_kv):
    out = np.empty((128, MEM_COLS), np.float32)
    w = w_mem_kv[l]
    for oc in range(8):
        out[:, oc * 4096:(oc + 1) * 4096] = _slabs(w[:, oc * 128:(oc + 1) * 128])
    e = 8
    for ch in range(2):
        for q in range(4):
            blk = w[q * 1024:(q + 1) * 1024, 1024 + ch * 512:1024 + (ch + 1) * 512]
            out[:, e * 4096:(e + 1) * 4096] = blk.reshape(8, 128, 512).transpose(1, 0, 2).reshape(128, 4096)
            e += 1
    return out


def _fm(v, n):
    return np.ascontiguousarray(v.reshape(n, 128).T)


def _build_params(l, g_pre, g_post, g_mem, conv_a_w, pool_scale, conv_c_w, conv_c_b, ln_c_g, ln_c_b):
    p = np.zeros((128, NPAR), np.float32)
    p[:, P_GMEM:P_GMEM + 32] = _fm(g_mem[l], 32)
    p[:, P_CAW:P_CAW + 48] = conv_a_w[l].reshape(3, 16, 128).transpose(2, 1, 0).reshape(128, 48)
    p[:, P_PS:P_PS + 16] = _fm(pool_scale[l], 16)
    p[:, P_CCW:P_CCW + 496] = conv_c_w[l].reshape(31, 16, 128).transpose(2, 1, 0).reshape(128, 496)
    p[:, P_CCB:P_CCB + 16] = _fm(conv_c_b[l], 16)
    p[:, P_LNG:P_LNG + 16] = _fm(ln_c_g[l], 16)
    p[:, P_LNB:P_LNB + 16] = _fm(ln_c_b[l], 16)
    return p


ENGS = ("pe", "act", "dve", "pool", "sp")
SAME_SYNC = os.environ.get("MK_SAME_SYNC", "1") == "1"


class Sched:
    def __init__(self):
        self.ops = {e: [] for e in ENGS}
        self.res = {}
        self.waited = {e: {} for e in ENGS}
        self.epoch = 0
        self.dma_cnt = {}
        self.const = set()

    def _need(self, eng, o, ev):
        kind, key, val = ev
        if kind == "c" and key == eng and (eng == "pe" or not SAME_SYNC):
            return
        wk = (kind, key)
        if self.waited[eng].get(wk, -1) >= val:
            return
        self.waited[eng][wk] = val
        o["waits"].append(ev)
        if kind == "c":
            self.ops[key][val]["sig"] = True

    def op(self, eng, fn, reads=(), writes=(), dsem=None):
        o = dict(fn=fn, waits=[], sig=False, dsem=dsem, epoch=self.epoch)
        idx = len(self.ops[eng])
        deps = []
        for r in reads:
            st = self.res.get(r)
            if st and st[0]:
                deps.append(st[0])
        for w in writes:
            st = self.res.get(w)
            if st:
                if st[0]:
                    deps.append(st[0])
                deps += st[1]
        for ev in deps:
            self._need(eng, o, ev)
        self.ops[eng].append(o)
        if dsem:
            c = self.dma_cnt.get(dsem, 0) + 1
            self.dma_cnt[dsem] = c
            ev = ("d", dsem, c * 16)
        else:
            ev = ("c", eng, idx)
        for r in reads:
            if r in self.const:
                continue
            self.res.setdefault(r, [None, []])[1].append(ev)
        for w in writes:
            self.res[w] = [ev, []]
        return ev

    def finalize(self):
        self.nepoch = self.epoch + 1
        for e in ENGS:
            counts = {}
            for o in self.ops[e]:
                if o["sig"]:
                    counts[o["epoch"]] = counts.get(o["epoch"], 0) + 1
                o["cnt"] = counts.get(o["epoch"], 0)

    def emit_engine(self, eng, e, semtab, dsems):
        for o in self.ops[eng]:
            for (kind, key, val) in o["waits"]:
                if kind == "c":
                    t = self.ops[key][val]
                    e.wait_ge(semtab[key][t["epoch"]], t["cnt"])
                else:
                    e.wait_ge(dsems[key], val)
            ins = o["fn"](e)
            if o["sig"]:
                ins.then_inc(semtab[eng][o["epoch"]], 1)
            if o["dsem"]:
                ins.then_inc(dsems[o["dsem"]], 16)


def build_program(depth=DEPTH, ntiles=NTILES):
    nc = bass.Bass("TRN2", target_bir_lowering=False)
    xT = nc.dram_tensor("xT", [D, NTOK], F32, kind="ExternalInput").ap()
    memT = nc.dram_tensor("memT", [D, NMEM], F32, kind="ExternalInput").ap()
    wmain = nc.dram_tensor("wmain", [DEPTH * 128, MAIN_COLS], F32, kind="ExternalInput").ap()
    wmem = nc.dram_tensor("wmem", [DEPTH * 128, MEM_COLS], F32, kind="ExternalInput").ap()
    params = nc.dram_tensor("params", [DEPTH * 128, NPAR], F32, kind="ExternalInput").ap()
    gvin = nc.dram_tensor("gvec", [128, 128], F32, kind="ExternalInput").ap()
    aux = nc.dram_tensor("aux", [128, 72], F32, kind="ExternalInput").ap()
    outT = nc.dram_tensor("outT", [D, NTOK], F32, kind="ExternalOutput").ap()
    ybuf = nc.dram_tensor("ybuf", [D, NT], F32, kind="Internal").ap()
    dcbuf = nc.dram_tensor("dcbuf", [2048, NT], F32, kind="Internal").ap()

    S = Sched()
    es = contextlib.ExitStack()
    with es:
        big = es.enter_context(nc.sbuf_tensor("big", [128, 45056], BF16))
        mg = es.enter_context(nc.sbuf_tensor("mg", [128, 22528], BF16))
        W = [es.enter_context(nc.sbuf_tensor(f"w{i}", [128, 4096], BF16)) for i in range(NBW)]
        T = [es.enter_context(nc.sbuf_tensor(f"t{i}", [128, PADW], F32)) for i in range(NTMP)]
        IO = [es.enter_context(nc.sbuf_tensor(f"io{i}", [128, NT], F32)) for i in range(NIO)]
        rstdp = es.enter_context(nc.sbuf_tensor("rstdp", [128, NT], F32))
        ones32 = es.enter_context(nc.sbuf_tensor("ones32", [128, 128], F32))
        ones16 = es.enter_context(nc.sbuf_tensor("ones16", [128, 128], BF16))
        par = es.enter_context(nc.sbuf_tensor("par", [128, NPAR], F32))
        gvec = es.enter_context(nc.sbuf_tensor("gvecs", [128, 128], F32))
        auxs = es.enter_context(nc.sbuf_tensor("auxs", [128, 72], F32))
        KT = es.enter_context(nc.sbuf_tensor("KT", [128, 8 * 256], BF16))
        VV = es.enter_context(nc.sbuf_tensor("VV", [128, 2 * 1024], BF16))
        acar = es.enter_context(nc.sbuf_tensor("acar", [128, 16 * 4], F32))
        ucar = es.enter_context(nc.sbuf_tensor("ucar", [128, 16 * 16], F32))
        ccar = es.enter_context(nc.sbuf_tensor("ccar", [128, 16 * 32], F32))
        lnA = es.enter_context(nc.sbuf_tensor("lnA", [128, NT], F32))
        lnB = es.enter_context(nc.sbuf_tensor("lnB", [128, NT], F32))
        ps = es.enter_context(nc.psum_tensor("ps", [128, 8, 512], F32))

        S.const.update({("aux",), ("ones",), ("gvec",)})

        def v2(ap):
            return ap.rearrange("p (a b) -> p a b", a=2)

        def hT(kc, st=None):
            if st is None:
                return big[:, kc * NT:(kc + 1) * NT]
            return big[:, kc * NT + st * NS: kc * NT + (st + 1) * NS]

        def s0(j, st=None):
            return hT(32 + j, st)

        def s1(j, st=None):
            return hT(48 + j, st)

        def mgc(kc, st=None):
            if st is None:
                return mg[:, kc * NT:(kc + 1) * NT]
            return mg[:, kc * NT + st * NS: kc * NT + (st + 1) * NS]

        def slot(s):
            return ps[:, 2 * s:2 * s + 2, 0:NS]

        def pcol(c):
            return par[:, c:c + 1]

        def gcol(c):
            return gvec[:, c:c + 1]

        st_ = dict(tmp=0, slot=0, w=0, io=0)

        def tmp():
            i = st_["tmp"]
            st_["tmp"] = (i + 1) % NTMP
            return i

        def io():
            i = st_["io"]
            st_["io"] = (i + 1) % NIO
            return i

        def next_slot():
            s = st_["slot"]
            st_["slot"] = (s + 1) % 3
            return s

        def rk_slot(s):
            return [("ps", 2 * s), ("ps", 2 * s + 1)]

        def load_w(src_ap, ncols):
            s = st_["w"]
            st_["w"] = (s + 1) % NBW
            S.op("pool", lambda e, s=s, src_ap=src_ap, ncols=ncols: e.dma_start(out=W[s][:, 0:ncols], in_=src_ap),
                 reads=[], writes=[("w", s)], dsem=f"w{s}")
            return s

        ent_pos = dict(i=0)

        def next_main(l, kind):
            i = ent_pos["i"]
            k, _ = MAIN_ENT[i]
            assert k == kind, (k, kind, i)
            o = int(MAIN_OFF[i])
            n = _entry_cols(k)
            ent_pos["i"] = i + 1
            return load_w(wmain[l * 128:(l + 1) * 128, o:o + n], n)

        def mm_job(ws, nk, rhs_fn, rhs_keys, slab0=0, sl=None):
            if sl is None:
                sl = next_slot()
            for kc in range(nk):
                for st in range(2):
                    S.op("pe", lambda e, sl=sl, ws=ws, kc=kc, st=st, nk=nk: e.matmul(
                        ps[:, 2 * sl + st, 0:NS], W[ws][:, (slab0 + kc) * 128:(slab0 + kc + 1) * 128],
                        rhs_fn(kc, st), start=(kc == 0), stop=(kc == nk - 1)),
                        reads=[("w", ws), rhs_keys(kc)], writes=[("ps", 2 * sl + st)])
            st_["jobs"] = st_.get("jobs", 0) + 1
            if NOP_CYC > 0 and st_["jobs"] % NOP_EVERY == 0:
                S.op("pe", lambda e: e.nop(cycle_cnt=NOP_CYC))
            return sl

        def proj_job(l):
            ws = next_main(l, "in")
            return mm_job(ws, 32, hT, lambda kc: ("big", kc))

        def act(fn, reads, writes):
            return S.op("act", fn, reads, writes)

        def dve(fn, reads, writes):
            return S.op("dve", fn, reads, writes)

        def ones_reduce(src, key):
            for st in range(2):
                S.op("pe", lambda e, st=st: e.matmul(ps[:, 6 + st, 0:NS], ones32[:], src[:, st * NS:(st + 1) * NS],
                                                     start=True, stop=True), [key, ("ones",)], [("ps", 6 + st)])

        def rstd_from_ps(dst, key, scale):
            sd = tmp()
            act(lambda e, sd=sd: e.activation(out=v2(T[sd][:, 0:NT]), in_=ps[:, 6:8, 0:NS], func=AF.Sqrt,
                                              bias=EPS, scale=scale), [("ps", 6), ("ps", 7)], [("T", sd)])
            dve(lambda e, sd=sd: e.reciprocal(out=dst[:], in_=T[sd][:, 0:NT]), [("T", sd)], [key])

        dve(lambda e: e.memset(ones32[:], 1.0), [], [("ones",)])
        dve(lambda e: e.memset(ones16[:], 1.0), [], [("ones",)])
        S.op("sp", lambda e: e.dma_start(out=auxs[:], in_=aux), [], [("aux",)], dsem="aux")
        S.op("sp", lambda e: e.dma_start(out=gvec[:], in_=gvin), [], [("gvec",)], dsem="gvec")

        def x_src(l):
            return xT if l == 0 else outT

        def load_io(l, ti, kc, src=None, key=None):
            i = io()
            if src is None:
                src = x_src(l)[kc * 128:(kc + 1) * 128, ti * NT:(ti + 1) * NT]
                rk = [("o", ti, kc)] if l > 0 else []
            else:
                rk = [key]
            S.op("sp", lambda e, i=i, src=src: e.dma_start(out=IO[i][:], in_=src), reads=rk, writes=[("IO", i)],
                 dsem=f"lio{i}")
            return i

        def phase1_pieces(l, ti):
            P = []

            def p1(kc):
                def f():
                    i = load_io(l, ti, kc)
                    if kc == 0:
                        act(lambda e, i=i: e.activation(out=lnA[:], in_=IO[i][:], func=AF.Square), [("IO", i)], [("lnA",)])
                    else:
                        q = tmp()
                        act(lambda e, i=i, q=q: e.activation(out=T[q][:, 0:NT], in_=IO[i][:], func=AF.Square),
                            [("IO", i)], [("T", q)])
                        dve(lambda e, q=q: e.tensor_tensor(out=lnA[:], in0=lnA[:], in1=T[q][:, 0:NT], op=ALU.add),
                            [("T", q), ("lnA",)], [("lnA",)])
                return f

            def fin():
                ones_reduce(lnA, ("lnA",))
                rstd_from_ps(lnA, ("lnA",), 1.0 / D)

            def p2(kc):
                def f():
                    i = load_io(l, ti, kc)
                    dve(lambda e, i=i, kc=kc: e.scalar_tensor_tensor(
                        out=hT(kc), in0=IO[i][:], scalar=gcol(l * 64 + kc), in1=lnA[:],
                        op0=ALU.mult, op1=ALU.mult), [("IO", i), ("lnA",), ("gvec",)], [("big", kc)])
                return f

            for kc in range(32):
                P.append(p1(kc))
            P.append(fin)
            for kc in range(32):
                P.append(p2(kc))
            return P

        def final_pieces(l, ti):
            P = []
            bufs = {}

            def loads(oc):
                a = load_io(l, ti, oc, src=ybuf[oc * 128:(oc + 1) * 128, :], key=("yb", oc))
                b = load_io(l, ti, oc)
                bufs[oc] = (a, b)

            def fp(oc):
                def f():
                    if oc == 0:
                        loads(0)
                    if oc + 1 < 32:
                        loads(oc + 1)
                    a, b = bufs[oc]
                    dve(lambda e, a=a, oc=oc: e.scalar_tensor_tensor(
                        out=IO[a][:], in0=IO[a][:], scalar=gcol(l * 64 + 32 + oc), in1=rstdp[:], op0=ALU.mult, op1=ALU.mult),
                        [("IO", a), ("rstdp",), ("gvec",)], [("IO", a)])
                    dve(lambda e, a=a, b=b: e.tensor_tensor(out=IO[b][:], in0=IO[b][:], in1=IO[a][:], op=ALU.add),
                        [("IO", a), ("IO", b)], [("IO", b)])
                    S.op("sp", lambda e, b=b, oc=oc: e.dma_start(
                        out=outT[oc * 128:(oc + 1) * 128, ti * NT:(ti + 1) * NT], in_=IO[b][:]),
                        reads=[("IO", b)], writes=[("o", ti, oc)], dsem=f"sio{b}")
                return f

            for oc in range(32):
                P.append(fp(oc))
            return P

        def drain(P, n):
            for _ in range(n):
                if P:
                    P.pop(0)()

        def layer_setup(l):
            S.op("sp", lambda e, l=l: e.dma_start(out=par[:], in_=params[l * 128:(l + 1) * 128, :]),
                 [], [("par",)], dsem="par")
            dve(lambda e: e.memset(acar[:], 0.0), [], [("acar",)])
            dve(lambda e: e.memset(ucar[:], 0.0), [], [("ucar",)])
            dve(lambda e: e.memset(ccar[:], 0.0), [], [("ccar",)])
            for kc in range(32):
                i = tmp()
                S.op("sp", lambda e, i=i, kc=kc: e.dma_start(out=T[i][:, 0:NMEM], in_=memT[kc * 128:(kc + 1) * 128, :]),
                     [], [("T", i)], dsem=f"ld{i}")
                q = tmp()
                act(lambda e, i=i, q=q: e.activation(out=T[q][:, 0:NMEM], in_=T[i][:, 0:NMEM], func=AF.Square),
                    [("T", i)], [("T", q)])
                S.op("pe", lambda e, q=q, kc=kc: e.matmul(ps[:, 6, 0:NMEM], ones32[:], T[q][:, 0:NMEM],
                                                          start=(kc == 0), stop=(kc == 31)),
                     [("T", q), ("ones",)], [("ps", 6)])
            sd = tmp()
            act(lambda e, sd=sd: e.activation(out=T[sd][:, 0:NMEM], in_=ps[:, 6, 0:NMEM], func=AF.Sqrt,
                                              bias=EPS, scale=1.0 / D), [("ps", 6)], [("T", sd)])
            rm = lnB
            dve(lambda e, sd=sd: e.reciprocal(out=rm[:, 0:NMEM], in_=T[sd][:, 0:NMEM]), [("T", sd)], [("lnB",)])

            def mh(kc):
                return big[:, 48 * NT + kc * NMEM: 48 * NT + (kc + 1) * NMEM]
            MHK = [("big", 48 + u) for u in range(16)]
            for kc in range(32):
                i = tmp()
                S.op("sp", lambda e, i=i, kc=kc: e.dma_start(out=T[i][:, 0:NMEM], in_=memT[kc * 128:(kc + 1) * 128, :]),
                     [], [("T", i)], dsem=f"ld{i}")
                dve(lambda e, i=i, kc=kc: e.scalar_tensor_tensor(
                    out=mh(kc), in0=T[i][:, 0:NMEM], scalar=pcol(P_GMEM + kc), in1=rm[:, 0:NMEM],
                    op0=ALU.mult, op1=ALU.mult), [("T", i), ("lnB",), ("par",)],
                    [("big", 48 + (kc * NMEM) // NT), ("big", 48 + (kc * NMEM + NMEM - 1) // NT)])
            for oc in range(8):
                ws = load_w(wmem[l * 128:(l + 1) * 128, oc * 4096:(oc + 1) * 4096], 4096)
                sl = next_slot()
                for kc in range(32):
                    S.op("pe", lambda e, sl=sl, ws=ws, kc=kc: e.matmul(
                        ps[:, 2 * sl, 0:NMEM], W[ws][:, kc * 128:(kc + 1) * 128], mh(kc),
                        start=(kc == 0), stop=(kc == 31)),
                        [("w", ws)] + MHK, [("ps", 2 * sl)])
                act(lambda e, sl=sl, oc=oc: e.activation(out=KT[:, oc * 256:(oc + 1) * 256], in_=ps[:, 2 * sl, 0:NMEM],
                                                         func=AF.Copy), [("ps", 2 * sl)], [("KT",)])
            e_i = 8
            for ch in range(2):
                for q4 in range(4):
                    ws = load_w(wmem[l * 128:(l + 1) * 128, e_i * 4096:(e_i + 1) * 4096], 4096)
                    e_i += 1
                    for i8 in range(8):
                        kc = q4 * 8 + i8
                        for mc in range(2):
                            S.op("pe", lambda e, ws=ws, kc=kc, mc=mc, i8=i8: e.matmul(
                                ps[:, 6 + mc, 0:512], mh(kc)[:, mc * 128:(mc + 1) * 128],
                                W[ws][:, i8 * 512:(i8 + 1) * 512], start=(kc == 0), stop=(kc == 31)),
                                [("w", ws)] + MHK, [("ps", 6 + mc)])
                for mc in range(2):
                    act(lambda e, mc=mc, ch=ch: e.activation(
                        out=VV[:, mc * 1024 + ch * 512: mc * 1024 + (ch + 1) * 512], in_=ps[:, 6 + mc, 0:512],
                        func=AF.Copy), [("ps", 6 + mc)], [("VV",)])

        def front_a(l, ti, pend):
            for j in range(16):
                sl = proj_job(l)
                vS = tmp()
                act(lambda e, sl=sl, vS=vS: e.activation(out=v2(T[vS][:, 0:NT]), in_=slot(sl), func=AF.Copy),
                    rk_slot(sl), [("T", vS)])
                sl = proj_job(l)
                cv = tmp()
                act(lambda e, cv=cv, j=j: e.activation(out=T[cv][:, 0:4], in_=acar[:, j * 4:(j + 1) * 4], func=AF.Copy),
                    [("acar",)], [("T", cv)])
                dve(lambda e, sl=sl, vS=vS, cv=cv: e.tensor_tensor(
                    out=v2(T[cv][:, 4:4 + NT]), in0=slot(sl), in1=v2(T[vS][:, 0:NT]), op=ALU.mult),
                    rk_slot(sl) + [("T", vS), ("T", cv)], [("T", cv)])
                act(lambda e, cv=cv, j=j: e.activation(out=acar[:, j * 4:(j + 1) * 4], in_=T[cv][:, NT:NT + 4], func=AF.Copy),
                    [("T", cv)], [("acar",)])
                acc = tmp()
                dve(lambda e, cv=cv, acc=acc, j=j: e.tensor_scalar(
                    out=T[acc][:, 0:NT], in0=T[cv][:, 2:2 + NT], scalar1=pcol(P_CAW + j * 3 + 0), scalar2=None,
                    op0=ALU.mult), [("T", cv), ("par",)], [("T", acc)])
                for k in (1, 2):
                    dve(lambda e, cv=cv, acc=acc, j=j, k=k: e.scalar_tensor_tensor(
                        out=T[acc][:, 0:NT], in0=T[cv][:, 2 + k:2 + k + NT], scalar=pcol(P_CAW + j * 3 + k),
                        in1=T[acc][:, 0:NT], op0=ALU.mult, op1=ALU.add), [("T", cv), ("T", acc), ("par",)], [("T", acc)])
                sl = proj_job(l)
                dve(lambda e, sl=sl, acc=acc: e.tensor_tensor(
                    out=v2(T[acc][:, 0:NT]), in0=slot(sl), in1=v2(T[acc][:, 0:NT]), op=ALU.mult),
                    rk_slot(sl) + [("T", acc)], [("T", acc)])
                sl = proj_job(l)
                sz = tmp()
                act(lambda e, sl=sl, sz=sz: e.activation(out=v2(T[sz][:, 0:NT]), in_=slot(sl), func=AF.Silu),
                    rk_slot(sl), [("T", sz)])
                dve(lambda e, acc=acc, sz=sz, j=j: e.tensor_tensor(
                    out=s0(j), in0=T[acc][:, 0:NT], in1=T[sz][:, 0:NT], op=ALU.mult),
                    [("T", acc), ("T", sz)], [("big", 32 + j)])
                drain(pend, 2)

        def front_b(l, ti):
            for j in range(16):
                g = j // 4
                wwin = 2 ** (g + 1)
                sl = proj_job(l)
                U = tmp()
                act(lambda e, U=U, j=j: e.activation(out=T[U][:, 0:16], in_=ucar[:, j * 16:(j + 1) * 16], func=AF.Copy),
                    [("ucar",)], [("T", U)])
                act(lambda e, sl=sl, U=U: e.activation(out=v2(T[U][:, 16:16 + NT]), in_=slot(sl), func=AF.Copy),
                    rk_slot(sl) + [("T", U)], [("T", U)])
                act(lambda e, U=U, j=j: e.activation(out=ucar[:, j * 16:(j + 1) * 16], in_=T[U][:, NT:NT + 16], func=AF.Copy),
                    [("T", U)], [("ucar",)])
                pp = [tmp(), tmp()]
                Sb = U
                sh = 1
                for step in range(g + 1):
                    lo = 2 * sh - 1
                    Tn = pp[step % 2]
                    dve(lambda e, Sb=Sb, Tn=Tn, lo=lo, sh=sh: e.tensor_tensor(
                        out=T[Tn][:, lo:16 + NT], in0=T[Sb][:, lo:16 + NT], in1=T[Sb][:, lo - sh:16 + NT - sh],
                        op=ALU.add), [("T", Sb)], [("T", Tn)])
                    Sb = Tn
                    sh *= 2
                P = tmp()
                dve(lambda e, Sb=Sb, U=U, P=P, wwin=wwin: e.scalar_tensor_tensor(
                    out=T[P][:, 0:NT], in0=T[Sb][:, 16:16 + NT], scalar=1.0 / wwin, in1=T[U][:, 16:16 + NT],
                    op0=ALU.mult, op1=ALU.subtract), [("T", Sb), ("T", U)], [("T", P)])
                if ti == 0:
                    fx = pp[(g + 1) % 2]
                    dve(lambda e, Sb=Sb, fx=fx, g=g: e.tensor_tensor(
                        out=T[fx][:, 0:16], in0=T[Sb][:, 16 + HALO:16 + HALO + 16], in1=auxs[:, g * 16:(g + 1) * 16],
                        op=ALU.mult), [("T", Sb), ("aux",)], [("T", fx)])
                    dve(lambda e, U=U, fx=fx, P=P: e.tensor_tensor(
                        out=T[P][:, HALO:HALO + 16], in0=T[fx][:, 0:16], in1=T[U][:, 16 + HALO:16 + HALO + 16],
                        op=ALU.subtract), [("T", fx), ("T", U), ("T", P)], [("T", P)])
                sl = proj_job(l)
                sz = tmp()
                act(lambda e, sl=sl, sz=sz: e.activation(out=v2(T[sz][:, 0:NT]), in_=slot(sl), func=AF.Silu),
                    rk_slot(sl), [("T", sz)])
                dve(lambda e, P=P, sz=sz, j=j: e.scalar_tensor_tensor(
                    out=s1(j), in0=T[P][:, 0:NT], scalar=pcol(P_PS + j), in1=T[sz][:, 0:NT],
                    op0=ALU.mult, op1=ALU.mult), [("T", P), ("T", sz), ("par",)], [("big", 48 + j)])

        def back(l, first, nk_y, rhs_y, keys_y, kind, hook=None, mid=None):
            for oc in range(32):
                sl = proj_job(l)
                sg1 = tmp()
                act(lambda e, sl=sl, sg1=sg1: e.activation(out=v2(T[sg1][:, 0:NT]), in_=slot(sl), func=AF.Sigmoid),
                    rk_slot(sl), [("T", sg1)])
                if mid is not None:
                    mid()
                sl = proj_job(l)
                sg2 = tmp()
                act(lambda e, sl=sl, sg2=sg2: e.activation(out=v2(T[sg2][:, 0:NT]), in_=slot(sl), func=AF.Sigmoid),
                    rk_slot(sl), [("T", sg2)])
                if mid is not None:
                    mid()
                ws = next_main(l, kind)
                sx = mm_job(ws, 16, s0, lambda kc: ("big", 32 + kc), slab0=0)
                sy = mm_job(ws, nk_y, lambda kc, st, oc=oc: rhs_y(oc, kc, st), lambda kc, oc=oc: keys_y(oc, kc), slab0=16)
                dve(lambda e, sx=sx, sg1=sg1: e.tensor_tensor(
                    out=v2(T[sg1][:, 0:NT]), in0=slot(sx), in1=v2(T[sg1][:, 0:NT]), op=ALU.mult),
                    rk_slot(sx) + [("T", sg1)], [("T", sg1)])
                dve(lambda e, sy=sy, sg2=sg2: e.tensor_tensor(
                    out=v2(T[sg2][:, 0:NT]), in0=slot(sy), in1=v2(T[sg2][:, 0:NT]), op=ALU.mult),
                    rk_slot(sy) + [("T", sg2)], [("T", sg2)])
                if first:
                    dve(lambda e, sg1=sg1, sg2=sg2, oc=oc: e.tensor_tensor(
                        out=mgc(oc), in0=T[sg1][:, 0:NT], in1=T[sg2][:, 0:NT], op=ALU.add),
                        [("T", sg1), ("T", sg2)], [("mg", oc)])
                else:
                    dve(lambda e, sg1=sg1, sg2=sg2: e.tensor_tensor(
                        out=T[sg1][:, 0:NT], in0=T[sg1][:, 0:NT], in1=T[sg2][:, 0:NT], op=ALU.add),
                        [("T", sg1), ("T", sg2)], [("T", sg1)])
                    dve(lambda e, sg1=sg1, oc=oc: e.tensor_tensor(
                        out=mgc(oc), in0=mgc(oc), in1=T[sg1][:, 0:NT], op=ALU.add),
                        [("T", sg1), ("mg", oc)], [("mg", oc)])
                if hook is not None:
                    hook(oc)

        def conv_chunk(l, j):
            tmp()
            sl = proj_job(l)
            sg = tmp()
            act(lambda e, sl=sl, sg=sg: e.activation(out=v2(T[sg][:, 0:NT]), in_=slot(sl), func=AF.Sigmoid),
                rk_slot(sl), [("T", sg)])
            sl = proj_job(l)
            gc = tmp()
            act(lambda e, gc=gc, j=j: e.activation(out=T[gc][:, 0:32], in_=ccar[:, j * 32:(j + 1) * 32], func=AF.Copy),
                [("ccar",)], [("T", gc)])
            dve(lambda e, sl=sl, sg=sg, gc=gc: e.tensor_tensor(
                out=v2(T[gc][:, 32:32 + NT]), in0=slot(sl), in1=v2(T[sg][:, 0:NT]), op=ALU.mult),
                rk_slot(sl) + [("T", sg), ("T", gc)], [("T", gc)])
            act(lambda e, gc=gc, j=j: e.activation(out=ccar[:, j * 32:(j + 1) * 32], in_=T[gc][:, NT:NT + 32], func=AF.Copy),
                [("T", gc)], [("ccar",)])
            acc = tmp()

            def taps(k0, k1):
                for k in range(k0, k1):
                    if k == 0:
                        dve(lambda e, gc=gc, acc=acc, j=j: e.tensor_scalar(
                            out=T[acc][:, 0:NT], in0=T[gc][:, 2:2 + NT], scalar1=pcol(P_CCW + j * 31), scalar2=pcol(P_CCB + j),
                            op0=ALU.mult, op1=ALU.add), [("T", gc), ("par",)], [("T", acc)])
                    else:
                        dve(lambda e, gc=gc, acc=acc, j=j, k=k: e.scalar_tensor_tensor(
                            out=T[acc][:, 0:NT], in0=T[gc][:, 2 + k:2 + k + NT], scalar=pcol(P_CCW + j * 31 + k),
                            in1=T[acc][:, 0:NT], op0=ALU.mult, op1=ALU.add), [("T", gc), ("T", acc), ("par",)], [("T", acc)])

            def tail():
                S.op("sp", lambda e, acc=acc, j=j: e.dma_start(out=dcbuf[j * 128:(j + 1) * 128, :], in_=T[acc][:, 0:NT]),
                     reads=[("T", acc)], writes=[("dc", j)], dsem=f"st{acc}")
                if j == 0:
                    act(lambda e, acc=acc: e.activation(out=lnB[:], in_=T[acc][:, 0:NT], func=AF.Square),
                        [("T", acc)], [("lnB",)])
                    dve(lambda e, acc=acc: e.tensor_copy(out=lnA[:], in_=T[acc][:, 0:NT]), [("T", acc)], [("lnA",)])
                else:
                    sq = tmp()
                    act(lambda e, acc=acc, sq=sq: e.activation(out=T[sq][:, 0:NT], in_=T[acc][:, 0:NT], func=AF.Square),
                        [("T", acc)], [("T", sq)])
                    dve(lambda e, acc=acc: e.tensor_tensor(out=lnA[:], in0=lnA[:], in1=T[acc][:, 0:NT], op=ALU.add),
                        [("T", acc), ("lnA",)], [("lnA",)])
                    dve(lambda e, sq=sq: e.tensor_tensor(out=lnB[:], in0=lnB[:], in1=T[sq][:, 0:NT], op=ALU.add),
                        [("T", sq), ("lnB",)], [("lnB",)])

            taps(0, 6)
            return [lambda: taps(6, 12), lambda: taps(12, 18), lambda: taps(18, 24),
                    lambda: taps(24, 31), tail]

        def front_c_x(l, ti):
            for i8 in range(8):
                sl = proj_job(l)
                act(lambda e, sl=sl, i8=i8: e.activation(out=v2(s1(8 + i8)), in_=slot(sl), func=AF.Copy),
                    rk_slot(sl), [("big", 48 + 8 + i8)])

            ones_reduce(lnA, ("lnA",))
            mu = tmp()
            dve(lambda e, mu=mu: e.tensor_scalar(out=v2(T[mu][:, 0:NT]), in0=ps[:, 6:8, 0:NS], scalar1=1.0 / 2048,
                                                 scalar2=None, op0=ALU.mult), [("ps", 6), ("ps", 7)], [("T", mu)])
            ones_reduce(lnB, ("lnB",))
            msq = tmp()
            dve(lambda e, mu=mu, msq=msq: e.tensor_tensor(out=T[msq][:, 0:NT], in0=T[mu][:, 0:NT], in1=T[mu][:, 0:NT],
                                                          op=ALU.mult), [("T", mu)], [("T", msq)])
            var = tmp()
            dve(lambda e, msq=msq, var=var: e.scalar_tensor_tensor(
                out=v2(T[var][:, 0:NT]), in0=ps[:, 6:8, 0:NS], scalar=1.0 / 2048, in1=v2(T[msq][:, 0:NT]),
                op0=ALU.mult, op1=ALU.subtract), [("ps", 6), ("ps", 7), ("T", msq)], [("T", var)])
            act(lambda e, var=var: e.activation(out=T[var][:, 0:NT], in_=T[var][:, 0:NT], func=AF.Sqrt, bias=EPS, scale=1.0),
                [("T", var)], [("T", var)])
            dve(lambda e, var=var: e.reciprocal(out=lnA[:], in_=T[var][:, 0:NT]), [("T", var)], [("lnA",)])
            dve(lambda e, mu=mu: e.scalar_tensor_tensor(out=lnB[:], in0=T[mu][:, 0:NT], scalar=-1.0, in1=lnA[:],
                                                        op0=ALU.mult, op1=ALU.mult), [("T", mu), ("lnA",)], [("lnB",)])

            def z_job(j):
                sl = proj_job(l)
                sz = tmp()
                act(lambda e, sl=sl, sz=sz: e.activation(out=v2(T[sz][:, 0:NT]), in_=slot(sl), func=AF.Silu),
                    rk_slot(sl), [("T", sz)])
                di = load_io(l, ti, j, src=dcbuf[j * 128:(j + 1) * 128, :], key=("dc", j))
                tn = tmp()
                dve(lambda e, tn=tn, di=di: e.tensor_tensor(out=T[tn][:, 0:NT], in0=IO[di][:], in1=lnA[:], op=ALU.mult),
                    [("IO", di), ("lnA",)], [("T", tn)])
                dve(lambda e, tn=tn: e.tensor_tensor(out=T[tn][:, 0:NT], in0=T[tn][:, 0:NT], in1=lnB[:], op=ALU.add),
                    [("T", tn), ("lnB",)], [("T", tn)])
                act(lambda e, tn=tn, j=j: e.activation(out=T[tn][:, 0:NT], in_=T[tn][:, 0:NT], func=AF.Silu,
                                                       bias=pcol(P_LNB + j), scale=pcol(P_LNG + j)),
                    [("T", tn), ("par",)], [("T", tn)])
                dve(lambda e, tn=tn, sz=sz, j=j: e.tensor_tensor(out=s0(j), in0=T[tn][:, 0:NT], in1=T[sz][:, 0:NT],
                                                                 op=ALU.mult), [("T", tn), ("T", sz)], [("big", 32 + j)])

            Ebuf = {}

            def scores(h):
                sls = [next_slot(), next_slot()]
                for mc in range(2):
                    for st in range(2):
                        for dc in range(2):
                            S.op("pe", lambda e, h=h, mc=mc, st=st, dc=dc, sl=sls[mc]: e.matmul(
                                ps[:, 2 * sl + st, 0:NS],
                                KT[:, (h * 2 + dc) * 256 + mc * 128:(h * 2 + dc) * 256 + (mc + 1) * 128],
                                s1(8 + h * 2 + dc, st), start=(dc == 0), stop=(dc == 1)),
                                [("KT",), ("big", 48 + 8 + h * 2 + dc)], [("ps", 2 * sls[mc] + st)])
                E = tmp()
                Eb = T[E].bitcast(BF16)
                for mc in range(2):
                    act(lambda e, mc=mc, sl=sls[mc], Eb=Eb: e.activation(
                        out=v2(Eb[:, mc * NT:(mc + 1) * NT]), in_=slot(sl), func=AF.Exp, scale=1.0 / 16.0),
                        rk_slot(sls[mc]) + [("T", E)], [("T", E)])
                Ebuf[h] = (E, Eb)

            def attv(h):
                E, Eb = Ebuf[h]
                for st in range(2):
                    for mc in range(2):
                        S.op("pe", lambda e, st=st, mc=mc, Eb=Eb: e.matmul(
                            ps[:, 6 + st, 0:NS], ones16[:], Eb[:, mc * NT + st * NS: mc * NT + (st + 1) * NS],
                            start=(mc == 0), stop=(mc == 1)), [("T", E), ("ones",)], [("ps", 6 + st)])
                rd = tmp()
                dve(lambda e, rd=rd: e.reciprocal(out=v2(T[rd][:, 0:NT]), in_=ps[:, 6:8, 0:NS]),
                    [("ps", 6), ("ps", 7)], [("T", rd)])
                for dcp in range(2):
                    sl = next_slot()
                    for st in range(2):
                        for mc in range(2):
                            S.op("pe", lambda e, sl=sl, st=st, mc=mc, h=h, dcp=dcp, Eb=Eb: e.matmul(
                                ps[:, 2 * sl + st, 0:NS],
                                VV[:, mc * 1024 + (h * 2 + dcp) * 128: mc * 1024 + (h * 2 + dcp + 1) * 128],
                                Eb[:, mc * NT + st * NS: mc * NT + (st + 1) * NS], start=(mc == 0), stop=(mc == 1)),
                                [("T", E), ("VV",)], [("ps", 2 * sl + st)])
                    dve(lambda e, sl=sl, rd=rd, h=h, dcp=dcp: e.tensor_tensor(
                        out=v2(s1(h * 2 + dcp)), in0=slot(sl), in1=v2(T[rd][:, 0:NT]), op=ALU.mult),
                        rk_slot(sl) + [("T", rd)], [("big", 48 + h * 2 + dcp)])

            steps = []
            for h in range(4):
                steps += [lambda h=h: scores(h), lambda h=h: attv(h)]
            for j in range(16):
                z_job(j)
                if steps and j % 2 == 1:
                    steps.pop(0)()

        def wo_phase(l, ti, nxt):
            for oc in range(32):
                ws = next_main(l, "wo")
                sl = mm_job(ws, 32, mgc, lambda kc: ("mg", kc))
                t1 = tmp()
                act(lambda e, sl=sl, t1=t1: e.activation(out=v2(T[t1][:, 0:NT]), in_=slot(sl), func=AF.Copy),
                    rk_slot(sl), [("T", t1)])
                S.op("sp", lambda e, t1=t1, oc=oc: e.dma_start(out=ybuf[oc * 128:(oc + 1) * 128, :], in_=T[t1][:, 0:NT]),
                     reads=[("T", t1)], writes=[("yb", oc)], dsem=f"st{t1}")
                if oc == 0:
                    act(lambda e, sl=sl: e.activation(out=v2(lnB[:]), in_=slot(sl), func=AF.Square),
                        rk_slot(sl), [("lnB",)])
                else:
                    t2 = tmp()
                    act(lambda e, sl=sl, t2=t2: e.activation(out=v2(T[t2][:, 0:NT]), in_=slot(sl), func=AF.Square),
                        rk_slot(sl), [("T", t2)])
                    dve(lambda e, t2=t2: e.tensor_tensor(out=lnB[:], in0=lnB[:], in1=T[t2][:, 0:NT], op=ALU.add),
                        [("T", t2), ("lnB",)], [("lnB",)])
                drain(nxt, 4 if oc == 0 else 3)
            drain(nxt, len(nxt))
            assert ent_pos["i"] == len(MAIN_ENT)
            ones_reduce(lnB, ("lnB",))
            rstd_from_ps(rstdp, ("rstdp",), 1.0 / D)
            if ti == 0:
                dve(lambda e: e.tensor_scalar(out=rstdp[:, 0:HALO], in0=rstdp[:, 0:HALO], scalar1=auxs[:, 64:65],
                                              scalar2=None, op0=ALU.mult), [("rstdp",), ("aux",)], [("rstdp",)])

        seq = [(l, ti) for l in range(depth) for ti in range(ntiles)]
        pend_final = []
        drain_all = lambda P: drain(P, len(P))
        drain_all(phase1_pieces(*seq[0]))
        for n, (l, ti) in enumerate(seq):
            if ti == 0:
                layer_setup(l)
            ent_pos["i"] = 0
            front_a(l, ti, pend_final)
            drain_all(pend_final)
            front_b(l, ti)
            pend_conv = []

            def conv_hook(oc, l=l, pend_conv=pend_conv):
                if oc % 2 == 1:
                    drain(pend_conv, len(pend_conv))
                    pend_conv.extend(conv_chunk(l, oc // 2))
                else:
                    drain(pend_conv, 1)

            back(l, True, 4, lambda oc, kc, st: s1(4 * (oc // 8) + kc, st), lambda oc, kc: ("big", 48 + 4 * (oc // 8) + kc), "ab",
                 hook=conv_hook, mid=lambda pend_conv=pend_conv: drain(pend_conv, 1))
            drain(pend_conv, len(pend_conv))
            front_c_x(l, ti)
            back(l, False, 8, lambda oc, kc, st: s1(kc, st), lambda oc, kc: ("big", 48 + kc), "cx")
            nxt = phase1_pieces(*seq[n + 1]) if n + 1 < len(seq) else []
            wo_phase(l, ti, nxt)
            pend_final = final_pieces(l, ti)
            S.epoch += 1
        drain_all(pend_final)

        S.finalize()
        dsem_names = sorted(S.dma_cnt.keys())
        dsems = {n: es.enter_context(nc.semaphore("d_" + n)) for n in dsem_names}
        semtab = {e: [es.enter_context(nc.semaphore(f"c_{e}_{k}")) for k in range(S.nepoch)] for e in ENGS}
        block = es.enter_context(nc.Block())

        @block.tensor
        def _(e):
            S.emit_engine("pe", e, semtab, dsems)

        @block.scalar
        def _(e):
            S.emit_engine("act", e, semtab, dsems)

        @block.vector
        def _(e):
            S.emit_engine("dve", e, semtab, dsems)

        @block.gpsimd
        def _(e):
            S.emit_engine("pool", e, semtab, dsems)

        @block.sync
        def _(e):
            S.emit_engine("sp", e, semtab, dsems)
            for n in dsem_names:
                if n.startswith("st") or n.startswith("sio"):
                    e.wait_ge(dsems[n], S.dma_cnt[n] * 16)
    stats = {e: len(S.ops[e]) for e in ENGS}
    return nc, stats


def _prep_inputs(x, mem, g_pre, g_post, g_mem, w_in, conv_a_w, w_out_a, pool_scale, w_pool, conv_c_w, conv_c_b,
                 ln_c_g, ln_c_b, w_out_c, w_mem_kv, w_out_x, w_o, cores=None):
    f = lambda a: np.asarray(a, dtype=np.float32)
    x, mem = f(x), f(mem)
    w_in, w_out_a, w_pool, w_out_c, w_out_x, w_o, w_mem_kv = map(f, (w_in, w_out_a, w_pool, w_out_c, w_out_x, w_o, w_mem_kv))
    conv_c_w = f(conv_c_w)
    wmain = np.concatenate([_build_wmain(l, w_in, w_out_a, w_pool, w_out_c, w_out_x, w_o, conv_c_w) for l in range(DEPTH)], axis=0)
    wmem = np.concatenate([_build_wmem(l, w_mem_kv) for l in range(DEPTH)], axis=0)
    params = np.concatenate([_build_params(l, f(g_pre), f(g_post), f(g_mem), f(conv_a_w), f(pool_scale), conv_c_w,
                                           f(conv_c_b), f(ln_c_g), f(ln_c_b)) for l in range(DEPTH)], axis=0)
    gv = np.zeros((128, 128), np.float32)
    for l in range(DEPTH):
        gv[:, l * 64:l * 64 + 32] = _fm(f(g_pre)[l], 32)
        gv[:, l * 64 + 32:l * 64 + 64] = _fm(f(g_post)[l], 32)
    in_maps = []
    for c in (range(NCORES) if cores is None else cores):
        b, sc = c // 4, c % 4
        t0 = sc * TOK
        xt = np.zeros((D, NTOK), np.float32)
        if t0 == 0:
            xt[:, HALO:] = x[b, 0:TOK, :].T
        else:
            xt[:, :] = x[b, t0 - HALO:t0 + TOK, :].T
        auxa = np.zeros((128, 72), np.float32)
        for g in range(4):
            w = 2 ** (g + 1)
            for t in range(16):
                cnt = min(t + 1, w) if t0 == 0 else w
                auxa[:, g * 16 + t] = 1.0 / cnt
        auxa[:, 64:72] = 0.0 if t0 == 0 else 1.0
        in_maps.append({"xT": xt, "memT": np.ascontiguousarray(mem[b].T), "wmain": wmain, "wmem": wmem,
                        "params": params, "gvec": gv, "aux": auxa})
    return in_maps


_PROG = {}


def kernel(**inputs):
    in_maps = _prep_inputs(**inputs)
    if "nc" not in _PROG:
        _PROG["nc"], _ = build_program()
    nc = _PROG["nc"]
    res = run_bass_kernel_spmd(nc, in_maps, core_ids=list(range(NCORES)))
    B = inputs["x"].shape[0]
    out = np.empty((B, SEQ, D), np.float32)
    for c in range(NCORES):
        b, sc = c // 4, c % 4
        o = np.asarray(res.results[c]["outT"])
        out[b, sc * TOK:(sc + 1) * TOK, :] = o[:, HALO:].T
    return out
```

```python
import os
import contextlib
import numpy as np
import concourse.bass as bass
import concourse.mybir as mybir
from concourse.bass_utils import run_bass_kernel_spmd

F32 = mybir.dt.float32
BF16 = mybir.dt.bfloat16
ALU = mybir.AluOpType
AF = mybir.ActivationFunctionType

D = 4096
NCORES = 8
SEQ = 8192
TOK = 2048
HALO = 64
NTOK = TOK + HALO
NT = 704
NS = 352
NTILES = 3
DEPTH = 2
NMEM = 256
EPS = 1e-6
PADW = 736
NBW = 3
NTMP = 6
NOP_CYC = int(os.environ.get("MK_NOP_CYC", "0"))
NOP_EVERY = int(os.environ.get("MK_NOP_EVERY", "1"))
NIO = 4

C_V, C_B, C_C, C_Z = 0, 2048, 4096, 6144
C_U, C_ZB = 8192, 10240
C_AC, C_GT, C_ZC = 12288, 14336, 16384
C_Q = 18432
C_G = 19456

P_GMEM = 0
P_CAW = 32
P_PS = 80
P_CCW = 96
P_CCB = 592
P_LNG = 608
P_LNB = 624
NPAR = 640


def _slabs(wcols):
    k = wcols.shape[0] // 128
    return wcols.reshape(k, 128, 128).transpose(1, 0, 2).reshape(128, k * 128)


def _main_entries():
    ent = []
    for j in range(16):
        ent += [("in", C_V + j * 128), ("in", C_C + j * 128), ("in", C_B + j * 128), ("in", C_Z + j * 128)]
    for j in range(16):
        ent += [("in", C_U + j * 128), ("in", C_ZB + j * 128)]
    for oc in range(32):
        ent += [("in", C_G + oc * 128), ("in", C_G + 4096 + oc * 128), ("ab", oc)]
        if oc % 2 == 1:
            j = oc // 2
            ent += [("in", C_GT + j * 128), ("in", C_AC + j * 128)]
    for i in range(8):
        ent += [("in", C_Q + i * 128)]
    for j in range(16):
        ent += [("in", C_ZC + j * 128)]
    for oc in range(32):
        ent += [("in", C_G + 8192 + oc * 128), ("in", C_G + 12288 + oc * 128), ("cx", oc)]
    for oc in range(32):
        ent += [("wo", oc)]
    return ent


def _entry_cols(kind):
    return {"in": 4096, "ab": 20 * 128, "cx": 24 * 128, "wo": 4096}[kind]


MAIN_ENT = _main_entries()
MAIN_OFF = np.concatenate([[0], np.cumsum([_entry_cols(k) for k, _ in MAIN_ENT])]).astype(np.int64)
MAIN_COLS = int(MAIN_OFF[-1])
MEM_COLS = 16 * 4096


def _build_wmain(l, w_in, w_out_a, w_pool, w_out_c, w_out_x, w_o, conv_c_w):
    out = np.empty((128, MAIN_COLS), np.float32)
    pidx = np.arange(128)
    for i, (kind, a) in enumerate(MAIN_ENT):
        o = int(MAIN_OFF[i])
        if kind == "in":
            out[:, o:o + 4096] = _slabs(w_in[l][:, a:a + 128])
        elif kind == "ab":
            g, ol = a // 8, a % 8
            out[:, o:o + 2048] = _slabs(w_out_a[l][:, a * 128:(a + 1) * 128])
            out[:, o + 2048:o + 2560] = _slabs(w_pool[l][g][:, ol * 128:(ol + 1) * 128])
        elif kind == "cx":
            out[:, o:o + 2048] = _slabs(w_out_c[l][:, a * 128:(a + 1) * 128])
            out[:, o + 2048:o + 3072] = _slabs(w_out_x[l][:, a * 128:(a + 1) * 128])
        else:
            out[:, o:o + 4096] = _slabs(w_o[l][:, a * 128:(a + 1) * 128])
    return out


def _build_wmem(l, w_mem_kv):
    out = np.empty((128, MEM_COLS), np.float32)
    w = w_mem_kv[l]
    for oc in range(8):
        out[:, oc * 4096:(oc + 1) * 4096] = _slabs(w[:, oc * 128:(oc + 1) * 128])
    e = 8
    for ch in range(2):
        for q in range(4):
            blk = w[q * 1024:(q + 1) * 1024, 1024 + ch * 512:1024 + (ch + 1) * 512]
            out[:, e * 4096:(e + 1) * 4096] = blk.reshape(8, 128, 512).transpose(1, 0, 2).reshape(128, 4096)
            e += 1
    return out


def _fm(v, n):
    return np.ascontiguousarray(v.reshape(n, 128).T)


def _build_params(l, g_pre, g_post, g_mem, conv_a_w, pool_scale, conv_c_w, conv_c_b, ln_c_g, ln_c_b):
    p = np.zeros((128, NPAR), np.float32)
    p[:, P_GMEM:P_GMEM + 32] = _fm(g_mem[l], 32)
    p[:, P_CAW:P_CAW + 48] = conv_a_w[l].reshape(3, 16, 128).transpose(2, 1, 0).reshape(128, 48)
    p[:, P_PS:P_PS + 16] = _fm(pool_scale[l], 16)
    p[:, P_CCW:P_CCW + 496] = conv_c_w[l].reshape(31, 16, 128).transpose(2, 1, 0).reshape(128, 496)
    p[:, P_CCB:P_CCB + 16] = _fm(conv_c_b[l], 16)
    p[:, P_LNG:P_LNG + 16] = _fm(ln_c_g[l], 16)
    p[:, P_LNB:P_LNB + 16] = _fm(ln_c_b[l], 16)
    return p


ENGS = ("pe", "act", "dve", "pool", "sp")
SAME_SYNC = os.environ.get("MK_SAME_SYNC", "1") == "1"


class Sched:
    def __init__(self):
        self.ops = {e: [] for e in ENGS}
        self.res = {}
        self.waited = {e: {} for e in ENGS}
        self.epoch = 0
        self.dma_cnt = {}
        self.const = set()

    def _need(self, eng, o, ev):
        kind, key, val = ev
        if kind == "c" and key == eng and (eng == "pe" or not SAME_SYNC):
            return
        wk = (kind, key)
        if self.waited[eng].get(wk, -1) >= val:
            return
        self.waited[eng][wk] = val
        o["waits"].append(ev)
        if kind == "c":
            self.ops[key][val]["sig"] = True

    def op(self, eng, fn, reads=(), writes=(), dsem=None):
        o = dict(fn=fn, waits=[], sig=False, dsem=dsem, epoch=self.epoch)
        idx = len(self.ops[eng])
        deps = []
        for r in reads:
            st = self.res.get(r)
            if st and st[0]:
                deps.append(st[0])
        for w in writes:
            st = self.res.get(w)
            if st:
                if st[0]:
                    deps.append(st[0])
                deps += st[1]
        for ev in deps:
            self._need(eng, o, ev)
        self.ops[eng].append(o)
        if dsem:
            c = self.dma_cnt.get(dsem, 0) + 1
            self.dma_cnt[dsem] = c
            ev = ("d", dsem, c * 16)
        else:
            ev = ("c", eng, idx)
        for r in reads:
            if r in self.const:
                continue
            self.res.setdefault(r, [None, []])[1].append(ev)
        for w in writes:
            self.res[w] = [ev, []]
        return ev

    def finalize(self):
        self.nepoch = self.epoch + 1
        for e in ENGS:
            counts = {}
            for o in self.ops[e]:
                if o["sig"]:
                    counts[o["epoch"]] = counts.get(o["epoch"], 0) + 1
                o["cnt"] = counts.get(o["epoch"], 0)

    def emit_engine(self, eng, e, semtab, dsems):
        for o in self.ops[eng]:
            for (kind, key, val) in o["waits"]:
                if kind == "c":
                    t = self.ops[key][val]
                    e.wait_ge(semtab[key][t["epoch"]], t["cnt"])
                else:
                    e.wait_ge(dsems[key], val)
            ins = o["fn"](e)
            if o["sig"]:
                ins.then_inc(semtab[eng][o["epoch"]], 1)
            if o["dsem"]:
                ins.then_inc(dsems[o["dsem"]], 16)


def build_program(depth=DEPTH, ntiles=NTILES):
    nc = bass.Bass("TRN2", target_bir_lowering=False)
    xT = nc.dram_tensor("xT", [D, NTOK], F32, kind="ExternalInput").ap()
    memT = nc.dram_tensor("memT", [D, NMEM], F32, kind="ExternalInput").ap()
    wmain = nc.dram_tensor("wmain", [DEPTH * 128, MAIN_COLS], F32, kind="ExternalInput").ap()
    wmem = nc.dram_tensor("wmem", [DEPTH * 128, MEM_COLS], F32, kind="ExternalInput").ap()
    params = nc.dram_tensor("params", [DEPTH * 128, NPAR], F32, kind="ExternalInput").ap()
    gvin = nc.dram_tensor("gvec", [128, 128], F32, kind="ExternalInput").ap()
    aux = nc.dram_tensor("aux", [128, 72], F32, kind="ExternalInput").ap()
    outT = nc.dram_tensor("outT", [D, NTOK], F32, kind="ExternalOutput").ap()
    ybuf = nc.dram_tensor("ybuf", [D, NT], F32, kind="Internal").ap()
    dcbuf = nc.dram_tensor("dcbuf", [2048, NT], F32, kind="Internal").ap()

    S = Sched()
    es = contextlib.ExitStack()
    with es:
        big = es.enter_context(nc.sbuf_tensor("big", [128, 45056], BF16))
        mg = es.enter_context(nc.sbuf_tensor("mg", [128, 22528], BF16))
        W = [es.enter_context(nc.sbuf_tensor(f"w{i}", [128, 4096], BF16)) for i in range(NBW)]
        T = [es.enter_context(nc.sbuf_tensor(f"t{i}", [128, PADW], F32)) for i in range(NTMP)]
        IO = [es.enter_context(nc.sbuf_tensor(f"io{i}", [128, NT], F32)) for i in range(NIO)]
        rstdp = es.enter_context(nc.sbuf_tensor("rstdp", [128, NT], F32))
        ones32 = es.enter_context(nc.sbuf_tensor("ones32", [128, 128], F32))
        ones16 = es.enter_context(nc.sbuf_tensor("ones16", [128, 128], BF16))
        par = es.enter_context(nc.sbuf_tensor("par", [128, NPAR], F32))
        gvec = es.enter_context(nc.sbuf_tensor("gvecs", [128, 128], F32))
        auxs = es.enter_context(nc.sbuf_tensor("auxs", [128, 72], F32))
        KT = es.enter_context(nc.sbuf_tensor("KT", [128, 8 * 256], BF16))
        VV = es.enter_context(nc.sbuf_tensor("VV", [128, 2 * 1024], BF16))
        acar = es.enter_context(nc.sbuf_tensor("acar", [128, 16 * 4], F32))
        ucar = es.enter_context(nc.sbuf_tensor("ucar", [128, 16 * 16], F32))
        ccar = es.enter_context(nc.sbuf_tensor("ccar", [128, 16 * 32], F32))
        lnA = es.enter_context(nc.sbuf_tensor("lnA", [128, NT], F32))
        lnB = es.enter_context(nc.sbuf_tensor("lnB", [128, NT], F32))
        ps = es.enter_context(nc.psum_tensor("ps", [128, 8, 512], F32))

        S.const.update({("aux",), ("ones",), ("gvec",)})

        def v2(ap):
            return ap.rearrange("p (a b) -> p a b", a=2)

        def hT(kc, st=None):
            if st is None:
                return big[:, kc * NT:(kc + 1) * NT]
            return big[:, kc * NT + st * NS: kc * NT + (st + 1) * NS]

        def s0(j, st=None):
            return hT(32 + j, st)

        def s1(j, st=None):
            return hT(48 + j, st)

        def mgc(kc, st=None):
            if st is None:
                return mg[:, kc * NT:(kc + 1) * NT]
            return mg[:, kc * NT + st * NS: kc * NT + (st + 1) * NS]

        def slot(s):
            return ps[:, 2 * s:2 * s + 2, 0:NS]

        def pcol(c):
            return par[:, c:c + 1]

        def gcol(c):
            return gvec[:, c:c + 1]

        st_ = dict(tmp=0, slot=0, w=0, io=0)

        def tmp():
            i = st_["tmp"]
            st_["tmp"] = (i + 1) % NTMP
            return i

        def io():
            i = st_["io"]
            st_["io"] = (i + 1) % NIO
            return i

        def next_slot():
            s = st_["slot"]
            st_["slot"] = (s + 1) % 3
            return s

        def rk_slot(s):
            return [("ps", 2 * s), ("ps", 2 * s + 1)]

        def load_w(src_ap, ncols):
            s = st_["w"]
            st_["w"] = (s + 1) % NBW
            S.op("pool", lambda e, s=s, src_ap=src_ap, ncols=ncols: e.dma_start(out=W[s][:, 0:ncols], in_=src_ap),
                 reads=[], writes=[("w", s)], dsem=f"w{s}")
            return s

        ent_pos = dict(i=0)

        def next_main(l, kind):
            i = ent_pos["i"]
            k, _ = MAIN_ENT[i]
            assert k == kind, (k, kind, i)
            o = int(MAIN_OFF[i])
            n = _entry_cols(k)
            ent_pos["i"] = i + 1
            return load_w(wmain[l * 128:(l + 1) * 128, o:o + n], n)

        def mm_job(ws, nk, rhs_fn, rhs_keys, slab0=0, sl=None):
            if sl is None:
                sl = next_slot()
            for kc in range(nk):
                for st in range(2):
                    S.op("pe", lambda e, sl=sl, ws=ws, kc=kc, st=st, nk=nk: e.matmul(
                        ps[:, 2 * sl + st, 0:NS], W[ws][:, (slab0 + kc) * 128:(slab0 + kc + 1) * 128],
                        rhs_fn(kc, st), start=(kc == 0), stop=(kc == nk - 1)),
                        reads=[("w", ws), rhs_keys(kc)], writes=[("ps", 2 * sl + st)])
            st_["jobs"] = st_.get("jobs", 0) + 1
            if NOP_CYC > 0 and st_["jobs"] % NOP_EVERY == 0:
                S.op("pe", lambda e: e.nop(cycle_cnt=NOP_CYC))
            return sl

        def proj_job(l):
            ws = next_main(l, "in")
            return mm_job(ws, 32, hT, lambda kc: ("big", kc))

        def act(fn, reads, writes):
            return S.op("act", fn, reads, writes)

        def dve(fn, reads, writes):
            return S.op("dve", fn, reads, writes)

        def ones_reduce(src, key):
            for st in range(2):
                S.op("pe", lambda e, st=st: e.matmul(ps[:, 6 + st, 0:NS], ones32[:], src[:, st * NS:(st + 1) * NS],
                                                     start=True, stop=True), [key, ("ones",)], [("ps", 6 + st)])

        def rstd_from_ps(dst, key, scale):
            sd = tmp()
            act(lambda e, sd=sd: e.activation(out=v2(T[sd][:, 0:NT]), in_=ps[:, 6:8, 0:NS], func=AF.Sqrt,
                                              bias=EPS, scale=scale), [("ps", 6), ("ps", 7)], [("T", sd)])
            dve(lambda e, sd=sd: e.reciprocal(out=dst[:], in_=T[sd][:, 0:NT]), [("T", sd)], [key])

        dve(lambda e: e.memset(ones32[:], 1.0), [], [("ones",)])
        dve(lambda e: e.memset(ones16[:], 1.0), [], [("ones",)])
        S.op("sp", lambda e: e.dma_start(out=auxs[:], in_=aux), [], [("aux",)], dsem="aux")
        S.op("sp", lambda e: e.dma_start(out=gvec[:], in_=gvin), [], [("gvec",)], dsem="gvec")

        def x_src(l):
            return xT if l == 0 else outT

        def load_io(l, ti, kc, src=None, key=None):
            i = io()
            if src is None:
                src = x_src(l)[kc * 128:(kc + 1) * 128, ti * NT:(ti + 1) * NT]
                rk = [("o", ti, kc)] if l > 0 else []
            else:
                rk = [key]
            S.op("sp", lambda e, i=i, src=src: e.dma_start(out=IO[i][:], in_=src), reads=rk, writes=[("IO", i)],
                 dsem=f"lio{i}")
            return i

        def phase1_pieces(l, ti):
            P = []
            issued = {}
            nl = dict(n=0)

            def ensure_loaded(upto):
                while nl["n"] <= min(upto, 63):
                    issued[nl["n"]] = load_io(l, ti, nl["n"] % 32)
                    nl["n"] += 1

            def p1(kc):
                def f():
                    ensure_loaded(kc + 2)
                    i = issued[kc]
                    if kc == 0:
                        act(lambda e, i=i: e.activation(out=lnA[:], in_=IO[i][:], func=AF.Square), [("IO", i)], [("lnA",)])
                    else:
                        q = tmp()
                        act(lambda e, i=i, q=q: e.activation(out=T[q][:, 0:NT], in_=IO[i][:], func=AF.Square),
                            [("IO", i)], [("T", q)])
                        dve(lambda e, q=q: e.tensor_tensor(out=lnA[:], in0=lnA[:], in1=T[q][:, 0:NT], op=ALU.add),
                            [("T", q), ("lnA",)], [("lnA",)])
                return f

            def fin():
                ensure_loaded(33)
                ones_reduce(lnA, ("lnA",))
                rstd_from_ps(lnA, ("lnA",), 1.0 / D)

            def p2(kc):
                def f():
                    ensure_loaded(32 + kc + 2)
                    i = issued[32 + kc]
                    dve(lambda e, i=i, kc=kc: e.scalar_tensor_tensor(
                        out=hT(kc), in0=IO[i][:], scalar=gcol(l * 64 + kc), in1=lnA[:],
                        op0=ALU.mult, op1=ALU.mult), [("IO", i), ("lnA",), ("gvec",)], [("big", kc)])
                return f

            for kc in range(32):
                P.append(p1(kc))
            P.append(fin)
            for kc in range(32):
                P.append(p2(kc))
            return P

        def final_pieces(l, ti):
            P = []
            bufs = {}

            def loads(oc):
                a = load_io(l, ti, oc, src=ybuf[oc * 128:(oc + 1) * 128, :], key=("yb", oc))
                b = load_io(l, ti, oc)
                bufs[oc] = (a, b)

            def fp(oc):
                def f():
                    if oc == 0:
                        loads(0)
                    if oc + 1 < 32:
                        loads(oc + 1)
                    a, b = bufs[oc]
                    dve(lambda e, a=a, oc=oc: e.scalar_tensor_tensor(
                        out=IO[a][:], in0=IO[a][:], scalar=gcol(l * 64 + 32 + oc), in1=rstdp[:], op0=ALU.mult, op1=ALU.mult),
                        [("IO", a), ("rstdp",), ("gvec",)], [("IO", a)])
                    dve(lambda e, a=a, b=b: e.tensor_tensor(out=IO[b][:], in0=IO[b][:], in1=IO[a][:], op=ALU.add),
                        [("IO", a), ("IO", b)], [("IO", b)])
                    S.op("sp", lambda e, b=b, oc=oc: e.dma_start(
                        out=outT[oc * 128:(oc + 1) * 128, ti * NT:(ti + 1) * NT], in_=IO[b][:]),
                        reads=[("IO", b)], writes=[("o", ti, oc)], dsem=f"sio{b}")
                return f

            for oc in range(32):
                P.append(fp(oc))
            return P

        def drain(P, n):
            for _ in range(n):
                if P:
                    P.pop(0)()

        def layer_setup(l):
            S.op("sp", lambda e, l=l: e.dma_start(out=par[:], in_=params[l * 128:(l + 1) * 128, :]),
                 [], [("par",)], dsem="par")
            dve(lambda e: e.memset(acar[:], 0.0), [], [("acar",)])
            dve(lambda e: e.memset(ucar[:], 0.0), [], [("ucar",)])
            dve(lambda e: e.memset(ccar[:], 0.0), [], [("ccar",)])
            for kc in range(32):
                i = tmp()
                S.op("sp", lambda e, i=i, kc=kc: e.dma_start(out=T[i][:, 0:NMEM], in_=memT[kc * 128:(kc + 1) * 128, :]),
                     [], [("T", i)], dsem=f"ld{i}")
                q = tmp()
                act(lambda e, i=i, q=q: e.activation(out=T[q][:, 0:NMEM], in_=T[i][:, 0:NMEM], func=AF.Square),
                    [("T", i)], [("T", q)])
                S.op("pe", lambda e, q=q, kc=kc: e.matmul(ps[:, 6, 0:NMEM], ones32[:], T[q][:, 0:NMEM],
                                                          start=(kc == 0), stop=(kc == 31)),
                     [("T", q), ("ones",)], [("ps", 6)])
            sd = tmp()
            act(lambda e, sd=sd: e.activation(out=T[sd][:, 0:NMEM], in_=ps[:, 6, 0:NMEM], func=AF.Sqrt,
                                              bias=EPS, scale=1.0 / D), [("ps", 6)], [("T", sd)])
            rm = lnB
            dve(lambda e, sd=sd: e.reciprocal(out=rm[:, 0:NMEM], in_=T[sd][:, 0:NMEM]), [("T", sd)], [("lnB",)])

            def mh(kc):
                return big[:, 48 * NT + kc * NMEM: 48 * NT + (kc + 1) * NMEM]
            MHK = [("big", 48 + u) for u in range(16)]
            for kc in range(32):
                i = tmp()
                S.op("sp", lambda e, i=i, kc=kc: e.dma_start(out=T[i][:, 0:NMEM], in_=memT[kc * 128:(kc + 1) * 128, :]),
                     [], [("T", i)], dsem=f"ld{i}")
                dve(lambda e, i=i, kc=kc: e.scalar_tensor_tensor(
                    out=mh(kc), in0=T[i][:, 0:NMEM], scalar=pcol(P_GMEM + kc), in1=rm[:, 0:NMEM],
                    op0=ALU.mult, op1=ALU.mult), [("T", i), ("lnB",), ("par",)],
                    [("big", 48 + (kc * NMEM) // NT), ("big", 48 + (kc * NMEM + NMEM - 1) // NT)])
            for oc in range(8):
                ws = load_w(wmem[l * 128:(l + 1) * 128, oc * 4096:(oc + 1) * 4096], 4096)
                sl = next_slot()
                for kc in range(32):
                    S.op("pe", lambda e, sl=sl, ws=ws, kc=kc: e.matmul(
                        ps[:, 2 * sl, 0:NMEM], W[ws][:, kc * 128:(kc + 1) * 128], mh(kc),
                        start=(kc == 0), stop=(kc == 31)),
                        [("w", ws)] + MHK, [("ps", 2 * sl)])
                act(lambda e, sl=sl, oc=oc: e.activation(out=KT[:, oc * 256:(oc + 1) * 256], in_=ps[:, 2 * sl, 0:NMEM],
                                                         func=AF.Copy), [("ps", 2 * sl)], [("KT",)])
            e_i = 8
            for ch in range(2):
                for q4 in range(4):
                    ws = load_w(wmem[l * 128:(l + 1) * 128, e_i * 4096:(e_i + 1) * 4096], 4096)
                    e_i += 1
                    for i8 in range(8):
                        kc = q4 * 8 + i8
                        for mc in range(2):
                            S.op("pe", lambda e, ws=ws, kc=kc, mc=mc, i8=i8: e.matmul(
                                ps[:, 6 + mc, 0:512], mh(kc)[:, mc * 128:(mc + 1) * 128],
                                W[ws][:, i8 * 512:(i8 + 1) * 512], start=(kc == 0), stop=(kc == 31)),
                                [("w", ws)] + MHK, [("ps", 6 + mc)])
                for mc in range(2):
                    act(lambda e, mc=mc, ch=ch: e.activation(
                        out=VV[:, mc * 1024 + ch * 512: mc * 1024 + (ch + 1) * 512], in_=ps[:, 6 + mc, 0:512],
                        func=AF.Copy), [("ps", 6 + mc)], [("VV",)])

        def front_a(l, ti, pend):
            for j in range(16):
                sl = proj_job(l)
                vS = tmp()
                act(lambda e, sl=sl, vS=vS: e.activation(out=v2(T[vS][:, 0:NT]), in_=slot(sl), func=AF.Copy),
                    rk_slot(sl), [("T", vS)])
                sl = proj_job(l)
                cv = tmp()
                act(lambda e, cv=cv, j=j: e.activation(out=T[cv][:, 0:4], in_=acar[:, j * 4:(j + 1) * 4], func=AF.Copy),
                    [("acar",)], [("T", cv)])
                dve(lambda e, sl=sl, vS=vS, cv=cv: e.tensor_tensor(
                    out=v2(T[cv][:, 4:4 + NT]), in0=slot(sl), in1=v2(T[vS][:, 0:NT]), op=ALU.mult),
                    rk_slot(sl) + [("T", vS), ("T", cv)], [("T", cv)])
                act(lambda e, cv=cv, j=j: e.activation(out=acar[:, j * 4:(j + 1) * 4], in_=T[cv][:, NT:NT + 4], func=AF.Copy),
                    [("T", cv)], [("acar",)])
                acc = tmp()
                dve(lambda e, cv=cv, acc=acc, j=j: e.tensor_scalar(
                    out=T[acc][:, 0:NT], in0=T[cv][:, 2:2 + NT], scalar1=pcol(P_CAW + j * 3 + 0), scalar2=None,
                    op0=ALU.mult), [("T", cv), ("par",)], [("T", acc)])
                for k in (1, 2):
                    dve(lambda e, cv=cv, acc=acc, j=j, k=k: e.scalar_tensor_tensor(
                        out=T[acc][:, 0:NT], in0=T[cv][:, 2 + k:2 + k + NT], scalar=pcol(P_CAW + j * 3 + k),
                        in1=T[acc][:, 0:NT], op0=ALU.mult, op1=ALU.add), [("T", cv), ("T", acc), ("par",)], [("T", acc)])
                sl = proj_job(l)
                dve(lambda e, sl=sl, acc=acc: e.tensor_tensor(
                    out=v2(T[acc][:, 0:NT]), in0=slot(sl), in1=v2(T[acc][:, 0:NT]), op=ALU.mult),
                    rk_slot(sl) + [("T", acc)], [("T", acc)])
                sl = proj_job(l)
                sz = tmp()
                act(lambda e, sl=sl, sz=sz: e.activation(out=v2(T[sz][:, 0:NT]), in_=slot(sl), func=AF.Silu),
                    rk_slot(sl), [("T", sz)])
                dve(lambda e, acc=acc, sz=sz, j=j: e.tensor_tensor(
                    out=s0(j), in0=T[acc][:, 0:NT], in1=T[sz][:, 0:NT], op=ALU.mult),
                    [("T", acc), ("T", sz)], [("big", 32 + j)])
                drain(pend, 2)

        def front_b(l, ti):
            for j in range(16):
                g = j // 4
                wwin = 2 ** (g + 1)
                sl = proj_job(l)
                U = tmp()
                act(lambda e, U=U, j=j: e.activation(out=T[U][:, 0:16], in_=ucar[:, j * 16:(j + 1) * 16], func=AF.Copy),
                    [("ucar",)], [("T", U)])
                act(lambda e, sl=sl, U=U: e.activation(out=v2(T[U][:, 16:16 + NT]), in_=slot(sl), func=AF.Copy),
                    rk_slot(sl) + [("T", U)], [("T", U)])
                act(lambda e, U=U, j=j: e.activation(out=ucar[:, j * 16:(j + 1) * 16], in_=T[U][:, NT:NT + 16], func=AF.Copy),
                    [("T", U)], [("ucar",)])
                pp = [tmp(), tmp()]
                Sb = U
                sh = 1
                for step in range(g + 1):
                    lo = 2 * sh - 1
                    Tn = pp[step % 2]
                    dve(lambda e, Sb=Sb, Tn=Tn, lo=lo, sh=sh: e.tensor_tensor(
                        out=T[Tn][:, lo:16 + NT], in0=T[Sb][:, lo:16 + NT], in1=T[Sb][:, lo - sh:16 + NT - sh],
                        op=ALU.add), [("T", Sb)], [("T", Tn)])
                    Sb = Tn
                    sh *= 2
                P = tmp()
                dve(lambda e, Sb=Sb, U=U, P=P, wwin=wwin: e.scalar_tensor_tensor(
                    out=T[P][:, 0:NT], in0=T[Sb][:, 16:16 + NT], scalar=1.0 / wwin, in1=T[U][:, 16:16 + NT],
                    op0=ALU.mult, op1=ALU.subtract), [("T", Sb), ("T", U)], [("T", P)])
                if ti == 0:
                    fx = pp[(g + 1) % 2]
                    dve(lambda e, Sb=Sb, fx=fx, g=g: e.tensor_tensor(
                        out=T[fx][:, 0:16], in0=T[Sb][:, 16 + HALO:16 + HALO + 16], in1=auxs[:, g * 16:(g + 1) * 16],
                        op=ALU.mult), [("T", Sb), ("aux",)], [("T", fx)])
                    dve(lambda e, U=U, fx=fx, P=P: e.tensor_tensor(
                        out=T[P][:, HALO:HALO + 16], in0=T[fx][:, 0:16], in1=T[U][:, 16 + HALO:16 + HALO + 16],
                        op=ALU.subtract), [("T", fx), ("T", U), ("T", P)], [("T", P)])
                sl = proj_job(l)
                sz = tmp()
                act(lambda e, sl=sl, sz=sz: e.activation(out=v2(T[sz][:, 0:NT]), in_=slot(sl), func=AF.Silu),
                    rk_slot(sl), [("T", sz)])
                dve(lambda e, P=P, sz=sz, j=j: e.scalar_tensor_tensor(
                    out=s1(j), in0=T[P][:, 0:NT], scalar=pcol(P_PS + j), in1=T[sz][:, 0:NT],
                    op0=ALU.mult, op1=ALU.mult), [("T", P), ("T", sz), ("par",)], [("big", 48 + j)])

        def back(l, first, nk_y, rhs_y, keys_y, kind, hook=None, mid=None):
            for oc in range(32):
                sl = proj_job(l)
                sg1 = tmp()
                act(lambda e, sl=sl, sg1=sg1: e.activation(out=v2(T[sg1][:, 0:NT]), in_=slot(sl), func=AF.Sigmoid),
                    rk_slot(sl), [("T", sg1)])
                if mid is not None:
                    mid()
                sl = proj_job(l)
                sg2 = tmp()
                act(lambda e, sl=sl, sg2=sg2: e.activation(out=v2(T[sg2][:, 0:NT]), in_=slot(sl), func=AF.Sigmoid),
                    rk_slot(sl), [("T", sg2)])
                if mid is not None:
                    mid()
                ws = next_main(l, kind)
                sx = mm_job(ws, 16, s0, lambda kc: ("big", 32 + kc), slab0=0)
                sy = mm_job(ws, nk_y, lambda kc, st, oc=oc: rhs_y(oc, kc, st), lambda kc, oc=oc: keys_y(oc, kc), slab0=16)
                dve(lambda e, sx=sx, sg1=sg1: e.tensor_tensor(
                    out=v2(T[sg1][:, 0:NT]), in0=slot(sx), in1=v2(T[sg1][:, 0:NT]), op=ALU.mult),
                    rk_slot(sx) + [("T", sg1)], [("T", sg1)])
                dve(lambda e, sy=sy, sg2=sg2: e.tensor_tensor(
                    out=v2(T[sg2][:, 0:NT]), in0=slot(sy), in1=v2(T[sg2][:, 0:NT]), op=ALU.mult),
                    rk_slot(sy) + [("T", sg2)], [("T", sg2)])
                if first:
                    dve(lambda e, sg1=sg1, sg2=sg2, oc=oc: e.tensor_tensor(
                        out=mgc(oc), in0=T[sg1][:, 0:NT], in1=T[sg2][:, 0:NT], op=ALU.add),
                        [("T", sg1), ("T", sg2)], [("mg", oc)])
                else:
                    dve(lambda e, sg1=sg1, sg2=sg2: e.tensor_tensor(
                        out=T[sg1][:, 0:NT], in0=T[sg1][:, 0:NT], in1=T[sg2][:, 0:NT], op=ALU.add),
                        [("T", sg1), ("T", sg2)], [("T", sg1)])
                    dve(lambda e, sg1=sg1, oc=oc: e.tensor_tensor(
                        out=mgc(oc), in0=mgc(oc), in1=T[sg1][:, 0:NT], op=ALU.add),
                        [("T", sg1), ("mg", oc)], [("mg", oc)])
                if hook is not None:
                    hook(oc)

        def conv_chunk(l, j):
            tmp()
            sl = proj_job(l)
            sg = tmp()
            act(lambda e, sl=sl, sg=sg: e.activation(out=v2(T[sg][:, 0:NT]), in_=slot(sl), func=AF.Sigmoid),
                rk_slot(sl), [("T", sg)])
            sl = proj_job(l)
            gc = tmp()
            act(lambda e, gc=gc, j=j: e.activation(out=T[gc][:, 0:32], in_=ccar[:, j * 32:(j + 1) * 32], func=AF.Copy),
                [("ccar",)], [("T", gc)])
            dve(lambda e, sl=sl, sg=sg, gc=gc: e.tensor_tensor(
                out=v2(T[gc][:, 32:32 + NT]), in0=slot(sl), in1=v2(T[sg][:, 0:NT]), op=ALU.mult),
                rk_slot(sl) + [("T", sg), ("T", gc)], [("T", gc)])
            act(lambda e, gc=gc, j=j: e.activation(out=ccar[:, j * 32:(j + 1) * 32], in_=T[gc][:, NT:NT + 32], func=AF.Copy),
                [("T", gc)], [("ccar",)])
            acc = tmp()

            def taps(k0, k1):
                for k in range(k0, k1):
                    if k == 0:
                        dve(lambda e, gc=gc, acc=acc, j=j: e.tensor_scalar(
                            out=T[acc][:, 0:NT], in0=T[gc][:, 2:2 + NT], scalar1=pcol(P_CCW + j * 31), scalar2=pcol(P_CCB + j),
                            op0=ALU.mult, op1=ALU.add), [("T", gc), ("par",)], [("T", acc)])
                    else:
                        dve(lambda e, gc=gc, acc=acc, j=j, k=k: e.scalar_tensor_tensor(
                            out=T[acc][:, 0:NT], in0=T[gc][:, 2 + k:2 + k + NT], scalar=pcol(P_CCW + j * 31 + k),
                            in1=T[acc][:, 0:NT], op0=ALU.mult, op1=ALU.add), [("T", gc), ("T", acc), ("par",)], [("T", acc)])

            def tail():
                S.op("sp", lambda e, acc=acc, j=j: e.dma_start(out=dcbuf[j * 128:(j + 1) * 128, :], in_=T[acc][:, 0:NT]),
                     reads=[("T", acc)], writes=[("dc", j)], dsem=f"st{acc}")
                if j == 0:
                    act(lambda e, acc=acc: e.activation(out=lnB[:], in_=T[acc][:, 0:NT], func=AF.Square),
                        [("T", acc)], [("lnB",)])
                    dve(lambda e, acc=acc: e.tensor_copy(out=lnA[:], in_=T[acc][:, 0:NT]), [("T", acc)], [("lnA",)])
                else:
                    sq = tmp()
                    act(lambda e, acc=acc, sq=sq: e.activation(out=T[sq][:, 0:NT], in_=T[acc][:, 0:NT], func=AF.Square),
                        [("T", acc)], [("T", sq)])
                    dve(lambda e, acc=acc: e.tensor_tensor(out=lnA[:], in0=lnA[:], in1=T[acc][:, 0:NT], op=ALU.add),
                        [("T", acc), ("lnA",)], [("lnA",)])
                    dve(lambda e, sq=sq: e.tensor_tensor(out=lnB[:], in0=lnB[:], in1=T[sq][:, 0:NT], op=ALU.add),
                        [("T", sq), ("lnB",)], [("lnB",)])

            taps(0, 6)
            return [lambda: taps(6, 12), lambda: taps(12, 18), lambda: taps(18, 24),
                    lambda: taps(24, 31), tail]

        def front_c_x(l, ti):
            for i8 in range(8):
                sl = proj_job(l)
                act(lambda e, sl=sl, i8=i8: e.activation(out=v2(s1(8 + i8)), in_=slot(sl), func=AF.Copy),
                    rk_slot(sl), [("big", 48 + 8 + i8)])

            ones_reduce(lnA, ("lnA",))
            mu = tmp()
            dve(lambda e, mu=mu: e.tensor_scalar(out=v2(T[mu][:, 0:NT]), in0=ps[:, 6:8, 0:NS], scalar1=1.0 / 2048,
                                                 scalar2=None, op0=ALU.mult), [("ps", 6), ("ps", 7)], [("T", mu)])
            ones_reduce(lnB, ("lnB",))
            msq = tmp()
            dve(lambda e, mu=mu, msq=msq: e.tensor_tensor(out=T[msq][:, 0:NT], in0=T[mu][:, 0:NT], in1=T[mu][:, 0:NT],
                                                          op=ALU.mult), [("T", mu)], [("T", msq)])
            var = tmp()
            dve(lambda e, msq=msq, var=var: e.scalar_tensor_tensor(
                out=v2(T[var][:, 0:NT]), in0=ps[:, 6:8, 0:NS], scalar=1.0 / 2048, in1=v2(T[msq][:, 0:NT]),
                op0=ALU.mult, op1=ALU.subtract), [("ps", 6), ("ps", 7), ("T", msq)], [("T", var)])
            act(lambda e, var=var: e.activation(out=T[var][:, 0:NT], in_=T[var][:, 0:NT], func=AF.Sqrt, bias=EPS, scale=1.0),
                [("T", var)], [("T", var)])
            dve(lambda e, var=var: e.reciprocal(out=lnA[:], in_=T[var][:, 0:NT]), [("T", var)], [("lnA",)])
            dve(lambda e, mu=mu: e.scalar_tensor_tensor(out=lnB[:], in0=T[mu][:, 0:NT], scalar=-1.0, in1=lnA[:],
                                                        op0=ALU.mult, op1=ALU.mult), [("T", mu), ("lnA",)], [("lnB",)])

            def z_job(j):
                sl = proj_job(l)
                sz = tmp()
                act(lambda e, sl=sl, sz=sz: e.activation(out=v2(T[sz][:, 0:NT]), in_=slot(sl), func=AF.Silu),
                    rk_slot(sl), [("T", sz)])
                di = load_io(l, ti, j, src=dcbuf[j * 128:(j + 1) * 128, :], key=("dc", j))
                tn = tmp()
                dve(lambda e, tn=tn, di=di: e.tensor_tensor(out=T[tn][:, 0:NT], in0=IO[di][:], in1=lnA[:], op=ALU.mult),
                    [("IO", di), ("lnA",)], [("T", tn)])
                dve(lambda e, tn=tn: e.tensor_tensor(out=T[tn][:, 0:NT], in0=T[tn][:, 0:NT], in1=lnB[:], op=ALU.add),
                    [("T", tn), ("lnB",)], [("T", tn)])
                act(lambda e, tn=tn, j=j: e.activation(out=T[tn][:, 0:NT], in_=T[tn][:, 0:NT], func=AF.Silu,
                                                       bias=pcol(P_LNB + j), scale=pcol(P_LNG + j)),
                    [("T", tn), ("par",)], [("T", tn)])
                dve(lambda e, tn=tn, sz=sz, j=j: e.tensor_tensor(out=s0(j), in0=T[tn][:, 0:NT], in1=T[sz][:, 0:NT],
                                                                 op=ALU.mult), [("T", tn), ("T", sz)], [("big", 32 + j)])

            Ebuf = {}

            def scores(h):
                sls = [next_slot(), next_slot()]
                for mc in range(2):
                    for st in range(2):
                        for dc in range(2):
                            S.op("pe", lambda e, h=h, mc=mc, st=st, dc=dc, sl=sls[mc]: e.matmul(
                                ps[:, 2 * sl + st, 0:NS],
                                KT[:, (h * 2 + dc) * 256 + mc * 128:(h * 2 + dc) * 256 + (mc + 1) * 128],
                                s1(8 + h * 2 + dc, st), start=(dc == 0), stop=(dc == 1)),
                                [("KT",), ("big", 48 + 8 + h * 2 + dc)], [("ps", 2 * sls[mc] + st)])
                E = tmp()
                Eb = T[E].bitcast(BF16)
                for mc in range(2):
                    act(lambda e, mc=mc, sl=sls[mc], Eb=Eb: e.activation(
                        out=v2(Eb[:, mc * NT:(mc + 1) * NT]), in_=slot(sl), func=AF.Exp, scale=1.0 / 16.0),
                        rk_slot(sls[mc]) + [("T", E)], [("T", E)])
                Ebuf[h] = (E, Eb)

            def attv(h):
                E, Eb = Ebuf[h]
                for st in range(2):
                    for mc in range(2):
                        S.op("pe", lambda e, st=st, mc=mc, Eb=Eb: e.matmul(
                            ps[:, 6 + st, 0:NS], ones16[:], Eb[:, mc * NT + st * NS: mc * NT + (st + 1) * NS],
                            start=(mc == 0), stop=(mc == 1)), [("T", E), ("ones",)], [("ps", 6 + st)])
                rd = tmp()
                dve(lambda e, rd=rd: e.reciprocal(out=v2(T[rd][:, 0:NT]), in_=ps[:, 6:8, 0:NS]),
                    [("ps", 6), ("ps", 7)], [("T", rd)])
                for dcp in range(2):
                    sl = next_slot()
                    for st in range(2):
                        for mc in range(2):
                            S.op("pe", lambda e, sl=sl, st=st, mc=mc, h=h, dcp=dcp, Eb=Eb: e.matmul(
                                ps[:, 2 * sl + st, 0:NS],
                                VV[:, mc * 1024 + (h * 2 + dcp) * 128: mc * 1024 + (h * 2 + dcp + 1) * 128],
                                Eb[:, mc * NT + st * NS: mc * NT + (st + 1) * NS], start=(mc == 0), stop=(mc == 1)),
                                [("T", E), ("VV",)], [("ps", 2 * sl + st)])
                    dve(lambda e, sl=sl, rd=rd, h=h, dcp=dcp: e.tensor_tensor(
                        out=v2(s1(h * 2 + dcp)), in0=slot(sl), in1=v2(T[rd][:, 0:NT]), op=ALU.mult),
                        rk_slot(sl) + [("T", rd)], [("big", 48 + h * 2 + dcp)])

            steps = []
            for h in range(4):
                steps += [lambda h=h: scores(h), lambda h=h: attv(h)]
            for j in range(16):
                z_job(j)
                if steps and j % 2 == 1:
                    steps.pop(0)()

        def wo_phase(l, ti, nxt):
            for oc in range(32):
                ws = next_main(l, "wo")
                sl = mm_job(ws, 32, mgc, lambda kc: ("mg", kc))
                t1 = tmp()
                act(lambda e, sl=sl, t1=t1: e.activation(out=v2(T[t1][:, 0:NT]), in_=slot(sl), func=AF.Copy),
                    rk_slot(sl), [("T", t1)])
                S.op("sp", lambda e, t1=t1, oc=oc: e.dma_start(out=ybuf[oc * 128:(oc + 1) * 128, :], in_=T[t1][:, 0:NT]),
                     reads=[("T", t1)], writes=[("yb", oc)], dsem=f"st{t1}")
                if oc == 0:
                    act(lambda e, sl=sl: e.activation(out=v2(lnB[:]), in_=slot(sl), func=AF.Square),
                        rk_slot(sl), [("lnB",)])
                else:
                    t2 = tmp()
                    act(lambda e, sl=sl, t2=t2: e.activation(out=v2(T[t2][:, 0:NT]), in_=slot(sl), func=AF.Square),
                        rk_slot(sl), [("T", t2)])
                    dve(lambda e, t2=t2: e.tensor_tensor(out=lnB[:], in0=lnB[:], in1=T[t2][:, 0:NT], op=ALU.add),
                        [("T", t2), ("lnB",)], [("lnB",)])
                drain(nxt, 4 if oc == 0 else 3)
            drain(nxt, len(nxt))
            assert ent_pos["i"] == len(MAIN_ENT)
            ones_reduce(lnB, ("lnB",))
            rstd_from_ps(rstdp, ("rstdp",), 1.0 / D)
            if ti == 0:
                dve(lambda e: e.tensor_scalar(out=rstdp[:, 0:HALO], in0=rstdp[:, 0:HALO], scalar1=auxs[:, 64:65],
                                              scalar2=None, op0=ALU.mult), [("rstdp",), ("aux",)], [("rstdp",)])

        seq = [(l, ti) for l in range(depth) for ti in range(ntiles)]
        pend_final = []
        drain_all = lambda P: drain(P, len(P))
        drain_all(phase1_pieces(*seq[0]))
        for n, (l, ti) in enumerate(seq):
            if ti == 0:
                layer_setup(l)
            ent_pos["i"] = 0
            front_a(l, ti, pend_final)
            drain_all(pend_final)
            front_b(l, ti)
            pend_conv = []

            def conv_hook(oc, l=l, pend_conv=pend_conv):
                if oc % 2 == 1:
                    drain(pend_conv, len(pend_conv))
                    pend_conv.extend(conv_chunk(l, oc // 2))
                else:
                    drain(pend_conv, 1)

            back(l, True, 4, lambda oc, kc, st: s1(4 * (oc // 8) + kc, st), lambda oc, kc: ("big", 48 + 4 * (oc // 8) + kc), "ab",
                 hook=conv_hook, mid=lambda pend_conv=pend_conv: drain(pend_conv, 1))
            drain(pend_conv, len(pend_conv))
            front_c_x(l, ti)
            back(l, False, 8, lambda oc, kc, st: s1(kc, st), lambda oc, kc: ("big", 48 + kc), "cx")
            nxt = phase1_pieces(*seq[n + 1]) if n + 1 < len(seq) else []
            wo_phase(l, ti, nxt)
            pend_final = final_pieces(l, ti)
            S.epoch += 1
        drain_all(pend_final)

        S.finalize()
        dsem_names = sorted(S.dma_cnt.keys())
        dsems = {n: es.enter_context(nc.semaphore("d_" + n)) for n in dsem_names}
        semtab = {e: [es.enter_context(nc.semaphore(f"c_{e}_{k}")) for k in range(S.nepoch)] for e in ENGS}
        block = es.enter_context(nc.Block())

        @block.tensor
        def _(e):
            S.emit_engine("pe", e, semtab, dsems)

        @block.scalar
        def _(e):
            S.emit_engine("act", e, semtab, dsems)

        @block.vector
        def _(e):
            S.emit_engine("dve", e, semtab, dsems)

        @block.gpsimd
        def _(e):
            S.emit_engine("pool", e, semtab, dsems)

        @block.sync
        def _(e):
            S.emit_engine("sp", e, semtab, dsems)
            for n in dsem_names:
                if n.startswith("st") or n.startswith("sio"):
                    e.wait_ge(dsems[n], S.dma_cnt[n] * 16)
    stats = {e: len(S.ops[e]) for e in ENGS}
    return nc, stats


def _prep_inputs(x, mem, g_pre, g_post, g_mem, w_in, conv_a_w, w_out_a, pool_scale, w_pool, conv_c_w, conv_c_b,
                 ln_c_g, ln_c_b, w_out_c, w_mem_kv, w_out_x, w_o, cores=None):
    f = lambda a: np.asarray(a, dtype=np.float32)
    x, mem = f(x), f(mem)
    w_in, w_out_a, w_pool, w_out_c, w_out_x, w_o, w_mem_kv = map(f, (w_in, w_out_a, w_pool, w_out_c, w_out_x, w_o, w_mem_kv))
    conv_c_w = f(conv_c_w)
    wmain = np.concatenate([_build_wmain(l, w_in, w_out_a, w_pool, w_out_c, w_out_x, w_o, conv_c_w) for l in range(DEPTH)], axis=0)
    wmem = np.concatenate([_build_wmem(l, w_mem_kv) for l in range(DEPTH)], axis=0)
    params = np.concatenate([_build_params(l, f(g_pre), f(g_post), f(g_mem), f(conv_a_w), f(pool_scale), conv_c_w,
                                           f(conv_c_b), f(ln_c_g), f(ln_c_b)) for l in range(DEPTH)], axis=0)
    gv = np.zeros((128, 128), np.float32)
    for l in range(DEPTH):
        gv[:, l * 64:l * 64 + 32] = _fm(f(g_pre)[l], 32)
        gv[:, l * 64 + 32:l * 64 + 64] = _fm(f(g_post)[l], 32)
    in_maps = []
    for c in (range(NCORES) if cores is None else cores):
        b, sc = c // 4, c % 4
        t0 = sc * TOK
        xt = np.zeros((D, NTOK), np.float32)
        if t0 == 0:
            xt[:, HALO:] = x[b, 0:TOK, :].T
        else:
            xt[:, :] = x[b, t0 - HALO:t0 + TOK, :].T
        auxa = np.zeros((128, 72), np.float32)
        for g in range(4):
            w = 2 ** (g + 1)
            for t in range(16):
                cnt = min(t + 1, w) if t0 == 0 else w
                auxa[:, g * 16 + t] = 1.0 / cnt
        auxa[:, 64:72] = 0.0 if t0 == 0 else 1.0
        in_maps.append({"xT": xt, "memT": np.ascontiguousarray(mem[b].T), "wmain": wmain, "wmem": wmem,
                        "params": params, "gvec": gv, "aux": auxa})
    return in_maps


_PROG = {}


def kernel(**inputs):
    in_maps = _prep_inputs(**inputs)
    if "nc" not in _PROG:
        _PROG["nc"], _ = build_program()
    nc = _PROG["nc"]
    res = run_bass_kernel_spmd(nc, in_maps, core_ids=list(range(NCORES)))
    B = inputs["x"].shape[0]
    out = np.empty((B, SEQ, D), np.float32)
    for c in range(NCORES):
        b, sc = c // 4, c % 4
        o = np.asarray(res.results[c]["outT"])
        out[b, sc * TOK:(sc + 1) * TOK, :] = o[:, HALO:].T
    return out
```
